# Optimizing a Trainium2 kernel written in Bass

```python
import math
import jax, jax.numpy as jnp
from jax import lax
import numpy as np

D_MODEL = 1024
BATCH = 8
SEQ = 4096
DEPTH = 1

A_HEADS = 8
A_GROUPS = 2
A_HPG = A_HEADS // A_GROUPS
HEAD_DIM = 64
A_WIDTH = A_HEADS * HEAD_DIM
KV_WIDTH = A_GROUPS * HEAD_DIM
CMP_BLOCK = 32
CMP_STRIDE = 16
CMP_HIDDEN = 128
SLC_BLOCK = 64
SLC_TOP_N = 16
WINDOW = 512
Q_BLOCK = 128
ROPE_THETA = 10000.0
FORCE_SCORE = 1e30
B_HEADS = 4
B_KDIM = 128
B_VDIM = 128
B_FWIDTH = B_HEADS * B_KDIM
B_WIDTH = B_HEADS * B_VDIM
CHUNK = 64
NORM_EPS = 1e-5

IN_SPLITS = (A_WIDTH,
             6 * KV_WIDTH,
             A_HEADS * 3,
             A_WIDTH,
             B_FWIDTH,
             B_FWIDTH,
             B_WIDTH,
             B_WIDTH,
             D_MODEL,
             D_MODEL)
IN_WIDTH = sum(IN_SPLITS)
SPLIT_POINTS = tuple(int(v) for v in np.cumsum(IN_SPLITS)[:-1])

kernel_name = "nsa_hgrn2_gated_parallel_deepnorm"


def layer_norm(x, g, b):
    xf = x.astype(jnp.float32)
    mu = jnp.mean(xf, axis=-1, keepdims=True)
    var = jnp.mean(jnp.square(xf - mu), axis=-1, keepdims=True)
    y = (xf - mu) * lax.rsqrt(var + NORM_EPS) * g.astype(jnp.float32) + b.astype(jnp.float32)
    return y.astype(x.dtype)


def rope(x, pos):
    inv = ROPE_THETA ** (-jnp.arange(0, HEAD_DIM, 2, dtype=jnp.float32) / HEAD_DIM)
    ang = pos.astype(jnp.float32)[:, None] * inv[None, :]
    cos = jnp.concatenate([jnp.cos(ang), jnp.cos(ang)], axis=-1)
    sin = jnp.concatenate([jnp.sin(ang), jnp.sin(ang)], axis=-1)
    xf = x.astype(jnp.float32)
    x1, x2 = jnp.split(xf, 2, axis=-1)
    rot = jnp.concatenate([-x2, x1], axis=-1)
    return (xf * cos + rot * sin).astype(x.dtype)


def masked_softmax(s, mask):
    s = jnp.where(mask, s.astype(jnp.float32), -jnp.inf)
    m = jnp.max(s, axis=-1, keepdims=True)
    m = jnp.where(jnp.isfinite(m), m, 0.0)
    p = jnp.exp(s - m)
    d = jnp.sum(p, axis=-1, keepdims=True)
    return p / jnp.where(d > 0, d, 1.0)


def compress(kv, pe, w1, w2):
    B, G, S, D = kv.shape
    r = CMP_BLOCK // CMP_STRIDE
    n = S // CMP_STRIDE - r + 1
    chunks = kv.reshape(B, G, S // CMP_STRIDE, CMP_STRIDE, D)
    blocks = jnp.concatenate([chunks[:, :, j:j + n] for j in range(r)], axis=3)
    h = (blocks + pe).reshape(B, G, n, CMP_BLOCK * D)
    return jax.nn.silu(h @ w1) @ w2


def nsa_attention(q, k_cmp, v_cmp, k_slc, v_slc, k_win, v_win, gates,
                  pe_k, w_k1, w_k2, pe_v, w_v1, w_v2):
    B, S, _ = q.shape
    G, HPG, D = A_GROUPS, A_HPG, HEAD_DIM
    pos = jnp.arange(S)
    q = q.reshape(B, S, G, HPG, D).transpose(0, 2, 3, 1, 4)
    kv_t = lambda a: a.reshape(B, S, G, D).transpose(0, 2, 1, 3)
    k_cmp, v_cmp, k_slc, v_slc, k_win, v_win = (kv_t(a) for a in (k_cmp, v_cmp, k_slc, v_slc, k_win, v_win))
    q_rope = rope(q, pos)
    k_slc = rope(k_slc, pos)
    k_win = rope(k_win, pos)
    kc = compress(k_cmp, pe_k, w_k1, w_k2)
    vc = compress(v_cmp, pe_v, w_v1, w_v2)
    n_cmp = kc.shape[2]
    n_slc = S // SLC_BLOCK
    n_sel = min(SLC_TOP_N, n_slc)
    cmp_start = jnp.arange(n_cmp) * CMP_STRIDE
    cmp_end = cmp_start + CMP_BLOCK - 1
    slc_start = jnp.arange(n_slc) * SLC_BLOCK
    overlap = ((cmp_start[:, None] < slc_start[None, :] + SLC_BLOCK)
               & (cmp_start[:, None] + CMP_BLOCK > slc_start[None, :])).astype(jnp.float32)
    ks_blocks = k_slc.reshape(B, G, n_slc, SLC_BLOCK, D)
    vs_blocks = v_slc.reshape(B, G, n_slc, SLC_BLOCK, D)
    kw_pad = jnp.pad(k_win, ((0, 0), (0, 0), (WINDOW, 0), (0, 0)))
    vw_pad = jnp.pad(v_win, ((0, 0), (0, 0), (WINDOW, 0), (0, 0)))
    gates = jax.nn.sigmoid(gates).reshape(B, S, G, HPG, 3).transpose(0, 2, 3, 1, 4)
    nq = S // Q_BLOCK

    def to_blocks(a):
        a = a.reshape(B, G, HPG, nq, Q_BLOCK, *a.shape[4:])
        return jnp.moveaxis(a, 3, 0)

    scale = HEAD_DIM ** -0.5
    bi = jnp.arange(B)[:, None, None, None]
    gi = jnp.arange(G)[None, :, None, None]
    j_idx = jnp.arange(n_slc)

    def block_fn(args):
        qr, qn, gt, blk = args
        s0 = blk * Q_BLOCK
        t = s0 + jnp.arange(Q_BLOCK)
        sc = jnp.einsum('bghqd,bgnd->bghqn', qn, kc) * scale
        pc = masked_softmax(sc, cmp_end[None, :] <= t[:, None])
        o_cmp = jnp.einsum('bghqn,bgnd->bghqd', pc.astype(vc.dtype), vc)
        imp = jnp.einsum('bghqn,nj->bgqj', pc, overlap)
        cur = (t // SLC_BLOCK)[:, None]
        forced = (j_idx[None, :] == 0) | (j_idx[None, :] == cur) | (j_idx[None, :] == cur - 1)
        imp = jnp.where(j_idx[None, :] > cur, -jnp.inf, jnp.where(forced, FORCE_SCORE, imp))
        top_s, top_i = lax.top_k(imp, n_sel)
        ksel = ks_blocks[bi, gi, top_i]
        vsel = vs_blocks[bi, gi, top_i]
        kpos = top_i[..., None] * SLC_BLOCK + jnp.arange(SLC_BLOCK)
        msel = jnp.isfinite(top_s)[..., None] & (kpos <= t[:, None, None])
        L = n_sel * SLC_BLOCK
        ksel = ksel.reshape(B, G, Q_BLOCK, L, D)
        vsel = vsel.reshape(B, G, Q_BLOCK, L, D)
        msel = msel.reshape(B, G, Q_BLOCK, L)
        ss = jnp.einsum('bghqd,bgqld->bghql', qr, ksel) * scale
        ps = masked_softmax(ss, msel[:, :, None])
        o_slc = jnp.einsum('bghql,bgqld->bghqd', ps.astype(vsel.dtype), vsel)
        kw = lax.dynamic_slice_in_dim(kw_pad, s0, WINDOW + Q_BLOCK, axis=2)
        vw = lax.dynamic_slice_in_dim(vw_pad, s0, WINDOW + Q_BLOCK, axis=2)
        kp = s0 - WINDOW + jnp.arange(WINDOW + Q_BLOCK)
        mw = (kp[None, :] <= t[:, None]) & (kp[None, :] > t[:, None] - WINDOW) & (kp[None, :] >= 0)
        sw = jnp.einsum('bghqd,bgkd->bghqk', qr, kw) * scale
        pw = masked_softmax(sw, mw)
        o_win = jnp.einsum('bghqk,bgkd->bghqd', pw.astype(vw.dtype), vw)
        return gt[..., 0:1] * o_cmp + gt[..., 1:2] * o_slc + gt[..., 2:3] * o_win

    out = lax.map(block_fn, (to_blocks(q_rope), to_blocks(q), to_blocks(gates), jnp.arange(nq)))
    return out.transpose(1, 0, 4, 2, 3, 5).reshape(B, S, A_WIDTH)


def hgrn2(q, f, i, lb):
    B, S, _ = q.shape
    H, DK, DV, C = B_HEADS, B_KDIM, B_VDIM, CHUNK
    n = S // C
    qf = jax.nn.silu(q.astype(jnp.float32))
    fg = lb + (1.0 - lb) * jax.nn.sigmoid(f.astype(jnp.float32))
    logf = jnp.log(fg)
    k = 1.0 - fg
    to_c = lambda a, d: a.reshape(B, n, C, H, d).transpose(0, 3, 1, 2, 4)
    qf, logf, k = to_c(qf, DK), to_c(logf, DK), to_c(k, DK)
    v = to_c(i.astype(jnp.float32), DV)
    b = jnp.cumsum(logf, axis=3)
    b_last = b[..., -1:, :]
    qe = qf * jnp.exp(b)
    ke = k * jnp.exp(-b)
    kd = k * jnp.exp(b_last - b)
    dl = jnp.exp(b_last[..., 0, :])
    causal = jnp.tril(jnp.ones((C, C), dtype=bool))
    attn = jnp.where(causal, jnp.einsum('bhncd,bhnsd->bhncs', qe, ke), 0.0)
    o_intra = jnp.einsum('bhncs,bhnse->bhnce', attn, v)

    def step(state, inp):
        qe_n, kd_n, v_n, dl_n = inp
        o = jnp.einsum('bhcd,bhde->bhce', qe_n, state)
        state = dl_n[..., None] * state + jnp.einsum('bhcd,bhce->bhde', kd_n, v_n)
        return state, o

    xs = (jnp.moveaxis(qe, 2, 0), jnp.moveaxis(kd, 2, 0), jnp.moveaxis(v, 2, 0), jnp.moveaxis(dl, 2, 0))
    s0 = jnp.zeros((B, H, DK, DV), jnp.float32)
    _, o_inter = lax.scan(step, s0, xs)
    o = o_intra + jnp.moveaxis(o_inter, 0, 2)
    return o.transpose(0, 2, 3, 1, 4).reshape(B, S, H, DV)


def setup_inputs(seed: int = 0) -> dict:
    key = jax.random.key(seed)
    ks = jax.random.split(key, 20)
    beta = (8 * DEPTH) ** -0.25
    nrm = lambda k, shape, s: jax.random.normal(k, shape, jnp.float32) * s
    return {
        "x": nrm(ks[0], (BATCH, SEQ, D_MODEL), 1.0),
        "w_in": nrm(ks[1], (DEPTH, D_MODEL, IN_WIDTH), D_MODEL ** -0.5),
        "b_in": nrm(ks[2], (DEPTH, IN_WIDTH), 0.01),
        "pe_cmp_k": nrm(ks[3], (DEPTH, CMP_BLOCK, HEAD_DIM), 0.1),
        "w_cmp_k1": nrm(ks[4], (DEPTH, CMP_BLOCK * HEAD_DIM, CMP_HIDDEN), (CMP_BLOCK * HEAD_DIM) ** -0.5),
        "w_cmp_k2": nrm(ks[5], (DEPTH, CMP_HIDDEN, HEAD_DIM), CMP_HIDDEN ** -0.5),
        "pe_cmp_v": nrm(ks[6], (DEPTH, CMP_BLOCK, HEAD_DIM), 0.1),
        "w_cmp_v1": nrm(ks[7], (DEPTH, CMP_BLOCK * HEAD_DIM, CMP_HIDDEN), (CMP_BLOCK * HEAD_DIM) ** -0.5),
        "w_cmp_v2": nrm(ks[8], (DEPTH, CMP_HIDDEN, HEAD_DIM), CMP_HIDDEN ** -0.5),
        "hgrn_lb_logits": nrm(ks[9], (DEPTH + 1, B_FWIDTH), 0.1),
        "hgrn_norm_g": 1.0 + nrm(ks[10], (DEPTH, B_WIDTH), 0.01),
        "w_branch_a": nrm(ks[11], (DEPTH, A_WIDTH, D_MODEL), beta * A_WIDTH ** -0.5),
        "w_branch_b": nrm(ks[12], (DEPTH, B_WIDTH, D_MODEL), beta * B_WIDTH ** -0.5),
        "w_out": nrm(ks[13], (DEPTH, D_MODEL, D_MODEL), beta * D_MODEL ** -0.5),
        "ln_g": 1.0 + nrm(ks[14], (DEPTH, D_MODEL), 0.01),
        "ln_b": nrm(ks[15], (DEPTH, D_MODEL), 0.01),
    }


def reference(x, w_in, b_in, pe_cmp_k, w_cmp_k1, w_cmp_k2, pe_cmp_v, w_cmp_v1, w_cmp_v2,
              hgrn_lb_logits, hgrn_norm_g, w_branch_a, w_branch_b, w_out, ln_g, ln_b):
    alpha = (2 * DEPTH) ** 0.25
    lb_all = jnp.cumsum(jax.nn.softmax(hgrn_lb_logits.astype(jnp.float32), axis=0), axis=0)
    B, S, _ = x.shape
    for l in range(DEPTH):
        h = x @ w_in[l] + b_in[l]
        (q_a, kv_a, g_nsa, z_a, q_b, f_b, i_b, z_b, gm_a, gm_b) = jnp.split(h, SPLIT_POINTS, axis=-1)
        k_cmp, v_cmp, k_slc, v_slc, k_win, v_win = jnp.split(kv_a, 6, axis=-1)
        o_a = nsa_attention(q_a, k_cmp, v_cmp, k_slc, v_slc, k_win, v_win, g_nsa,
                            pe_cmp_k[l], w_cmp_k1[l], w_cmp_k2[l], pe_cmp_v[l], w_cmp_v1[l], w_cmp_v2[l])
        o_a = o_a * jax.nn.silu(z_a)
        o_b = hgrn2(q_b, f_b, i_b, lb_all[l])
        o_b = o_b * lax.rsqrt(jnp.mean(jnp.square(o_b), axis=-1, keepdims=True) + NORM_EPS)
        o_b = (o_b.reshape(B, S, B_WIDTH) * hgrn_norm_g[l].astype(jnp.float32)).astype(x.dtype)
        o_b = o_b * jax.nn.silu(z_b)
        y = jax.nn.sigmoid(gm_a) * (o_a @ w_branch_a[l]) + jax.nn.sigmoid(gm_b) * (o_b @ w_branch_b[l])
        x = layer_norm(alpha * x + y @ w_out[l], ln_g[l], ln_b[l])
    return x
```

```python
import numpy as np
from contextlib import ExitStack
import concourse.bass as bass
import concourse.mybir as mybir
from concourse.bass_utils import run_bass_kernel_spmd

F32 = mybir.dt.float32
BF16 = mybir.dt.bfloat16
AF = mybir.ActivationFunctionType
ALU = mybir.AluOpType

NTILES = 32
SEQ = 4096
NEG = -30000.0
TINY = 1e-30
ALPHA = 2.0 ** 0.25
KEEPWARM = False


class Res:
    __slots__ = ("name", "w", "r")

    def __init__(self, name=""):
        self.name = name
        self.w = None
        self.r = {}


class Prog:
    ENG = ("pe", "act", "dve", "pool", "sp")
    CAP = 30000

    def __init__(self, nc, same_engine_sync=True):
        self.nc = nc
        self.ops = {e: [] for e in self.ENG}
        self.waited = {e: {} for e in self.ENG}
        self.dma_count = {}
        self.same_engine_sync = same_engine_sync

    def _deps(self, eng, reads, writes):
        toks = {}

        def add(ch, idx):
            if toks.get(ch, -1) < idx:
                toks[ch] = idx

        for r in reads:
            if r.w is not None:
                add(*r.w)
        for w in writes:
            if w.w is not None:
                add(*w.w)
            for ch, idx in w.r.items():
                add(ch, idx)
        waits = []
        for ch, idx in toks.items():
            if ch == eng and (eng == "pe" or not self.same_engine_sync):
                continue
            if self.waited[eng].get(ch, -1) >= idx:
                continue
            self.waited[eng][ch] = idx
            waits.append((ch, idx))
        return waits

    def _finish(self, tok, reads, writes):
        ch, idx = tok
        for r in reads:
            if r.r.get(ch, -1) < idx:
                r.r[ch] = idx
        for w in writes:
            w.w = tok
            w.r = {}

    def op(self, eng, fn, reads=(), writes=()):
        waits = self._deps(eng, reads, writes)
        idx = len(self.ops[eng])
        self.ops[eng].append(dict(fn=fn, waits=waits, ms=False, dma=None))
        self._finish((eng, idx), reads, writes)

    def dma(self, eng, fn, chan, reads=(), writes=()):
        waits = self._deps(eng, reads, writes)
        k = self.dma_count.get(chan, 0)
        self.dma_count[chan] = k + 1
        self.ops[eng].append(dict(fn=fn, waits=waits, ms=False, dma=chan))
        self._finish((("dma", chan), k), reads, writes)

    def _all_waits(self, eng, engines=True):
        waits = []
        if engines:
            for ch in self.ENG:
                if ch == eng:
                    continue
                last = -1
                for i in range(len(self.ops[ch]) - 1, -1, -1):
                    o = self.ops[ch][i]
                    if o["fn"] is not None and o["dma"] is None:
                        last = i
                        break
                if last >= 0 and self.waited[eng].get(ch, -1) < last:
                    self.waited[eng][ch] = last
                    waits.append((ch, last))
        for chan, k in self.dma_count.items():
            ch = ("dma", chan)
            if self.waited[eng].get(ch, -1) < k - 1:
                self.waited[eng][ch] = k - 1
                waits.append((ch, k - 1))
        return waits

    def barrier(self):
        allw = {e: self._all_waits(e) for e in self.ENG}
        for e in self.ENG:
            self.ops[e].append(dict(fn=None, waits=allw[e], ms=False, dma=None))

    def wait_all_dma(self, eng):
        self.ops[eng].append(dict(fn=None, waits=self._all_waits(eng, engines=False), ms=False, dma=None))

    def emit(self, stack):
        nc = self.nc
        for e in self.ENG:
            for o in self.ops[e]:
                for ch, idx in o["waits"]:
                    if isinstance(ch, str):
                        self.ops[ch][idx]["ms"] = True
        msnum = {}
        nsem = {}
        for e in self.ENG:
            c = 0
            for i, o in enumerate(self.ops[e]):
                if o["ms"]:
                    c += 1
                    msnum[(e, i)] = c
            nsem[e] = max(1, (c + self.CAP - 1) // self.CAP)
        esems = {e: [stack.enter_context(nc.semaphore(f"s_{e}_{j}")) for j in range(nsem[e])] for e in self.ENG}
        dsems = {ch: stack.enter_context(nc.semaphore(f"d_{ch}")) for ch in self.dma_count}
        CAP = self.CAP

        def semval(ch, idx):
            if isinstance(ch, str):
                m = msnum[(ch, idx)]
                return esems[ch][(m - 1) // CAP], (m - 1) % CAP + 1
            return dsems[ch[1]], 16 * (idx + 1)

        block = stack.enter_context(nc.Block())

        def run(e):
            def body(eng):
                for i, o in enumerate(self.ops[e]):
                    for ch, idx in o["waits"]:
                        s, v = semval(ch, idx)
                        eng.wait_ge(s, v)
                    if o["fn"] is None:
                        continue
                    ins = o["fn"](eng)
                    if o["dma"] is not None:
                        ins.then_inc(dsems[o["dma"]], 16)
                    elif o["ms"]:
                        m = msnum[(e, i)]
                        ins.then_inc(esems[e][(m - 1) // CAP], 1)
            return body

        block.tensor(run("pe"))
        block.scalar(run("act"))
        block.vector(run("dve"))
        block.gpsimd(run("pool"))
        block.sync(run("sp"))


class Arena:
    def __init__(self, handle, nwords):
        self.h = handle
        self.n = nwords
        self.off = 0

    def alloc(self, shape, dtype=F32, parts=128):
        nel = int(np.prod(shape))
        nw = (nel * (2 if dtype == BF16 else 4) + 3) // 4
        nw = (nw + 1) // 2 * 2
        assert self.off + nw <= self.n, f"SBUF arena overflow: {self.off}+{nw} > {self.n}"
        a = self.h[0:parts, self.off:self.off + nw]
        self.off += nw
        if dtype == BF16:
            a = a.bitcast(BF16)
        a = a[:, 0:nel]
        if len(shape) == 2:
            a = a.rearrange("p (a b) -> p a b", a=shape[0], b=shape[1])
        elif len(shape) == 3:
            a = a.rearrange("p (a b c) -> p a b c", a=shape[0], b=shape[1], c=shape[2])
        elif len(shape) != 1:
            raise ValueError(shape)
        return a


def bc(ap, shape):
    return ap.to_broadcast(list(shape))


class K:
    pass


def build(nt_tiles=NTILES, dbg=False, passes=(1, 2, 3), stop=99):
    nc = bass.Bass("TRN2", target_bir_lowering=False)
    k = K()
    k.stop = stop
    k.nc = nc
    k.NT = nt_tiles
    k.dbg = dbg
    S = SEQ

    def din(name, shape):
        return nc.dram_tensor(name, shape, F32, kind="ExternalInput").ap()

    k.xT_d = din("xT", [8, 128, S])
    k.x_d = din("x", [S, 1024])
    k.w1_d = din("w1", [1024, 1816])
    k.b1_d = din("b1", [1, 1816])
    k.w2_d = din("w2", [1024, 2048])
    k.b2_d = din("b2", [1, 2048])
    k.w3_d = din("w3", [1024, 2048])
    k.b3_d = din("b3", [1, 2048])
    k.wk1_d = din("wk1", [64, 32 * 128])
    k.wv1_d = din("wv1", [64, 32 * 128])
    k.wk2_d = din("wk2", [128, 64])
    k.wv2_d = din("wv2", [128, 64])
    k.pek_d = din("pek", [64, 32])
    k.pev_d = din("pev", [64, 32])
    k.lbl_d = din("lbl", [2, 512])
    k.hg_d = din("hg", [1, 512])
    k.wba_d = din("wba", [512, 1024])
    k.wbb_d = din("wbb", [512, 1024])
    k.wo_d = din("wo", [1024, 1024])
    k.lng_d = din("lng", [1, 1024])
    k.lnb_d = din("lnb", [1, 1024])
    k.cst_d = din("cst", [128, 642])
    k.maskc_d = din("maskc", [128, 33 * 128])
    k.ovl_d = din("ovl", [128, 2 * 65])
    k.cos_d = din("cos", [128, 32 * 32])
    k.sin_d = din("sin", [128, 32 * 32])
    k.et_d = din("et", [64, S])
    k.out_d = nc.dram_tensor("out", [S, 1024], F32, kind="ExternalOutput").ap()
    if dbg:
        k.dbg_oa = nc.dram_tensor("dbg_oa", [128, 4 * S], BF16, kind="ExternalOutput").ap()
        k.dbg_ob = nc.dram_tensor("dbg_ob", [128, 4 * S], BF16, kind="ExternalOutput").ap()

    P = Prog(nc)
    k.P = P
    with ExitStack() as st:
        ARENA_WORDS = 50 * 1024
        arena_h = st.enter_context(nc.sbuf_tensor("arena", [128, ARENA_WORDS], F32))
        k.A = Arena(arena_h, ARENA_WORDS)
        k.ps = [st.enter_context(nc.psum_tensor(f"ps{i}", [128, 512], F32)) for i in range(8)]
        k.psb = [p.bitcast(BF16) for p in k.ps]
        k.Rps = [Res(f"ps{i}") for i in range(8)]
        setup_persistent(k)
        if 1 in passes:
            mark = k.A.off
            pass1(k)
            P.barrier()
            k.A.off = mark
        if dbg:
            P.dma("sp", lambda e: e.dma_start(out=k.dbg_oa, in_=k.oaT[:].rearrange("p a b -> p (a b)")), "dbg", reads=[k.R_oaT])
        if 2 in passes:
            mark = k.A.off
            pass2(k)
            P.barrier()
            if dbg:
                P.dma("sp", lambda e: e.dma_start(out=k.dbg_ob, in_=k.obT[:].rearrange("p a b -> p (a b)")), "dbg", reads=[k.R_obT])
                P.barrier()
            k.A.off = mark
        if 3 in passes:
            pass3(k)
        P.wait_all_dma("sp")
        P.emit(st)
    return nc


def stage_cast(k, dst, src_d, parts, ncols, eng_cycle=("act", "dve", "pool"), p0=0):
    P = k.P
    c0 = 0
    while c0 < ncols:
        n = min(1024, ncols - c0)
        s = k.stage_i % 2
        k.stage_i += 1
        stg = k.stage[s]
        Rs = k.R_stage[s]
        P.dma("sp", lambda e, stg=stg, c0=c0, n=n: e.dma_start(out=stg[p0:p0 + parts, 0:n], in_=src_d[:, c0:c0 + n]), f"stg{s}", writes=[Rs])
        eng = eng_cycle[k.stage_i % len(eng_cycle)]
        d = dst[:, c0:c0 + n]
        if eng == "act":
            P.op("act", lambda e, d=d, stg=stg, n=n: e.copy(out=d, in_=stg[p0:p0 + parts, 0:n]), reads=[Rs], writes=[k.R_init])
        else:
            P.op(eng, lambda e, d=d, stg=stg, n=n: e.tensor_copy(out=d, in_=stg[p0:p0 + parts, 0:n]), reads=[Rs], writes=[k.R_init])
        c0 += n


def load_weight_groups(k, name, W, w_d, nchunks, colgroups):
    P = k.P
    res = []
    for gi, (c0, c1) in enumerate(colgroups):
        r = Res(f"{name}{gi}")
        src = w_d[:, c0:c1].rearrange("(c p) n -> p c n", p=128)
        P.dma("pool", lambda e, c0=c0, c1=c1, src=src: e.dma_start(out=W[:, :, c0:c1], in_=src), f"{name}{gi}", writes=[r])
        res.append(r)
    return res


def load_cast(k, dst, src_d):
    k.P.dma("pool", lambda e: e.dma_start(out=dst, in_=src_d), "initc", writes=[k.R_initc])


def join_init(k):
    k.P.op("pool", lambda e: e.memset(k.joinbuf, 0.0), reads=[k.R_init, k.R_initc], writes=[k.R_init])


def setup_persistent(k):
    P, A = k.P, k.A
    k.R_init = Res("init")
    k.R_initc = Res("initc")
    k.joinbuf = A.alloc([2])
    k.stage = [A.alloc([1024]), A.alloc([1024])]
    k.R_stage = [Res("stg0"), Res("stg1")]
    k.stage_i = 0
    k.cstf = A.alloc([642])
    P.dma("sp", lambda e: e.dma_start(out=k.cstf, in_=k.cst_d), "init", writes=[k.R_init])
    k.cstb = A.alloc([512], BF16)
    P.op("dve", lambda e: e.tensor_copy(out=k.cstb, in_=k.cstf[:, 0:512]), reads=[k.R_init], writes=[k.R_init])
    k.ident = k.cstb[:, 0:128]
    k.tri = k.cstb[:, 128:256]
    k.win2 = k.cstb[:, 256:384]
    k.mintra_b = k.cstb[:, 384:512]
    k.mintra_f = k.cstf[:, 384:512]
    k.mrev_f = k.cstf[:, 512:640]
    k.cind_f = k.cstf[:, 640:642]
    k.oaT = A.alloc([4, SEQ], BF16)
    k.R_oaT = Res("oaT")


def load_xT(k, i, slot, dma=True, cast=True):
    P = k.P
    xs = k.xTs[slot]
    xb = k.xTb[slot]
    src = k.xT_d[:, :, i * 128:(i + 1) * 128].rearrange("c p t -> p c t")
    if dma:
        P.dma("sp", lambda e: e.dma_start(out=xs, in_=src), f"xT{slot}", writes=[k.R_xTs[slot]])
    if not cast:
        return
    if getattr(k, "xcast", "pool") == "act":
        P.op("act", lambda e: e.copy(out=xb, in_=xs), reads=[k.R_xTs[slot]], writes=[k.R_xTb[slot]])
    else:
        P.op("pool", lambda e: e.tensor_copy(out=xb, in_=xs), reads=[k.R_xTs[slot]], writes=[k.R_xTb[slot]])


def project(k, slot, W, bbc, groups, h, R_h, banks=(0, 1), R_W=None):
    P = k.P
    xb = k.xTb[slot]
    for gi, (c0, c1) in enumerate(groups):
        b = banks[gi % len(banks)]
        bank = k.ps[b]
        for c in range(8):
            P.op("pe", lambda e, bank=bank, c=c, c0=c0, c1=c1: e.matmul(bank[:, 0:c1 - c0], lhsT=xb[:, c, :], rhs=W[:, c, c0:c1], start=(c == 0), stop=(c == 7)),
                 reads=[k.R_xTb[slot], k.R_init if R_W is None else R_W[c0 // 512]], writes=[k.Rps[b]])
        P.op("dve", lambda e, bank=bank, c0=c0, c1=c1: e.tensor_tensor(out=h[:, c0:c1], in0=bank[:, 0:c1 - c0], in1=bbc[:, c0:c1], op=ALU.add),
             reads=[k.Rps[b], k.R_init], writes=[R_h[gi]])


def pass1(k):
    P, A, NT = k.P, k.A, k.NT
    ps, psb, Rps = k.ps, k.psb, k.Rps
    RI = k.R_init
    W1 = A.alloc([8, 1816], BF16)
    R_W1 = load_weight_groups(k, "W1g", W1, k.w1_d, 8, [(0, 512), (512, 1024), (1024, 1304), (1304, 1816)])
    b1bc = A.alloc([1816])
    P.dma("sp", lambda e: e.dma_start(out=b1bc, in_=k.b1_d.broadcast_to([128, 1816])), "init", writes=[RI])
    cos = A.alloc([32, 32])
    sin = A.alloc([32, 32])
    P.dma("sp", lambda e: e.dma_start(out=cos[:].rearrange("p a b -> p (a b)"), in_=k.cos_d), "init", writes=[RI])
    P.dma("sp", lambda e: e.dma_start(out=sin[:].rearrange("p a b -> p (a b)"), in_=k.sin_d), "init", writes=[RI])
    maskc = A.alloc([33, 128], BF16)
    load_cast(k, maskc[:].rearrange("p a b -> p (a b)"), k.maskc_d)
    ovl = A.alloc([2, 65], BF16)
    load_cast(k, ovl[:].rearrange("p a b -> p (a b)"), k.ovl_d)
    wk1 = A.alloc([32, 128], BF16)
    wv1 = A.alloc([32, 128], BF16)
    load_cast(k, wk1[0:64].rearrange("p a b -> p (a b)"), k.wk1_d)
    load_cast(k, wv1[0:64].rearrange("p a b -> p (a b)"), k.wv1_d)
    wk2 = A.alloc([64], BF16)
    wv2 = A.alloc([64], BF16)
    load_cast(k, wk2, k.wk2_d)
    load_cast(k, wv2, k.wv2_d)
    pek = A.alloc([32], BF16)
    pev = A.alloc([32], BF16)
    load_cast(k, pek[0:64], k.pek_d)
    load_cast(k, pev[0:64], k.pev_d)
    KaT = A.alloc([2, SEQ], BF16)
    for g in range(2):
        load_cast(k, KaT[64:128, g, :], k.et_d)
    R_KaT = [Res() for _ in range(NT)]
    KwT = A.alloc([6, 2, 128], BF16)
    R_KwT = [Res() for _ in range(6)]
    Vsel = A.alloc([32, 2, 65], BF16)
    R_Vsel = [Res() for _ in range(NT)]
    Vwin = A.alloc([6, 2, 65], BF16)
    R_Vwin = [Res() for _ in range(6)]
    kcT = A.alloc([2, 256], BF16)
    hsTv = A.alloc([2, 256], BF16)
    vca = A.alloc([2, 2, 65], BF16)
    R_kc, R_hsv, R_vca = Res("kc"), Res("hsv"), Res("vca")
    P.op("pool", lambda e: e.memset(kcT, 0.0), writes=[R_kc])
    P.op("pool", lambda e: e.memset(hsTv, 0.0), writes=[R_hsv])
    P.op("pool", lambda e: e.memset(vca, 0.0), writes=[R_vca])
    P.op("pool", lambda e: e.memset(vca[:, :, :, 64:65], 1.0), reads=[R_vca], writes=[R_vca])
    P.op("pool", lambda e: e.memset(Vsel[:, :, :, 64:65], 1.0), writes=R_Vsel)
    P.op("pool", lambda e: e.memset(Vwin[:, :, :, 64:65], 1.0), writes=R_Vwin)
    kvcT = A.alloc([4, 144], BF16)
    R_kvcT = Res("kvcT")
    P.op("pool", lambda e: e.memset(kvcT, 0.0), writes=[R_kvcT])
    ck = A.alloc([2])
    R_ck = Res("ck")

    def emit_ck():
        for (w1_, pe_, col) in ((wk1, pek, 0), (wv1, pev, 1)):
            for l in range(32):
                P.op("pe", lambda e, w1_=w1_, pe_=pe_, l=l, col=col: e.matmul(ps[3][:, col:col + 1], lhsT=w1_[0:64, l, :], rhs=pe_[0:64, l:l + 1], start=(l == 0), stop=(l == 31)),
                     reads=[RI], writes=[Rps[3]])
        P.op("dve", lambda e: e.tensor_copy(out=ck, in_=ps[3][:, 0:2]), reads=[Rps[3]], writes=[R_ck])

    join_init(k)
    k.xTs = [A.alloc([8, 128]), A.alloc([8, 128])]
    k.xTb = [A.alloc([8, 128], BF16), A.alloc([8, 128], BF16)]
    k.R_xTs = [Res(), Res()]
    k.R_xTb = [Res(), Res()]
    h = A.alloc([1816])
    R_h = [Res() for _ in range(4)]
    groups = [(0, 512), (512, 1024), (1024, 1304), (1304, 1816)]
    tq = [A.alloc([8, 32]) for _ in range(4)]
    R_tq = [Res() for _ in range(4)]
    Qaug = A.alloc([8, 128], BF16)
    R_Qaug = Res()
    qn = A.alloc([512], BF16)
    R_qn = Res()
    kr = A.alloc([4, 64], BF16)
    R_kr = Res()
    kvc = A.alloc([256], BF16)
    R_kvc = Res()
    QnT = A.alloc([8, 128], BF16)
    R_QnT = Res()
    QaT = A.alloc([8, 128], BF16)
    R_QaT = Res()
    gth = A.alloc([24])
    gs = A.alloc([8, 3])
    R_gs = Res()
    zs = A.alloc([512])
    R_zs = Res()
    u = A.alloc([32])
    th = A.alloc([32])
    hsf = A.alloc([32])
    hsk = A.alloc([16], BF16)
    R_u, R_th, R_hsf, R_hsk = Res(), Res(), Res(), Res()
    NPB = 6
    Pb = [A.alloc([512], BF16) for _ in range(NPB)]
    R_Pb = [Res() for _ in range(NPB)]
    pb_i = [0]
    rd = A.alloc([4])
    R_rd = Res()
    imp = A.alloc([2, 64])
    R_imp = Res()
    m8a = A.alloc([8])
    m8b = A.alloc([8])
    impt = A.alloc([64])
    R_m8 = Res()
    negm = A.alloc([2, 64])
    R_negm = Res()
    cfs = [A.alloc([4]), A.alloc([4])]
    R_cfs = [Res(), Res()]
    tmpc = A.alloc([4, 64])
    R_tmpc = Res()
    oab = A.alloc([512], BF16)
    R_oab = Res()
    QaTs = [QaT, A.alloc([8, 128], BF16)]
    R_QaTs = [Res(), Res()]
    accs = [A.alloc([8, 64]), A.alloc([8, 64])]
    R_accs = [[Res(), Res()], [Res(), Res()]]
    gss = [gs, A.alloc([8, 3])]
    R_gss = [Res(), Res()]
    zss = [zs, A.alloc([512])]
    R_zss = [Res(), Res()]
    sc_cnt = {}
    pv_cnt = {}
    pvs = A.alloc([260])
    R_pvs = Res()
    print("pass1 arena words", A.off)

    def add_branch(items, kts, lhs_of, rhs_q, qres, v_of, masks, sbanks, pvbanks, extra=None, done=None, ci=1):
        key = tuple(pvbanks)
        cnt = pv_cnt.get(key, 0)
        pv_cnt[key] = cnt + 1
        pvb = pvbanks[cnt % len(pvbanks)]
        pv = ps[pvb][:, 0:260].rearrange("p (h e) -> p h e", h=4)
        for idx, kt in enumerate(kts):
            lhsT, lres = lhs_of(kt)
            va, vres = v_of(kt)
            items.append(dict(kt=kt, idx=idx, n=len(kts), lhsT=lhsT, lres=lres, rhs=rhs_q, qres=qres, va=va, vres=vres, mask=masks(kt),
                              sbanks=sbanks, pvb=pvb, pv=pv, extra=extra, done=done, ci=ci))

    def flush(items, hook=None):
        def score(it):
            assert len(it["sbanks"]) >= 2
            key = tuple(it["sbanks"])
            cnt = sc_cnt.get(key, 0)
            sc_cnt[key] = cnt + 1
            sb = it["sbanks"][cnt % len(it["sbanks"])]
            it["sb"] = sb
            P.op("pe", lambda e: e.matmul(ps[sb][:, :], lhsT=it["lhsT"], rhs=it["rhs"], start=True, stop=True),
                 reads=it["lres"] + it["qres"], writes=[Rps[sb]])

        def rest(it):
            sb = it["sb"]
            pi = pb_i[0] % NPB
            pb_i[0] += 1
            pt = Pb[pi]
            P.op("act", lambda e: e.activation(out=pt, in_=ps[sb][:, :], func=AF.Exp, scale=0.125), reads=[Rps[sb]], writes=[R_Pb[pi]])
            m = it["mask"]
            if m is not None:
                pt3 = pt.rearrange("p (h q) -> p h q", h=4)
                P.op("dve", lambda e: e.tensor_tensor(out=pt3, in0=pt3, in1=bc(m[:, None, :], [128, 4, 128]), op=ALU.mult),
                     reads=[R_Pb[pi], RI], writes=[R_Pb[pi]])
            pv, pvb, idx, n, va = it["pv"], it["pvb"], it["idx"], it["n"], it["va"]
            for hh in range(4):
                P.op("pe", lambda e, hh=hh: e.matmul(pv[:, hh, :], lhsT=pt[:, hh * 128:(hh + 1) * 128], rhs=va, start=(idx == 0 and hh == 0), stop=(idx == n - 1), skip_group_check=True),
                     reads=[R_Pb[pi]] + it["vres"], writes=[Rps[pvb]])
            if KEEPWARM:
                P.op("pe", lambda e: e.matmul(ps[pvb][:, 260:512], lhsT=pt[:, 384:512], rhs=pt[:, 0:252], start=False, stop=False, skip_group_check=True),
                     reads=[R_Pb[pi]], writes=[Rps[pvb]])
            if it["extra"] is not None:
                it["extra"](it, pt, pi)
            if idx == n - 1 and it["done"] is not None:
                if it["ci"] == 1:
                    P.op("dve", lambda e: e.tensor_copy(out=pvs, in_=ps[pvb][:, 0:260]), reads=[Rps[pvb]], writes=[R_pvs])
                    it["done"](None, pvs.rearrange("p (h e) -> p h e", h=4))
                else:
                    it["done"](pvb, pv)

        if not items:
            return
        look = len(items[0]["sbanks"]) - 1
        for j in range(min(look, len(items))):
            score(items[j])
        for j, it in enumerate(items):
            if j + look < len(items):
                score(items[j + look])
            rest(it)
            if hook is not None:
                hook(j)

    def combine(par, g, br, pvb, pv, first, ci):
        cf, R_cf = cfs[ci], R_cfs[ci]
        acc, R_acc, gs_, R_gs_ = accs[par], R_accs[par], gss[par], R_gss[par]
        R_src = R_pvs if pvb is None else Rps[pvb]
        P.op("dve", lambda e: e.tensor_scalar_max(out=cf, in0=pv[:, :, 64], scalar1=TINY), reads=[R_src], writes=[R_cf])
        P.op("dve", lambda e: e.reciprocal(out=cf, in_=cf), reads=[R_cf], writes=[R_cf])
        P.op("dve", lambda e: e.tensor_tensor(out=cf, in0=cf, in1=gs_[:, 4 * g:4 * g + 4, br], op=ALU.mult), reads=[R_cf, R_gs_], writes=[R_cf])
        accg = acc[:, 4 * g:4 * g + 4, :]
        if first:
            P.op("dve", lambda e: e.tensor_tensor(out=accg, in0=pv[:, :, 0:64], in1=bc(cf[:, :, None], [128, 4, 64]), op=ALU.mult),
                 reads=[R_src, R_cf], writes=[R_acc[g]])
        else:
            P.op("dve", lambda e: e.tensor_tensor(out=tmpc, in0=pv[:, :, 0:64], in1=bc(cf[:, :, None], [128, 4, 64]), op=ALU.mult),
                 reads=[R_src, R_cf], writes=[R_tmpc])
            P.op("pool", lambda e: e.tensor_tensor(out=accg, in0=accg, in1=tmpc, op=ALU.add), reads=[R_tmpc, R_acc[g]], writes=[R_acc[g]])

    def rope(src, nh, cb, sb_, dst, R_src, R_dst, eng):
        t = [x[:, 0:nh, :] for x in tq]
        P.op(eng, lambda e: e.tensor_tensor(out=t[0], in0=src[:, :, 0, :], in1=cb, op=ALU.mult), reads=[R_src, RI], writes=[R_tq[0]])
        P.op(eng, lambda e: e.tensor_tensor(out=t[1], in0=src[:, :, 1, :], in1=sb_, op=ALU.mult), reads=[R_src, RI], writes=[R_tq[1]])
        P.op(eng, lambda e: e.tensor_tensor(out=t[2], in0=src[:, :, 1, :], in1=cb, op=ALU.mult), reads=[R_src, RI], writes=[R_tq[2]])
        P.op(eng, lambda e: e.tensor_tensor(out=t[3], in0=src[:, :, 0, :], in1=sb_, op=ALU.mult), reads=[R_src, RI], writes=[R_tq[3]])
        P.op(eng, lambda e: e.tensor_tensor(out=dst[:, :, 0:32], in0=t[0], in1=t[1], op=ALU.subtract), reads=[R_tq[0], R_tq[1]], writes=[R_dst])
        P.op(eng, lambda e: e.tensor_tensor(out=dst[:, :, 32:64], in0=t[2], in1=t[3], op=ALU.add), reads=[R_tq[2], R_tq[3]], writes=[R_dst])

    def stage_a(i):
        slot = i % 2
        par = i % 2
        QaT_, R_QaT_ = QaTs[par], R_QaTs[par]
        gs_, R_gs_, zs_, R_zs_ = gss[par], R_gss[par], zss[par], R_zss[par]
        if i == 0:
            load_xT(k, 0, 0, cast=False)
        if i + 1 < NT:
            load_xT(k, i + 1, (i + 1) % 2, cast=False)
        k.xcast = "pool"
        load_xT(k, i, slot, dma=False)
        yield
        yield
        xb = k.xTb[slot]
        for gi, (c0, c1) in enumerate(groups):
            b = gi % 2
            for c in range(8):
                P.op("pe", lambda e, b=b, c=c, c0=c0, c1=c1: e.matmul(ps[b][:, 0:c1 - c0], lhsT=xb[:, c, :], rhs=W1[:, c, c0:c1], start=(c == 0), stop=(c == 7)),
                     reads=[k.R_xTb[slot], R_W1[gi]], writes=[Rps[b]])
                if c == 3:
                    yield
            P.op("dve", lambda e, b=b, c0=c0, c1=c1: e.tensor_tensor(out=h[:, c0:c1], in0=ps[b][:, 0:c1 - c0], in1=b1bc[:, c0:c1], op=ALU.add),
                 reads=[Rps[b], RI], writes=[R_h[gi]])
            yield
        cosb8 = bc(cos[:, i:i + 1, :], [128, 8, 32])
        sinb8 = bc(sin[:, i:i + 1, :], [128, 8, 32])
        cosb4 = bc(cos[:, i:i + 1, :], [128, 4, 32])
        sinb4 = bc(sin[:, i:i + 1, :], [128, 4, 32])
        hq = h[:, 0:512].rearrange("p (h t j) -> p h t j", h=8, t=2, j=32)
        hk = h[:, 512:768].rearrange("p (h t j) -> p h t j", h=4, t=2, j=32)
        P.op("pool", lambda e: e.tensor_copy(out=qn, in_=h[:, 0:512]), reads=[R_h[0]], writes=[R_qn])
        P.op("pool", lambda e: e.tensor_copy(out=kvc, in_=h[:, 768:1024]), reads=[R_h[1]], writes=[R_kvc])
        rope(hk, 4, cosb4, sinb4, kr, R_h[1], R_kr, "pool")
        rope(hq, 8, cosb8, sinb8, Qaug, R_h[0], R_Qaug, "pool")
        ws = i % 6
        P.op("pool", lambda e: e.tensor_copy(out=Vsel[:, i, :, 0:64], in_=h[:, 1024:1152].rearrange("p (g d) -> p g d", g=2)), reads=[R_h[2]], writes=[R_Vsel[i]])
        P.op("pool", lambda e: e.tensor_copy(out=Vwin[:, ws, :, 0:64], in_=h[:, 1152:1280].rearrange("p (g d) -> p g d", g=2)), reads=[R_h[2]], writes=[R_Vwin[ws]])
        yield
        P.op("act", lambda e: e.activation(out=gth, in_=h[:, 1280:1304], func=AF.Tanh, scale=0.5), reads=[R_h[2]], writes=[R_gs_])
        P.op("dve", lambda e: e.tensor_scalar(out=gs_[:].rearrange("p a b -> p (a b)"), in0=gth, scalar1=0.5, scalar2=0.5, op0=ALU.mult, op1=ALU.add), reads=[R_gs_], writes=[R_gs_])
        P.op("act", lambda e: e.activation(out=zs_, in_=h[:, 1304:1816], func=AF.Tanh, scale=0.5), reads=[R_h[3]], writes=[R_zs_])
        P.op("dve", lambda e: e.scalar_tensor_tensor(out=zs_, in0=zs_, scalar=1.0, in1=h[:, 1304:1816], op0=ALU.add, op1=ALU.mult), reads=[R_zs_, R_h[3]], writes=[R_zs_])
        yield
        yield
        for hh in range(8):
            P.op("pe", lambda e, hh=hh: e.transpose(out=psb[2][0:64, hh * 128:(hh + 1) * 128], in_=qn[:, hh * 64:(hh + 1) * 64], identity=k.ident), reads=[R_qn, RI], writes=[Rps[2]])
        P.op("dve", lambda e: e.tensor_copy(out=QnT[0:64].rearrange("p a b -> p (a b)"), in_=psb[2][0:64, 0:1024]), reads=[Rps[2]], writes=[R_QnT])
        yield
        for j in range(4):
            P.op("pe", lambda e, j=j: e.transpose(out=psb[3][0:64, (4 + j) * 128:(5 + j) * 128], in_=kvc[:, j * 64:(j + 1) * 64], identity=k.ident), reads=[R_kvc, RI], writes=[Rps[3]])
        P.op("dve", lambda e: e.tensor_copy(out=kvcT[0:64, :, 0:16], in_=kvcT[0:64, :, 128:144]), reads=[R_kvcT], writes=[R_kvcT])
        P.op("dve", lambda e: e.tensor_copy(out=kvcT[0:64, :, 16:144], in_=psb[3][0:64, 512:1024].rearrange("p (j t) -> p j t", j=4)), reads=[Rps[3], R_kvcT], writes=[R_kvcT])
        yield
        yield
        if i == 0:
            emit_ck()
        m0 = 1 if i == 0 else 0
        nb = 8 - m0
        n0 = 8 * i - 1 + m0
        for (w1_, j0, col0) in ((wk1, 0, 0), (wv1, 2, 16)):
            o_ap = ps[3][:, col0:col0 + 16]
            for l in range(32):
                P.op("pe", lambda e, w1_=w1_, l=l, j0=j0, o_ap=o_ap: e.matmul(o_ap, lhsT=w1_[0:64, l, :], rhs=kvcT[0:64, j0:j0 + 2, l:l + 16 * 7 + 1:16], start=(l == 0), stop=(l == 31)),
                     reads=[R_kvcT, RI], writes=[Rps[3]])
                if l == 15:
                    yield
            yield
        for col0, cc in ((0, 0), (16, 1)):
            P.op("dve", lambda e, col0=col0, cc=cc: e.tensor_scalar(out=u[:, col0:col0 + 16], in0=ps[3][:, col0:col0 + 16], scalar1=ck[:, cc:cc + 1], scalar2=None, op0=ALU.add), reads=[Rps[3], R_ck], writes=[R_u])
        P.op("act", lambda e: e.activation(out=th, in_=u, func=AF.Tanh, scale=0.5), reads=[R_u], writes=[R_th])
        P.op("dve", lambda e: e.scalar_tensor_tensor(out=hsf, in0=th, scalar=1.0, in1=u, op0=ALU.add, op1=ALU.mult), reads=[R_th, R_u], writes=[R_hsf])
        P.op("dve", lambda e: e.tensor_scalar(out=hsk, in0=hsf[:, 0:16], scalar1=0.5, scalar2=None, op0=ALU.mult), reads=[R_hsf], writes=[R_hsk])
        P.op("dve", lambda e: e.tensor_scalar(out=hsTv[:, :, n0:n0 + nb], in0=hsf[:, 16:32].rearrange("p (g m) -> p g m", g=2)[:, :, m0:8], scalar1=0.5, scalar2=None, op0=ALU.mult), reads=[R_hsf, R_hsv], writes=[R_hsv])
        for j in range(4):
            P.op("pe", lambda e, j=j: e.transpose(out=psb[2][0:64, j * 128:(j + 1) * 128], in_=kr[:, j, :], identity=k.ident), reads=[R_kr, RI], writes=[Rps[2]])
        P.op("dve", lambda e: e.tensor_copy(out=KaT[0:64, :, i * 128:(i + 1) * 128], in_=psb[2][0:64, 0:256].rearrange("p (g t) -> p g t", g=2)), reads=[Rps[2]], writes=[R_KaT[i]])
        P.op("dve", lambda e: e.tensor_copy(out=KwT[0:64, ws, :, :], in_=psb[2][0:64, 256:512].rearrange("p (g t) -> p g t", g=2)), reads=[Rps[2]], writes=[R_KwT[ws]])
        yield
        yield
        yield
        P.op("pe", lambda e: e.matmul(ps[3][0:64, 32:48], lhsT=wk2, rhs=hsk, start=True, stop=True), reads=[R_hsk, RI], writes=[Rps[3]])
        kc_ps = ps[3][0:64, 32:48].rearrange("p (g m) -> p g m", g=2)[:, :, m0:8]
        P.op("dve", lambda e: e.tensor_copy(out=kcT[0:64, :, n0:n0 + nb], in_=kc_ps), reads=[Rps[3], R_kc], writes=[R_kc])
        for nt in sorted({n0 // 128, (8 * i + 6) // 128}):
            for g in range(2):
                P.op("pe", lambda e, nt=nt, g=g: e.matmul(ps[3][:, 64:128], lhsT=hsTv[:, g, nt * 128:(nt + 1) * 128], rhs=wv2, start=True, stop=True), reads=[R_hsv, RI], writes=[Rps[3]])
                P.op("dve", lambda e, nt=nt, g=g: e.tensor_copy(out=vca[:, nt, g, 0:64], in_=ps[3][:, 64:128]), reads=[Rps[3], R_vca], writes=[R_vca])
        yield
        yield
        nts = [0] if 8 * i + 6 < 128 else [0, 1]

        def cmp_mask(nt):
            if nt == 0 and i <= 16:
                return maskc[:, i, :]
            if nt == 1 and i >= 16:
                return maskc[:, 17 + i - 16, :]
            return None

        imp_ps = ps[3][:, 128:388].rearrange("p (h e) -> p h e", h=4)

        def cmp_group(g):
            def extra(it, pt, pi):
                nt = it["kt"]
                for hh in range(4):
                    P.op("pe", lambda e, hh=hh: e.matmul(imp_ps[:, hh, :], lhsT=pt[:, hh * 128:(hh + 1) * 128], rhs=ovl[:, nt, :], start=(nt == nts[0] and hh == 0), stop=(nt == nts[-1]), skip_group_check=True),
                         reads=[R_Pb[pi], RI], writes=[Rps[3]])

            def done(pvb, pv):
                combine(par, g, 0, pvb, pv, True, 0)
                P.op("dve", lambda e: e.tensor_scalar_max(out=rd, in0=imp_ps[:, :, 64], scalar1=TINY), reads=[Rps[3]], writes=[R_rd])
                P.op("dve", lambda e: e.reciprocal(out=rd, in_=rd), reads=[R_rd], writes=[R_rd])
                P.op("dve", lambda e: e.tensor_scalar(out=imp[:, g, :], in0=imp_ps[:, 0, 0:64], scalar1=rd[:, 0:1], scalar2=None, op0=ALU.mult), reads=[Rps[3], R_rd], writes=[R_imp])
                for hh in range(1, 4):
                    P.op("dve", lambda e, hh=hh: e.scalar_tensor_tensor(out=imp[:, g, :], in0=imp_ps[:, hh, 0:64], scalar=rd[:, hh:hh + 1], in1=imp[:, g, :], op0=ALU.mult, op1=ALU.add),
                         reads=[Rps[3], R_rd, R_imp], writes=[R_imp])

            items = []
            add_branch(items, nts, lambda nt: (kcT[0:64, g, nt * 128:(nt + 1) * 128], [R_kc]), QnT[0:64, 4 * g:4 * g + 4, :], [R_QnT],
                       lambda nt: (vca[:, nt, g, :], [R_vca]), cmp_mask, [0, 1], [2], extra=extra, done=done, ci=0)
            flush(items)

        for g in range(2):
            cmp_group(g)
            yield
        if i < 8:
            P.op("pool", lambda e: e.memset(Qaug[:, :, 64:128], 0.0), reads=[R_Qaug], writes=[R_Qaug])
        else:
            c0, c1 = 2 * i, 2 * i + 1
            P.op("pool", lambda e: e.memset(imp[0:64, :, c0 - 1:64], -1.0), reads=[R_imp], writes=[R_imp])
            P.op("pool", lambda e: e.memset(imp[64:128, :, c1 - 1:64], -1.0), reads=[R_imp], writes=[R_imp])
            P.op("pool", lambda e: e.memset(imp[:, :, 0:1], -1.0), reads=[R_imp], writes=[R_imp])
            for g in range(2):
                P.op("dve", lambda e, g=g: e.max(out=m8a, in_=imp[:, g, :]), reads=[R_imp], writes=[R_m8])
                P.op("dve", lambda e, g=g: e.match_replace(out=impt, in_to_replace=m8a, in_values=imp[:, g, :], imm_value=-2.0), reads=[R_imp, R_m8], writes=[R_m8])
                P.op("dve", lambda e: e.max(out=m8b, in_=impt), reads=[R_m8], writes=[R_m8])
                P.op("dve", lambda e, g=g: e.tensor_scalar(out=negm[:, g, :], in0=imp[:, g, :], scalar1=m8b[:, 4:5], scalar2=NEG, op0=ALU.is_lt, op1=ALU.mult), reads=[R_imp, R_m8], writes=[R_negm])
            P.op("pool", lambda e: e.memset(negm[0:64, :, c0 - 1:c0 + 1], 0.0), reads=[R_negm], writes=[R_negm])
            P.op("pool", lambda e: e.memset(negm[64:128, :, c1 - 1:c1 + 1], 0.0), reads=[R_negm], writes=[R_negm])
            P.op("pool", lambda e: e.memset(negm[:, :, 0:1], 0.0), reads=[R_negm], writes=[R_negm])
            for g in range(2):
                P.op("pool", lambda e, g=g: e.tensor_copy(out=Qaug[:, 4 * g:4 * g + 4, 64:128], in_=bc(negm[:, g:g + 1, :], [128, 4, 64])), reads=[R_negm, R_Qaug], writes=[R_Qaug])
        yield
        yield
        yield
        yield
        for hh in range(8):
            P.op("pe", lambda e, hh=hh: e.transpose(out=psb[2][:, hh * 128:(hh + 1) * 128], in_=Qaug[:, hh, :], identity=k.ident), reads=[R_Qaug, RI], writes=[Rps[2]])
        P.op("dve", lambda e: e.tensor_copy(out=QaT_[:].rearrange("p a b -> p (a b)"), in_=psb[2][:, 0:1024]), reads=[Rps[2]], writes=[R_QaT_])
        yield

    N_A_STEPS = 34

    def stage_b(i, agen):
        par = i % 2
        QaT_, R_QaT_ = QaTs[par], R_QaTs[par]
        wkts = list(range(max(0, i - 4), i + 1))

        def win_mask(kt):
            if kt == i:
                return k.tri
            if kt == i - 4:
                return k.win2
            return None

        items = []
        for g in range(2):
            add_branch(items, list(range(i + 1)), (lambda g: lambda kt: (KaT[:, g, kt * 128:(kt + 1) * 128], [R_KaT[kt], RI]))(g), QaT_[:, 4 * g:4 * g + 4, :], [R_QaT_],
                       (lambda g: lambda kt: (Vsel[:, kt, g, :], [R_Vsel[kt]]))(g), lambda kt: k.tri if kt == i else None, [4, 5, 6], [7],
                       done=(lambda g: lambda pvb, pv: combine(par, g, 1, pvb, pv, False, 1))(g))
        for g in range(2):
            add_branch(items, wkts, (lambda g: lambda kt: (KwT[0:64, kt % 6, g, :], [R_KwT[kt % 6]]))(g), QaT_[0:64, 4 * g:4 * g + 4, :], [R_QaT_],
                       (lambda g: lambda kt: (Vwin[:, kt % 6, g, :], [R_Vwin[kt % 6]]))(g), win_mask, [4, 5, 6], [7],
                       done=(lambda g: lambda pvb, pv: combine(par, g, 2, pvb, pv, False, 1))(g))
        n = len(items)
        taken = [0]

        def hook(j):
            if agen is None:
                return
            want = ((j + 1) * N_A_STEPS + n - 1) // n
            while taken[0] < want:
                taken[0] += 1
                next(agen, None)

        flush(items, hook)
        if agen is not None:
            for _ in agen:
                pass
        acc, R_acc, zs_, R_zs_ = accs[par], R_accs[par], zss[par], R_zss[par]
        P.op("dve", lambda e: e.scalar_tensor_tensor(out=oab, in0=acc[:].rearrange("p a b -> p (a b)"), scalar=0.5, in1=zs_, op0=ALU.mult, op1=ALU.mult), reads=[R_acc[0], R_acc[1], R_zs_], writes=[R_oab])
        for c in range(4):
            P.op("pe", lambda e, c=c: e.transpose(out=psb[4][:, c * 128:(c + 1) * 128], in_=oab[:, c * 128:(c + 1) * 128], identity=k.ident), reads=[R_oab, RI], writes=[Rps[4]])
        P.op("act", lambda e: e.copy(out=k.oaT[:, :, i * 128:(i + 1) * 128], in_=psb[4][:, 0:512].rearrange("p (c t) -> p c t", c=4)), reads=[Rps[4]], writes=[k.R_oaT])

    for _ in stage_a(0):
        pass
    for i in range(NT):
        stage_b(i, stage_a(i + 1) if i + 1 < NT else None)


def alloc_xT(k):
    A = k.A
    k.xTs = [A.alloc([8, 128]), A.alloc([8, 128])]
    k.xTb = [A.alloc([8, 128], BF16), A.alloc([8, 128], BF16)]
    k.R_xTs = [Res(), Res()]
    k.R_xTb = [Res(), Res()]


def alloc_obT(k):
    k.obT = k.A.alloc([4, SEQ], BF16)
    if not hasattr(k, "R_obT"):
        k.R_obT = Res("obT")


def pass2(k):
    P, A, NT = k.P, k.A, k.NT
    k.xcast = "act"
    ps, psb, Rps = k.ps, k.psb, k.Rps
    RI = k.R_init
    alloc_obT(k)
    W2 = A.alloc([8, 2048], BF16)
    R_W2 = load_weight_groups(k, "W2g", W2, k.w2_d, 8, [(0, 512), (512, 1024), (1024, 1536), (1536, 2048)])
    b2bc = A.alloc([2048])
    P.dma("sp", lambda e: e.dma_start(out=b2bc, in_=k.b2_d.broadcast_to([128, 2048])), "init", writes=[RI])
    lbA = A.alloc([512])
    lbB = A.alloc([512])
    ghalf = A.alloc([512])
    P.dma("sp", lambda e: e.dma_start(out=lbA, in_=k.lbl_d[0:1, :].broadcast_to([128, 512])), "init", writes=[RI])
    P.dma("sp", lambda e: e.dma_start(out=lbB, in_=k.lbl_d[1:2, :].broadcast_to([128, 512])), "init", writes=[RI])
    P.dma("sp", lambda e: e.dma_start(out=ghalf, in_=k.hg_d.broadcast_to([128, 512])), "init", writes=[RI])
    P.op("dve", lambda e: e.tensor_tensor(out=lbA, in0=lbA, in1=lbB, op=ALU.subtract), reads=[RI], writes=[RI])
    P.op("act", lambda e: e.activation(out=lbA, in_=lbA, func=AF.Tanh, scale=0.5), reads=[RI], writes=[RI])
    P.op("dve", lambda e: e.tensor_scalar(out=lbB, in0=lbA, scalar1=0.25, scalar2=0.75, op0=ALU.mult, op1=ALU.add), reads=[RI], writes=[RI])
    P.op("dve", lambda e: e.tensor_scalar(out=lbA, in0=lbA, scalar1=-0.25, scalar2=0.25, op0=ALU.mult, op1=ALU.add), reads=[RI], writes=[RI])
    P.op("dve", lambda e: e.tensor_scalar(out=ghalf, in0=ghalf, scalar1=0.5, scalar2=None, op0=ALU.mult), reads=[RI], writes=[RI])
    St = A.alloc([4, 128])
    R_St = Res()
    Sb0 = [A.alloc([4, 128], BF16), A.alloc([4, 128], BF16)]
    R_Sb0 = [Res(), Res()]
    Sb1 = A.alloc([4, 128], BF16)
    R_Sb1 = Res()
    P.op("pool", lambda e: e.memset(St, 0.0), writes=[R_St])
    P.op("pool", lambda e: e.memset(Sb0[0], 0.0), writes=[R_Sb0[0]])
    alloc_xT(k)
    h2s = [A.alloc([2048]), A.alloc([2048])]
    R_h2s = [[Res() for _ in range(4)] for _ in range(2)]
    groups = [(0, 512), (512, 1024), (1024, 1536), (1536, 2048)]
    tqz, tff, tzz, logf, kk = (A.alloc([512]) for _ in range(5))
    R_tqz, R_tff, R_tzz, R_logf, R_kk = (Res() for _ in range(5))
    tzzs = [tzz, A.alloc([512])]
    R_tzzs = [R_tzz, Res()]
    eb, enb, erev = (A.alloc([512]) for _ in range(3))
    R_eb, R_enb, R_erev = (Res() for _ in range(3))
    qe_b, ke_b, kd_b, v_b, ob_b = (A.alloc([512], BF16) for _ in range(5))
    R_qe, R_ke, R_kd, R_v, R_ob = (Res() for _ in range(5))
    qeT, qeT0, qeT1, keT, attn_b = (A.alloc([4, 128], BF16) for _ in range(5))
    R_qeT, R_qeT0, R_qeT1, R_keT, R_attn = (Res() for _ in range(5))
    P.op("pool", lambda e: e.memset(qeT0, 0.0), writes=[R_qeT0])
    kd1_b = A.alloc([512], BF16)
    P.op("pool", lambda e: e.memset(kd_b, 0.0), writes=[R_kd])
    P.op("pool", lambda e: e.memset(kd1_b, 0.0), writes=[R_kd])
    P.op("pool", lambda e: e.memset(qeT1, 0.0), writes=[R_qeT1])
    dl = A.alloc([8])
    R_dl = Res()
    ssq, lnv, rstd = (A.alloc([4]) for _ in range(3))
    R_ssq = Res()
    junk = A.alloc([128])
    R_junk = Res()

    def s1(i):
        slot = i % 2
        load_xT(k, i, slot)
        project(k, slot, W2, b2bc, groups, h2s[slot], R_h2s[slot], R_W=R_W2)

    def tail_a(i):
        tzz, R_tzz = tzzs[i % 2], R_tzzs[i % 2]
        for hh in range(4):
            P.op("act", lambda e, hh=hh: e.activation(out=junk, in_=ps[7][:, hh * 128:(hh + 1) * 128], func=AF.Square, accum_out=ssq[:, hh:hh + 1]), reads=[Rps[7]], writes=[R_junk, R_ssq])
        P.op("act", lambda e: e.activation(out=lnv, in_=ssq, func=AF.Ln, scale=1.0 / 128.0, bias=1e-5), reads=[R_ssq], writes=[R_ssq])
        P.op("act", lambda e: e.activation(out=rstd, in_=lnv, func=AF.Exp, scale=-0.5), reads=[R_ssq], writes=[R_ssq])
        for hh in range(4):
            P.op("dve", lambda e, hh=hh: e.scalar_tensor_tensor(out=ob_b[:, hh * 128:(hh + 1) * 128], in0=ps[7][:, hh * 128:(hh + 1) * 128], scalar=rstd[:, hh:hh + 1], in1=tzz[:, hh * 128:(hh + 1) * 128], op0=ALU.mult, op1=ALU.mult),
                 reads=[Rps[7], R_ssq, R_tzz], writes=[R_ob])

    def tail_b(i):
        for c in range(4):
            P.op("pe", lambda e, c=c: e.transpose(out=psb[4][:, c * 128:(c + 1) * 128], in_=ob_b[:, c * 128:(c + 1) * 128], identity=k.ident), reads=[R_ob, RI], writes=[Rps[4]])
        P.op("act", lambda e: e.copy(out=k.obT[:, :, i * 128:(i + 1) * 128], in_=psb[4][:, 0:512].rearrange("p (c t) -> p c t", c=4)), reads=[Rps[4]], writes=[k.R_obT])

    def tile(i):
        slot = i % 2
        tzz, R_tzz = tzzs[i % 2], R_tzzs[i % 2]
        h2, R_h2 = h2s[slot], R_h2s[slot]
        hq, hf, hi, hz = (h2[:, a:a + 512] for a in (0, 512, 1024, 1536))
        if getattr(k, 'stop', 99) <= 1:
            return
        P.op("act", lambda e: e.activation(out=tqz, in_=hq, func=AF.Tanh, scale=0.5), reads=[R_h2[0]], writes=[R_tqz])
        P.op("act", lambda e: e.activation(out=tff, in_=hf, func=AF.Tanh, scale=0.5), reads=[R_h2[1]], writes=[R_tff])
        P.op("act", lambda e: e.activation(out=tzz, in_=hz, func=AF.Tanh, scale=0.5), reads=[R_h2[3]], writes=[R_tzz])
        P.op("act", lambda e: e.copy(out=v_b, in_=hi), reads=[R_h2[2]], writes=[R_v])
        P.op("dve", lambda e: e.scalar_tensor_tensor(out=tqz, in0=tqz, scalar=1.0, in1=hq, op0=ALU.add, op1=ALU.mult), reads=[R_tqz, R_h2[0]], writes=[R_tqz])
        P.op("pool", lambda e: e.tensor_tensor(out=tff, in0=tff, in1=lbA, op=ALU.mult), reads=[R_tff, RI], writes=[R_tff])
        P.op("pool", lambda e: e.tensor_tensor(out=tff, in0=tff, in1=lbB, op=ALU.add), reads=[R_tff, RI], writes=[R_tff])
        P.op("dve", lambda e: e.scalar_tensor_tensor(out=tzz, in0=tzz, scalar=1.0, in1=hz, op0=ALU.add, op1=ALU.mult), reads=[R_tzz, R_h2[3]], writes=[R_tzz])
        P.op("pool", lambda e: e.tensor_tensor(out=tzz, in0=tzz, in1=ghalf, op=ALU.mult), reads=[R_tzz, RI], writes=[R_tzz])
        P.op("act", lambda e: e.activation(out=logf, in_=tff, func=AF.Ln), reads=[R_tff], writes=[R_logf])
        P.op("pool", lambda e: e.tensor_scalar(out=kk, in0=tff, scalar1=-1.0, scalar2=1.0, op0=ALU.mult, op1=ALU.add), reads=[R_tff], writes=[R_kk])
        if getattr(k, 'stop', 99) <= 2:
            return
        P.op("pe", lambda e: e.matmul(ps[2][:, :], lhsT=k.mintra_f, rhs=logf, start=True, stop=True), reads=[R_logf, RI], writes=[Rps[2]])
        P.op("pe", lambda e: e.matmul(ps[3][:, :], lhsT=k.mrev_f, rhs=logf, start=True, stop=True), reads=[R_logf, RI], writes=[Rps[3]])
        for hh in range(4):
            P.op("pe", lambda e, hh=hh: e.matmul(ps[4][:, 2 * hh:2 * hh + 2], lhsT=logf[:, hh * 128:(hh + 1) * 128], rhs=k.cind_f, start=True, stop=True), reads=[R_logf, RI], writes=[Rps[4]])
        if getattr(k, 'stop', 99) <= 3:
            return
        P.op("act", lambda e: e.activation(out=eb, in_=ps[2][:, :], func=AF.Exp), reads=[Rps[2]], writes=[R_eb])
        P.op("act", lambda e: e.activation(out=enb, in_=ps[2][:, :], func=AF.Exp, scale=-1.0), reads=[Rps[2]], writes=[R_enb])
        P.op("act", lambda e: e.activation(out=erev, in_=ps[3][:, :], func=AF.Exp), reads=[Rps[3]], writes=[R_erev])
        P.op("act", lambda e: e.activation(out=dl, in_=ps[4][:, 0:8], func=AF.Exp), reads=[Rps[4]], writes=[R_dl])
        if i > 0:
            tail_a(i - 1)
        P.op("dve", lambda e: e.scalar_tensor_tensor(out=qe_b, in0=tqz, scalar=0.5, in1=eb, op0=ALU.mult, op1=ALU.mult), reads=[R_tqz, R_eb], writes=[R_qe])
        P.op("pool", lambda e: e.tensor_tensor(out=ke_b, in0=kk, in1=enb, op=ALU.mult), reads=[R_kk, R_enb], writes=[R_ke])
        P.op("pool", lambda e: e.tensor_tensor(out=kd_b[0:64, :], in0=kk[0:64, :], in1=erev[0:64, :], op=ALU.mult), reads=[R_kk, R_erev], writes=[R_kd])
        P.op("pool", lambda e: e.tensor_tensor(out=kd1_b[64:128, :], in0=kk[64:128, :], in1=erev[64:128, :], op=ALU.mult), reads=[R_kk, R_erev], writes=[R_kd])
        if getattr(k, 'stop', 99) <= 4:
            return
        for hh in range(4):
            P.op("pe", lambda e, hh=hh: e.transpose(out=psb[5][:, hh * 128:(hh + 1) * 128], in_=qe_b[:, hh * 128:(hh + 1) * 128], identity=k.ident), reads=[R_qe, RI], writes=[Rps[5]])
        for hh in range(4):
            P.op("pe", lambda e, hh=hh: e.transpose(out=psb[5][:, (4 + hh) * 128:(5 + hh) * 128], in_=ke_b[:, hh * 128:(hh + 1) * 128], identity=k.ident), reads=[R_ke, RI], writes=[Rps[5]])
        if k.stop <= 4.2:
            return
        q3 = psb[5][:, 0:512].rearrange("p (h t) -> p h t", h=4)
        P.op("dve", lambda e: e.tensor_copy(out=qeT, in_=q3), reads=[Rps[5]], writes=[R_qeT])
        if k.stop <= 4.4:
            return
        P.op("dve", lambda e: e.tensor_copy(out=qeT0[:, :, 0:64], in_=q3[:, :, 0:64]), reads=[Rps[5]], writes=[R_qeT0])
        P.op("dve", lambda e: e.tensor_copy(out=qeT1[:, :, 64:128], in_=q3[:, :, 64:128]), reads=[Rps[5]], writes=[R_qeT1])
        if k.stop <= 4.6:
            return
        P.op("dve", lambda e: e.tensor_copy(out=keT, in_=psb[5][:, 512:1024].rearrange("p (h t) -> p h t", h=4)), reads=[Rps[5]], writes=[R_keT])
        if getattr(k, 'stop', 99) <= 5:
            return
        for hh in range(4):
            P.op("pe", lambda e, hh=hh: e.matmul(ps[6][:, hh * 128:(hh + 1) * 128], lhsT=keT[:, hh, :], rhs=qeT[:, hh, :], start=True, stop=True), reads=[R_keT, R_qeT], writes=[Rps[6]])
        if i > 0:
            tail_b(i - 1)
        P.op("dve", lambda e: e.tensor_tensor(out=attn_b, in0=ps[6][:, :].rearrange("p (h t) -> p h t", h=4), in1=bc(k.mintra_b[:, None, :], [128, 4, 128]), op=ALU.mult), reads=[Rps[6], RI], writes=[R_attn])
        if getattr(k, 'stop', 99) <= 6:
            return
        def ub(hh, cc):
            b = 2 if hh < 2 else 3
            o = ((hh % 2) * 2 + cc) * 128
            return b, ps[b][:, o:o + 128]
        for hh in range(4):
            for cc in range(2):
                b, o = ub(hh, cc)
                P.op("pe", lambda e, hh=hh, cc=cc, o=o: e.matmul(o, lhsT=(kd_b, kd1_b)[cc][:, hh * 128:(hh + 1) * 128], rhs=v_b[:, hh * 128:(hh + 1) * 128], start=True, stop=True),
                     reads=[R_kd, R_v], writes=[Rps[b]])
        if getattr(k, 'stop', 99) <= 7:
            return
        nxt = (i + 1) % 2
        for cc in range(2):
            for hh in range(4):
                b, o = ub(hh, cc)
                P.op("dve", lambda e, hh=hh, cc=cc, o=o: e.scalar_tensor_tensor(out=St[:, hh, :], in0=St[:, hh, :], scalar=dl[:, 2 * hh + cc:2 * hh + cc + 1], in1=o, op0=ALU.mult, op1=ALU.add),
                     reads=[R_St, R_dl, Rps[b]], writes=[R_St])
            if cc == 0:
                P.op("act", lambda e: e.copy(out=Sb1, in_=St), reads=[R_St], writes=[R_Sb1])
            else:
                P.op("act", lambda e: e.copy(out=Sb0[nxt], in_=St), reads=[R_St], writes=[R_Sb0[nxt]])
        if getattr(k, 'stop', 99) <= 8:
            return
        cur = i % 2
        for hh in range(4):
            o = ps[7][:, hh * 128:(hh + 1) * 128]
            P.op("pe", lambda e, hh=hh, o=o: e.matmul(o, lhsT=attn_b[:, hh, :], rhs=v_b[:, hh * 128:(hh + 1) * 128], start=True, stop=False), reads=[R_attn, R_v], writes=[Rps[7]])
            P.op("pe", lambda e, hh=hh, o=o: e.matmul(o, lhsT=qeT0[:, hh, :], rhs=Sb0[cur][:, hh, :], start=False, stop=False), reads=[R_qeT0, R_Sb0[cur]], writes=[Rps[7]])
            P.op("pe", lambda e, hh=hh, o=o: e.matmul(o, lhsT=qeT1[:, hh, :], rhs=Sb1[:, hh, :], start=False, stop=True), reads=[R_qeT1, R_Sb1], writes=[Rps[7]])

    s1(0)
    for i in range(NT):
        if i + 1 < NT:
            s1(i + 1)
        tile(i)
    tail_a(NT - 1)
    tail_b(NT - 1)
    print("pass2 arena words", A.off)


def pass3(k):
    P, A, NT = k.P, k.A, k.NT
    k.xcast = "act"
    ps, psb, Rps = k.ps, k.psb, k.Rps
    RI = k.R_init
    alloc_obT(k)
    W3 = A.alloc([8, 2048], BF16)
    R_W3 = load_weight_groups(k, "W3g", W3, k.w3_d, 8, [(0, 512), (512, 1024), (1024, 1536), (1536, 2048)])
    b3bc = A.alloc([2048])
    P.dma("sp", lambda e: e.dma_start(out=b3bc, in_=k.b3_d.broadcast_to([128, 2048])), "init", writes=[RI])
    wba = A.alloc([4, 1024], BF16)
    wbb = A.alloc([4, 1024], BF16)
    wo = A.alloc([8, 1024], BF16)
    R_wba = load_weight_groups(k, "wbag", wba, k.wba_d, 4, [(0, 512), (512, 1024)])
    R_wbb = load_weight_groups(k, "wbbg", wbb, k.wbb_d, 4, [(0, 512), (512, 1024)])
    R_wo = load_weight_groups(k, "wog", wo, k.wo_d, 8, [(0, 512), (512, 1024)])
    lng = A.alloc([1024])
    lnb = A.alloc([1024])
    P.dma("sp", lambda e: e.dma_start(out=lng, in_=k.lng_d.broadcast_to([128, 1024])), "init", writes=[RI])
    P.dma("sp", lambda e: e.dma_start(out=lnb, in_=k.lnb_d.broadcast_to([128, 1024])), "init", writes=[RI])
    alloc_xT(k)
    xt = [A.alloc([1024]), A.alloc([1024]), A.alloc([1024])]
    R_xt = [Res(), Res(), Res()]
    hg = A.alloc([1024])
    R_hg = [Res(), Res()]
    t1 = k.stage[0][:, 0:512]
    t2 = k.stage[0][:, 512:1024]
    R_t1 = R_t2 = k.R_stage[0]
    y2 = [A.alloc([1024], BF16), A.alloc([1024], BF16)]
    R_y2 = [Res(), Res()]
    yT = A.alloc([8, 128], BF16)
    R_yT = Res()
    r = k.stage[1]
    R_r = k.R_stage[1]
    ot = [A.alloc([1024]), A.alloc([1024])]
    R_ot = [Res(), Res()]
    st6 = A.alloc([12])
    mv = A.alloc([2])
    lnv = A.alloc([1])
    rstd = A.alloc([1])
    nb = A.alloc([1])
    R_st = Res()

    def s1_load(i):
        load_xT(k, i, i % 2, cast=False)
        P.dma("sp", lambda e: e.dma_start(out=xt[i % 3], in_=k.x_d[i * 128:(i + 1) * 128, :]), f"xt{i % 3}", writes=[R_xt[i % 3]])

    def s1(i):
        slot = i % 2
        load_xT(k, i, slot, dma=False)
        ts = slice(i * 128, (i + 1) * 128)
        for half in range(2):
            hs = slice(half * 512, (half + 1) * 512)
            project(k, slot, W3, b3bc, [(half * 512, half * 512 + 512), (1024 + half * 512, 1536 + half * 512)], _HG(hg, half), R_hg, R_W=R_W3)
            P.op("act", lambda e: e.activation(out=hg, in_=hg, func=AF.Tanh, scale=0.5), reads=R_hg, writes=R_hg)
            for c in range(4):
                P.op("pe", lambda e, c=c, hs=hs: e.matmul(ps[2][:, :], lhsT=k.oaT[:, c, ts], rhs=wba[:, c, hs], start=(c == 0), stop=(c == 3)), reads=[k.R_oaT, R_wba[half]], writes=[Rps[2]])
            for c in range(4):
                P.op("pe", lambda e, c=c, hs=hs: e.matmul(ps[3][:, :], lhsT=k.obT[:, c, ts], rhs=wbb[:, c, hs], start=(c == 0), stop=(c == 3)), reads=[k.R_obT, R_wbb[half]], writes=[Rps[3]])
            P.op("dve", lambda e: e.scalar_tensor_tensor(out=t1, in0=hg[:, 0:512], scalar=1.0, in1=ps[2][:, :], op0=ALU.add, op1=ALU.mult), reads=[R_hg[0], Rps[2]], writes=[R_t1])
            P.op("dve", lambda e: e.scalar_tensor_tensor(out=t2, in0=hg[:, 512:1024], scalar=1.0, in1=ps[3][:, :], op0=ALU.add, op1=ALU.mult), reads=[R_hg[1], Rps[3]], writes=[R_t2])
            P.op("pool", lambda e, hs=hs: e.tensor_tensor(out=y2[slot][:, hs], in0=t1, in1=t2, op=ALU.add), reads=[R_t1, R_t2], writes=[R_y2[slot]])

    def s2(i):
        slot = i % 2
        for c in range(8):
            P.op("pe", lambda e, c=c: e.transpose(out=psb[4][:, c * 128:(c + 1) * 128], in_=y2[slot][:, c * 128:(c + 1) * 128], identity=k.ident), reads=[R_y2[slot], RI], writes=[Rps[4]])
        P.op("act", lambda e: e.copy(out=yT[:].rearrange("p a b -> p (a b)"), in_=psb[4][:, 0:1024]), reads=[Rps[4]], writes=[R_yT])
        for half in range(2):
            hs = slice(half * 512, (half + 1) * 512)
            b = 5 + half
            for c in range(8):
                P.op("pe", lambda e, c=c, hs=hs, b=b: e.matmul(ps[b][:, :], lhsT=yT[:, c, :], rhs=wo[:, c, hs], start=(c == 0), stop=(c == 7)), reads=[R_yT, R_wo[half]], writes=[Rps[b]])
            P.op("dve", lambda e, hs=hs, b=b: e.scalar_tensor_tensor(out=r[:, hs], in0=ps[b][:, :], scalar=0.5 / ALPHA, in1=xt[i % 3][:, hs], op0=ALU.mult, op1=ALU.add), reads=[Rps[b], R_xt[i % 3]], writes=[R_r])
            P.op("dve", lambda e, hs=hs, half=half: e.bn_stats(out=st6[:, half * 6:half * 6 + 6], in_=r[:, hs]), reads=[R_r], writes=[R_st])
        P.op("dve", lambda e: e.bn_aggr(out=mv, in_=st6), reads=[R_st], writes=[R_st])
        P.op("act", lambda e: e.activation(out=lnv, in_=mv[:, 1:2], func=AF.Ln, bias=1e-5 / (ALPHA * ALPHA)), reads=[R_st], writes=[R_st])
        P.op("act", lambda e: e.activation(out=rstd, in_=lnv, func=AF.Exp, scale=-0.5), reads=[R_st], writes=[R_st])
        P.op("dve", lambda e: e.scalar_tensor_tensor(out=nb, in0=mv[:, 0:1], scalar=-1.0, in1=rstd, op0=ALU.mult, op1=ALU.mult), reads=[R_st], writes=[R_st])
        o = ot[slot]
        P.op("dve", lambda e: e.tensor_scalar(out=o, in0=r, scalar1=rstd[:, 0:1], scalar2=nb[:, 0:1], op0=ALU.mult, op1=ALU.add), reads=[R_r, R_st], writes=[R_ot[slot]])
        P.op("pool", lambda e: e.tensor_tensor(out=o, in0=o, in1=lng, op=ALU.mult), reads=[R_ot[slot], RI], writes=[R_ot[slot]])
        P.op("pool", lambda e: e.tensor_tensor(out=o, in0=o, in1=lnb, op=ALU.add), reads=[R_ot[slot], RI], writes=[R_ot[slot]])
        P.dma("sp", lambda e: e.dma_start(out=k.out_d[i * 128:(i + 1) * 128, :], in_=o), f"out{slot}", reads=[R_ot[slot]])

    s1_load(0)
    if NT > 1:
        s1_load(1)
    s1(0)
    for i in range(NT):
        if i + 2 < NT:
            s1_load(i + 2)
        if i + 1 < NT:
            s1(i + 1)
        s2(i)
    print("pass3 arena words", A.off)


class _HG:
    def __init__(self, hg, half):
        self.hg = hg
        self.half = half

    def __getitem__(self, key):
        _, cs = key
        c0 = cs.start
        o = 0 if c0 < 1024 else 512
        return self.hg[:, o:o + (cs.stop - cs.start)]


def _consts():
    p = np.arange(128)
    cst = np.zeros((128, 642), np.float32)
    cst[:, 0:128] = np.eye(128)
    cst[:, 128:256] = (p[:, None] <= p[None, :])
    cst[:, 256:384] = (p[:, None] > p[None, :])
    same = (p[:, None] // 64) == (p[None, :] // 64)
    cst[:, 384:512] = same & (p[:, None] <= p[None, :])
    cst[:, 512:640] = same & (p[:, None] > p[None, :])
    cst[:, 640] = p < 64
    cst[:, 641] = p >= 64
    maskc = np.zeros((128, 33, 128), np.float32)
    q = np.arange(128)
    for idx in range(33):
        if idx <= 16:
            i, nt = idx, 0
        else:
            i, nt = idx - 17 + 16, 1
        n = nt * 128 + p
        maskc[:, idx, :] = (16 * n[:, None] + 31) <= (128 * i + q[None, :])
    ovl = np.zeros((128, 2, 65), np.float32)
    n = np.arange(256)
    cs = n * 16
    js = np.arange(64) * 64
    ov = ((cs[:, None] < js[None, :] + 64) & (cs[:, None] + 32 > js[None, :])).astype(np.float32)
    ov[255] = 0.0
    ovl[:, :, 0:64] = ov.reshape(2, 128, 64).transpose(1, 0, 2)
    ovl[:, :, 64] = 1.0
    ovl[127, 1, 64] = 0.0
    inv = np.float32(10000.0) ** (-(np.arange(0, 64, 2, dtype=np.float32)) / np.float32(64))
    pos = np.arange(SEQ, dtype=np.float32)
    ang = (pos[:, None] * inv[None, :]).astype(np.float32)
    cos = np.cos(ang).astype(np.float32).reshape(32, 128, 32).transpose(1, 0, 2)
    sin = np.sin(ang).astype(np.float32).reshape(32, 128, 32).transpose(1, 0, 2)
    et = (np.arange(SEQ)[None, :] // 64 == np.arange(64)[:, None]).astype(np.float32)
    return dict(cst=cst, maskc=np.ascontiguousarray(maskc.reshape(128, -1)), ovl=np.ascontiguousarray(ovl.reshape(128, -1)),
                cos=np.ascontiguousarray(cos.reshape(128, -1)), sin=np.ascontiguousarray(sin.reshape(128, -1)), et=et)


def prep_shared(w_in, b_in, pe_cmp_k, w_cmp_k1, w_cmp_k2, pe_cmp_v, w_cmp_v1, w_cmp_v2,
                hgrn_lb_logits, hgrn_norm_g, w_branch_a, w_branch_b, w_out, ln_g, ln_b):
    f = lambda a: np.ascontiguousarray(np.asarray(a, dtype=np.float32))
    w, b = np.asarray(w_in[0]), np.asarray(b_in[0])
    perm1 = np.concatenate([np.arange(0, 512), np.arange(768, 896), np.arange(1024, 1152), np.arange(512, 640), np.arange(640, 768),
                            np.arange(896, 1024), np.arange(1152, 1280), np.arange(1280, 1304), np.arange(1304, 1816)])
    d = dict(
        w1=f(w[:, perm1]), b1=f(b[perm1][None, :]),
        w2=f(w[:, 1816:3864]), b2=f(b[None, 1816:3864]),
        w3=f(w[:, 3864:5912]), b3=f(b[None, 3864:5912]),
        wk1=f(np.asarray(w_cmp_k1[0]).reshape(32, 64, 128).transpose(1, 0, 2).reshape(64, 4096)),
        wv1=f(np.asarray(w_cmp_v1[0]).reshape(32, 64, 128).transpose(1, 0, 2).reshape(64, 4096)),
        wk2=f(w_cmp_k2[0]), wv2=f(w_cmp_v2[0]),
        pek=f(np.asarray(pe_cmp_k[0]).T), pev=f(np.asarray(pe_cmp_v[0]).T),
        lbl=f(hgrn_lb_logits), hg=f(hgrn_norm_g),
        wba=f(w_branch_a[0]), wbb=f(w_branch_b[0]), wo=f(w_out[0]), lng=f(ln_g), lnb=f(ln_b),
    )
    d.update(_consts())
    return d


def kernel(x, **params):
    x = np.asarray(x, dtype=np.float32)
    shared = prep_shared(**params)
    nc = build()
    in_maps = []
    for b in range(8):
        m = dict(shared)
        m["x"] = np.ascontiguousarray(x[b])
        m["xT"] = np.ascontiguousarray(x[b].T.reshape(8, 128, SEQ))
        in_maps.append(m)
    res = run_bass_kernel_spmd(nc, in_maps, core_ids=list(range(8)))
    return np.stack([np.asarray(r["out"]) for r in res.results], axis=0)
```

```python
import numpy as np
from contextlib import ExitStack
import concourse.bass as bass
import concourse.mybir as mybir
from concourse.bass_utils import run_bass_kernel_spmd

F32 = mybir.dt.float32
BF16 = mybir.dt.bfloat16
AF = mybir.ActivationFunctionType
ALU = mybir.AluOpType

NTILES = 32
SEQ = 4096
NEG = -30000.0
TINY = 1e-30
ALPHA = 2.0 ** 0.25
KEEPWARM = False


class Res:
    __slots__ = ("name", "w", "r")

    def __init__(self, name=""):
        self.name = name
        self.w = None
        self.r = {}


class Prog:
    ENG = ("pe", "act", "dve", "pool", "sp")
    CAP = 30000

    def __init__(self, nc, same_engine_sync=True):
        self.nc = nc
        self.ops = {e: [] for e in self.ENG}
        self.waited = {e: {} for e in self.ENG}
        self.dma_count = {}
        self.same_engine_sync = same_engine_sync

    def _deps(self, eng, reads, writes):
        toks = {}

        def add(ch, idx):
            if toks.get(ch, -1) < idx:
                toks[ch] = idx

        for r in reads:
            if r.w is not None:
                add(*r.w)
        for w in writes:
            if w.w is not None:
                add(*w.w)
            for ch, idx in w.r.items():
                add(ch, idx)
        waits = []
        for ch, idx in toks.items():
            if ch == eng and (eng == "pe" or not self.same_engine_sync):
                continue
            if self.waited[eng].get(ch, -1) >= idx:
                continue
            self.waited[eng][ch] = idx
            waits.append((ch, idx))
        return waits

    def _finish(self, tok, reads, writes):
        ch, idx = tok
        for r in reads:
            if r.r.get(ch, -1) < idx:
                r.r[ch] = idx
        for w in writes:
            w.w = tok
            w.r = {}

    def op(self, eng, fn, reads=(), writes=()):
        waits = self._deps(eng, reads, writes)
        idx = len(self.ops[eng])
        self.ops[eng].append(dict(fn=fn, waits=waits, ms=False, dma=None))
        self._finish((eng, idx), reads, writes)

    def dma(self, eng, fn, chan, reads=(), writes=()):
        waits = self._deps(eng, reads, writes)
        k = self.dma_count.get(chan, 0)
        self.dma_count[chan] = k + 1
        self.ops[eng].append(dict(fn=fn, waits=waits, ms=False, dma=chan))
        self._finish((("dma", chan), k), reads, writes)

    def _all_waits(self, eng, engines=True):
        waits = []
        if engines:
            for ch in self.ENG:
                if ch == eng:
                    continue
                last = -1
                for i in range(len(self.ops[ch]) - 1, -1, -1):
                    o = self.ops[ch][i]
                    if o["fn"] is not None and o["dma"] is None:
                        last = i
                        break
                if last >= 0 and self.waited[eng].get(ch, -1) < last:
                    self.waited[eng][ch] = last
                    waits.append((ch, last))
        for chan, k in self.dma_count.items():
            ch = ("dma", chan)
            if self.waited[eng].get(ch, -1) < k - 1:
                self.waited[eng][ch] = k - 1
                waits.append((ch, k - 1))
        return waits

    def barrier(self):
        allw = {e: self._all_waits(e) for e in self.ENG}
        for e in self.ENG:
            self.ops[e].append(dict(fn=None, waits=allw[e], ms=False, dma=None))

    def wait_all_dma(self, eng):
        self.ops[eng].append(dict(fn=None, waits=self._all_waits(eng, engines=False), ms=False, dma=None))

    def emit(self, stack):
        nc = self.nc
        for e in self.ENG:
            for o in self.ops[e]:
                for ch, idx in o["waits"]:
                    if isinstance(ch, str):
                        self.ops[ch][idx]["ms"] = True
        msnum = {}
        nsem = {}
        for e in self.ENG:
            c = 0
            for i, o in enumerate(self.ops[e]):
                if o["ms"]:
                    c += 1
                    msnum[(e, i)] = c
            nsem[e] = max(1, (c + self.CAP - 1) // self.CAP)
        esems = {e: [stack.enter_context(nc.semaphore(f"s_{e}_{j}")) for j in range(nsem[e])] for e in self.ENG}
        dsems = {ch: stack.enter_context(nc.semaphore(f"d_{ch}")) for ch in self.dma_count}
        CAP = self.CAP

        def semval(ch, idx):
            if isinstance(ch, str):
                m = msnum[(ch, idx)]
                return esems[ch][(m - 1) // CAP], (m - 1) % CAP + 1
            return dsems[ch[1]], 16 * (idx + 1)

        block = stack.enter_context(nc.Block())

        def run(e):
            def body(eng):
                for i, o in enumerate(self.ops[e]):
                    for ch, idx in o["waits"]:
                        s, v = semval(ch, idx)
                        eng.wait_ge(s, v)
                    if o["fn"] is None:
                        continue
                    ins = o["fn"](eng)
                    if o["dma"] is not None:
                        ins.then_inc(dsems[o["dma"]], 16)
                    elif o["ms"]:
                        m = msnum[(e, i)]
                        ins.then_inc(esems[e][(m - 1) // CAP], 1)
            return body

        block.tensor(run("pe"))
        block.scalar(run("act"))
        block.vector(run("dve"))
        block.gpsimd(run("pool"))
        block.sync(run("sp"))


class Arena:
    def __init__(self, handle, nwords):
        self.h = handle
        self.n = nwords
        self.off = 0

    def alloc(self, shape, dtype=F32, parts=128):
        nel = int(np.prod(shape))
        nw = (nel * (2 if dtype == BF16 else 4) + 3) // 4
        nw = (nw + 1) // 2 * 2
        assert self.off + nw <= self.n, f"SBUF arena overflow: {self.off}+{nw} > {self.n}"
        a = self.h[0:parts, self.off:self.off + nw]
        self.off += nw
        if dtype == BF16:
            a = a.bitcast(BF16)
        a = a[:, 0:nel]
        if len(shape) == 2:
            a = a.rearrange("p (a b) -> p a b", a=shape[0], b=shape[1])
        elif len(shape) == 3:
            a = a.rearrange("p (a b c) -> p a b c", a=shape[0], b=shape[1], c=shape[2])
        elif len(shape) != 1:
            raise ValueError(shape)
        return a


def bc(ap, shape):
    return ap.to_broadcast(list(shape))


class K:
    pass


def build(nt_tiles=NTILES, dbg=False, passes=(1, 2, 3), stop=99):
    nc = bass.Bass("TRN2", target_bir_lowering=False)
    k = K()
    k.stop = stop
    k.nc = nc
    k.NT = nt_tiles
    k.dbg = dbg
    S = SEQ

    def din(name, shape):
        return nc.dram_tensor(name, shape, F32, kind="ExternalInput").ap()

    k.xT_d = din("xT", [8, 128, S])
    k.x_d = din("x", [S, 1024])
    k.w1_d = din("w1", [1024, 1816])
    k.b1_d = din("b1", [1, 1816])
    k.w2_d = din("w2", [1024, 2048])
    k.b2_d = din("b2", [1, 2048])
    k.w3_d = din("w3", [1024, 2048])
    k.b3_d = din("b3", [1, 2048])
    k.wk1_d = din("wk1", [64, 32 * 128])
    k.wv1_d = din("wv1", [64, 32 * 128])
    k.wk2_d = din("wk2", [128, 64])
    k.wv2_d = din("wv2", [128, 64])
    k.pek_d = din("pek", [64, 32])
    k.pev_d = din("pev", [64, 32])
    k.lbl_d = din("lbl", [2, 512])
    k.hg_d = din("hg", [1, 512])
    k.wba_d = din("wba", [512, 1024])
    k.wbb_d = din("wbb", [512, 1024])
    k.wo_d = din("wo", [1024, 1024])
    k.lng_d = din("lng", [1, 1024])
    k.lnb_d = din("lnb", [1, 1024])
    k.cst_d = din("cst", [128, 642])
    k.maskc_d = din("maskc", [128, 33 * 128])
    k.ovl_d = din("ovl", [128, 2 * 65])
    k.cos_d = din("cos", [128, 32 * 32])
    k.sin_d = din("sin", [128, 32 * 32])
    k.et_d = din("et", [64, S])
    k.out_d = nc.dram_tensor("out", [S, 1024], F32, kind="ExternalOutput").ap()
    if dbg:
        k.dbg_oa = nc.dram_tensor("dbg_oa", [128, 4 * S], BF16, kind="ExternalOutput").ap()
        k.dbg_ob = nc.dram_tensor("dbg_ob", [128, 4 * S], BF16, kind="ExternalOutput").ap()

    P = Prog(nc)
    k.P = P
    with ExitStack() as st:
        ARENA_WORDS = 50 * 1024
        arena_h = st.enter_context(nc.sbuf_tensor("arena", [128, ARENA_WORDS], F32))
        k.A = Arena(arena_h, ARENA_WORDS)
        k.ps = [st.enter_context(nc.psum_tensor(f"ps{i}", [128, 512], F32)) for i in range(8)]
        k.psb = [p.bitcast(BF16) for p in k.ps]
        k.Rps = [Res(f"ps{i}") for i in range(8)]
        setup_persistent(k)
        if 1 in passes:
            mark = k.A.off
            pass1(k)
            P.barrier()
            k.A.off = mark
        if dbg:
            P.dma("sp", lambda e: e.dma_start(out=k.dbg_oa, in_=k.oaT[:].rearrange("p a b -> p (a b)")), "dbg", reads=[k.R_oaT])
        if 2 in passes:
            mark = k.A.off
            pass2(k)
            P.barrier()
            if dbg:
                P.dma("sp", lambda e: e.dma_start(out=k.dbg_ob, in_=k.obT[:].rearrange("p a b -> p (a b)")), "dbg", reads=[k.R_obT])
                P.barrier()
            k.A.off = mark
        if 3 in passes:
            pass3(k)
        P.wait_all_dma("sp")
        P.emit(st)
    return nc


def stage_cast(k, dst, src_d, parts, ncols, eng_cycle=("act", "dve", "pool"), p0=0):
    P = k.P
    c0 = 0
    while c0 < ncols:
        n = min(1024, ncols - c0)
        s = k.stage_i % 2
        k.stage_i += 1
        stg = k.stage[s]
        Rs = k.R_stage[s]
        P.dma("sp", lambda e, stg=stg, c0=c0, n=n: e.dma_start(out=stg[p0:p0 + parts, 0:n], in_=src_d[:, c0:c0 + n]), f"stg{s}", writes=[Rs])
        eng = eng_cycle[k.stage_i % len(eng_cycle)]
        d = dst[:, c0:c0 + n]
        if eng == "act":
            P.op("act", lambda e, d=d, stg=stg, n=n: e.copy(out=d, in_=stg[p0:p0 + parts, 0:n]), reads=[Rs], writes=[k.R_init])
        else:
            P.op(eng, lambda e, d=d, stg=stg, n=n: e.tensor_copy(out=d, in_=stg[p0:p0 + parts, 0:n]), reads=[Rs], writes=[k.R_init])
        c0 += n


def load_weight_groups(k, name, W, w_d, nchunks, colgroups):
    P = k.P
    res = []
    for gi, (c0, c1) in enumerate(colgroups):
        r = Res(f"{name}{gi}")
        src = w_d[:, c0:c1].rearrange("(c p) n -> p c n", p=128)
        P.dma("pool", lambda e, c0=c0, c1=c1, src=src: e.dma_start(out=W[:, :, c0:c1], in_=src), f"{name}{gi}", writes=[r])
        res.append(r)
    return res


def load_cast(k, dst, src_d):
    k.P.dma("pool", lambda e: e.dma_start(out=dst, in_=src_d), "initc", writes=[k.R_initc])


def join_init(k):
    k.P.op("pool", lambda e: e.memset(k.joinbuf, 0.0), reads=[k.R_init, k.R_initc], writes=[k.R_init])


def setup_persistent(k):
    P, A = k.P, k.A
    k.R_init = Res("init")
    k.R_initc = Res("initc")
    k.joinbuf = A.alloc([2])
    k.stage = [A.alloc([1024]), A.alloc([1024])]
    k.R_stage = [Res("stg0"), Res("stg1")]
    k.stage_i = 0
    k.cstf = A.alloc([642])
    P.dma("sp", lambda e: e.dma_start(out=k.cstf, in_=k.cst_d), "init", writes=[k.R_init])
    k.cstb = A.alloc([512], BF16)
    P.op("dve", lambda e: e.tensor_copy(out=k.cstb, in_=k.cstf[:, 0:512]), reads=[k.R_init], writes=[k.R_init])
    k.ident = k.cstb[:, 0:128]
    k.tri = k.cstb[:, 128:256]
    k.win2 = k.cstb[:, 256:384]
    k.mintra_b = k.cstb[:, 384:512]
    k.mintra_f = k.cstf[:, 384:512]
    k.mrev_f = k.cstf[:, 512:640]
    k.cind_f = k.cstf[:, 640:642]
    k.oaT = A.alloc([4, SEQ], BF16)
    k.R_oaT = Res("oaT")


def load_xT(k, i, slot, dma=True, cast=True):
    P = k.P
    xs = k.xTs[slot]
    xb = k.xTb[slot]
    src = k.xT_d[:, :, i * 128:(i + 1) * 128].rearrange("c p t -> p c t")
    if dma:
        P.dma("sp", lambda e: e.dma_start(out=xs, in_=src), f"xT{slot}", writes=[k.R_xTs[slot]])
    if not cast:
        return
    if getattr(k, "xcast", "pool") == "act":
        P.op("act", lambda e: e.copy(out=xb, in_=xs), reads=[k.R_xTs[slot]], writes=[k.R_xTb[slot]])
    else:
        P.op("pool", lambda e: e.tensor_copy(out=xb, in_=xs), reads=[k.R_xTs[slot]], writes=[k.R_xTb[slot]])


def project(k, slot, W, bbc, groups, h, R_h, banks=(0, 1), R_W=None):
    P = k.P
    xb = k.xTb[slot]
    for gi, (c0, c1) in enumerate(groups):
        b = banks[gi % len(banks)]
        bank = k.ps[b]
        for c in range(8):
            P.op("pe", lambda e, bank=bank, c=c, c0=c0, c1=c1: e.matmul(bank[:, 0:c1 - c0], lhsT=xb[:, c, :], rhs=W[:, c, c0:c1], start=(c == 0), stop=(c == 7)),
                 reads=[k.R_xTb[slot], k.R_init if R_W is None else R_W[c0 // 512]], writes=[k.Rps[b]])
        P.op("dve", lambda e, bank=bank, c0=c0, c1=c1: e.tensor_tensor(out=h[:, c0:c1], in0=bank[:, 0:c1 - c0], in1=bbc[:, c0:c1], op=ALU.add),
             reads=[k.Rps[b], k.R_init], writes=[R_h[gi]])


def pass1(k):
    P, A, NT = k.P, k.A, k.NT
    ps, psb, Rps = k.ps, k.psb, k.Rps
    RI = k.R_init
    W1 = A.alloc([8, 1816], BF16)
    R_W1 = load_weight_groups(k, "W1g", W1, k.w1_d, 8, [(0, 512), (512, 1024), (1024, 1304), (1304, 1816)])
    b1bc = A.alloc([1816])
    P.dma("sp", lambda e: e.dma_start(out=b1bc, in_=k.b1_d.broadcast_to([128, 1816])), "init", writes=[RI])
    cos = A.alloc([32, 32])
    sin = A.alloc([32, 32])
    P.dma("sp", lambda e: e.dma_start(out=cos[:].rearrange("p a b -> p (a b)"), in_=k.cos_d), "init", writes=[RI])
    P.dma("sp", lambda e: e.dma_start(out=sin[:].rearrange("p a b -> p (a b)"), in_=k.sin_d), "init", writes=[RI])
    maskc = A.alloc([33, 128], BF16)
    load_cast(k, maskc[:].rearrange("p a b -> p (a b)"), k.maskc_d)
    ovl = A.alloc([2, 65], BF16)
    load_cast(k, ovl[:].rearrange("p a b -> p (a b)"), k.ovl_d)
    wk1 = A.alloc([32, 128], BF16)
    wv1 = A.alloc([32, 128], BF16)
    load_cast(k, wk1[0:64].rearrange("p a b -> p (a b)"), k.wk1_d)
    load_cast(k, wv1[0:64].rearrange("p a b -> p (a b)"), k.wv1_d)
    wk2 = A.alloc([64], BF16)
    wv2 = A.alloc([64], BF16)
    load_cast(k, wk2, k.wk2_d)
    load_cast(k, wv2, k.wv2_d)
    pek = A.alloc([32], BF16)
    pev = A.alloc([32], BF16)
    load_cast(k, pek[0:64], k.pek_d)
    load_cast(k, pev[0:64], k.pev_d)
    KaT = A.alloc([2, SEQ], BF16)
    for g in range(2):
        load_cast(k, KaT[64:128, g, :], k.et_d)
    R_KaT = [Res() for _ in range(NT)]
    KwT = A.alloc([6, 2, 128], BF16)
    R_KwT = [Res() for _ in range(6)]
    Vsel = A.alloc([32, 2, 65], BF16)
    R_Vsel = [Res() for _ in range(NT)]
    Vwin = A.alloc([6, 2, 65], BF16)
    R_Vwin = [Res() for _ in range(6)]
    kcT = A.alloc([2, 256], BF16)
    hsTv = A.alloc([2, 256], BF16)
    vca = A.alloc([2, 2, 65], BF16)
    R_kc, R_hsv, R_vca = Res("kc"), Res("hsv"), Res("vca")
    P.op("pool", lambda e: e.memset(kcT, 0.0), writes=[R_kc])
    P.op("pool", lambda e: e.memset(hsTv, 0.0), writes=[R_hsv])
    P.op("pool", lambda e: e.memset(vca, 0.0), writes=[R_vca])
    P.op("pool", lambda e: e.memset(vca[:, :, :, 64:65], 1.0), reads=[R_vca], writes=[R_vca])
    P.op("pool", lambda e: e.memset(Vsel[:, :, :, 64:65], 1.0), writes=R_Vsel)
    P.op("pool", lambda e: e.memset(Vwin[:, :, :, 64:65], 1.0), writes=R_Vwin)
    kvcT = A.alloc([4, 144], BF16)
    R_kvcT = Res("kvcT")
    P.op("pool", lambda e: e.memset(kvcT, 0.0), writes=[R_kvcT])
    ck = A.alloc([2])
    R_ck = Res("ck")

    def emit_ck():
        for (w1_, pe_, col) in ((wk1, pek, 0), (wv1, pev, 1)):
            for l in range(32):
                P.op("pe", lambda e, w1_=w1_, pe_=pe_, l=l, col=col: e.matmul(ps[3][:, col:col + 1], lhsT=w1_[0:64, l, :], rhs=pe_[0:64, l:l + 1], start=(l == 0), stop=(l == 31)),
                     reads=[RI], writes=[Rps[3]])
        P.op("dve", lambda e: e.tensor_copy(out=ck, in_=ps[3][:, 0:2]), reads=[Rps[3]], writes=[R_ck])

    join_init(k)
    k.xTs = [A.alloc([8, 128]), A.alloc([8, 128])]
    k.xTb = [A.alloc([8, 128], BF16), A.alloc([8, 128], BF16)]
    k.R_xTs = [Res(), Res()]
    k.R_xTb = [Res(), Res()]
    h = A.alloc([1816])
    R_h = [Res() for _ in range(4)]
    groups = [(0, 512), (512, 1024), (1024, 1304), (1304, 1816)]
    tq = [A.alloc([8, 32]) for _ in range(4)]
    R_tq = [Res() for _ in range(4)]
    Qaug = A.alloc([8, 128], BF16)
    R_Qaug = Res()
    qn = A.alloc([512], BF16)
    R_qn = Res()
    kr = A.alloc([4, 64], BF16)
    R_kr = Res()
    kvc = A.alloc([256], BF16)
    R_kvc = Res()
    QnT = A.alloc([8, 128], BF16)
    R_QnT = Res()
    QaT = A.alloc([8, 128], BF16)
    R_QaT = Res()
    gth = A.alloc([24])
    gs = A.alloc([8, 3])
    R_gs = Res()
    zs = A.alloc([512])
    R_zs = Res()
    u = A.alloc([32])
    th = A.alloc([32])
    hsf = A.alloc([32])
    hsk = A.alloc([16], BF16)
    R_u, R_th, R_hsf, R_hsk = Res(), Res(), Res(), Res()
    NPB = 6
    Pb = [A.alloc([512], BF16) for _ in range(NPB)]
    R_Pb = [Res() for _ in range(NPB)]
    pb_i = [0]
    rd = A.alloc([4])
    R_rd = Res()
    imp = A.alloc([2, 64])
    R_imp = Res()
    m8a = A.alloc([8])
    m8b = A.alloc([8])
    impt = A.alloc([64])
    R_m8 = Res()
    negm = A.alloc([2, 64])
    R_negm = Res()
    cfs = [A.alloc([4]), A.alloc([4])]
    R_cfs = [Res(), Res()]
    tmpc = A.alloc([4, 64])
    R_tmpc = Res()
    oab = A.alloc([512], BF16)
    R_oab = Res()
    QaTs = [QaT, A.alloc([8, 128], BF16)]
    R_QaTs = [Res(), Res()]
    accs = [A.alloc([8, 64]), A.alloc([8, 64])]
    R_accs = [[Res(), Res()], [Res(), Res()]]
    gss = [gs, A.alloc([8, 3])]
    R_gss = [Res(), Res()]
    zss = [zs, A.alloc([512])]
    R_zss = [Res(), Res()]
    sc_cnt = {}
    pv_cnt = {}
    pvs = A.alloc([260])
    R_pvs = Res()
    print("pass1 arena words", A.off)

    def add_branch(items, kts, lhs_of, rhs_q, qres, v_of, masks, sbanks, pvbanks, extra=None, done=None, ci=1):
        key = tuple(pvbanks)
        cnt = pv_cnt.get(key, 0)
        pv_cnt[key] = cnt + 1
        pvb = pvbanks[cnt % len(pvbanks)]
        pv = ps[pvb][:, 0:260].rearrange("p (h e) -> p h e", h=4)
        for idx, kt in enumerate(kts):
            lhsT, lres = lhs_of(kt)
            va, vres = v_of(kt)
            items.append(dict(kt=kt, idx=idx, n=len(kts), lhsT=lhsT, lres=lres, rhs=rhs_q, qres=qres, va=va, vres=vres, mask=masks(kt),
                              sbanks=sbanks, pvb=pvb, pv=pv, extra=extra, done=done, ci=ci))

    def flush(items, hook=None):
        def score(it):
            assert len(it["sbanks"]) >= 2
            key = tuple(it["sbanks"])
            cnt = sc_cnt.get(key, 0)
            sc_cnt[key] = cnt + 1
            sb = it["sbanks"][cnt % len(it["sbanks"])]
            it["sb"] = sb
            P.op("pe", lambda e: e.matmul(ps[sb][:, :], lhsT=it["lhsT"], rhs=it["rhs"], start=True, stop=True),
                 reads=it["lres"] + it["qres"], writes=[Rps[sb]])

        def rest(it):
            sb = it["sb"]
            pi = pb_i[0] % NPB
            pb_i[0] += 1
            pt = Pb[pi]
            P.op("act", lambda e: e.activation(out=pt, in_=ps[sb][:, :], func=AF.Exp, scale=0.125), reads=[Rps[sb]], writes=[R_Pb[pi]])
            m = it["mask"]
            if m is not None:
                pt3 = pt.rearrange("p (h q) -> p h q", h=4)
                P.op("dve", lambda e: e.tensor_tensor(out=pt3, in0=pt3, in1=bc(m[:, None, :], [128, 4, 128]), op=ALU.mult),
                     reads=[R_Pb[pi], RI], writes=[R_Pb[pi]])
            pv, pvb, idx, n, va = it["pv"], it["pvb"], it["idx"], it["n"], it["va"]
            for hh in range(4):
                P.op("pe", lambda e, hh=hh: e.matmul(pv[:, hh, :], lhsT=pt[:, hh * 128:(hh + 1) * 128], rhs=va, start=(idx == 0 and hh == 0), stop=(idx == n - 1), skip_group_check=True),
                     reads=[R_Pb[pi]] + it["vres"], writes=[Rps[pvb]])
            if KEEPWARM:
                P.op("pe", lambda e: e.matmul(ps[pvb][:, 260:512], lhsT=pt[:, 384:512], rhs=pt[:, 0:252], start=False, stop=False, skip_group_check=True),
                     reads=[R_Pb[pi]], writes=[Rps[pvb]])
            if it["extra"] is not None:
                it["extra"](it, pt, pi)
            if idx == n - 1 and it["done"] is not None:
                if it["ci"] == 1:
                    P.op("dve", lambda e: e.tensor_copy(out=pvs, in_=ps[pvb][:, 0:260]), reads=[Rps[pvb]], writes=[R_pvs])
                    it["done"](None, pvs.rearrange("p (h e) -> p h e", h=4))
                else:
                    it["done"](pvb, pv)

        if not items:
            return
        look = len(items[0]["sbanks"]) - 1
        for j in range(min(look, len(items))):
            score(items[j])
        for j, it in enumerate(items):
            if j + look < len(items):
                score(items[j + look])
            rest(it)
            if hook is not None:
                hook(j)

    def combine(par, g, br, pvb, pv, first, ci):
        cf, R_cf = cfs[ci], R_cfs[ci]
        acc, R_acc, gs_, R_gs_ = accs[par], R_accs[par], gss[par], R_gss[par]
        R_src = R_pvs if pvb is None else Rps[pvb]
        P.op("dve", lambda e: e.tensor_scalar_max(out=cf, in0=pv[:, :, 64], scalar1=TINY), reads=[R_src], writes=[R_cf])
        P.op("dve", lambda e: e.reciprocal(out=cf, in_=cf), reads=[R_cf], writes=[R_cf])
        P.op("dve", lambda e: e.tensor_tensor(out=cf, in0=cf, in1=gs_[:, 4 * g:4 * g + 4, br], op=ALU.mult), reads=[R_cf, R_gs_], writes=[R_cf])
        accg = acc[:, 4 * g:4 * g + 4, :]
        if first:
            P.op("dve", lambda e: e.tensor_tensor(out=accg, in0=pv[:, :, 0:64], in1=bc(cf[:, :, None], [128, 4, 64]), op=ALU.mult),
                 reads=[R_src, R_cf], writes=[R_acc[g]])
        else:
            P.op("dve", lambda e: e.tensor_tensor(out=tmpc, in0=pv[:, :, 0:64], in1=bc(cf[:, :, None], [128, 4, 64]), op=ALU.mult),
                 reads=[R_src, R_cf], writes=[R_tmpc])
            P.op("pool", lambda e: e.tensor_tensor(out=accg, in0=accg, in1=tmpc, op=ALU.add), reads=[R_tmpc, R_acc[g]], writes=[R_acc[g]])

    def rope(src, nh, cb, sb_, dst, R_src, R_dst, eng):
        t = [x[:, 0:nh, :] for x in tq]
        P.op(eng, lambda e: e.tensor_tensor(out=t[0], in0=src[:, :, 0, :], in1=cb, op=ALU.mult), reads=[R_src, RI], writes=[R_tq[0]])
        P.op(eng, lambda e: e.tensor_tensor(out=t[1], in0=src[:, :, 1, :], in1=sb_, op=ALU.mult), reads=[R_src, RI], writes=[R_tq[1]])
        P.op(eng, lambda e: e.tensor_tensor(out=t[2], in0=src[:, :, 1, :], in1=cb, op=ALU.mult), reads=[R_src, RI], writes=[R_tq[2]])
        P.op(eng, lambda e: e.tensor_tensor(out=t[3], in0=src[:, :, 0, :], in1=sb_, op=ALU.mult), reads=[R_src, RI], writes=[R_tq[3]])
        P.op(eng, lambda e: e.tensor_tensor(out=dst[:, :, 0:32], in0=t[0], in1=t[1], op=ALU.subtract), reads=[R_tq[0], R_tq[1]], writes=[R_dst])
        P.op(eng, lambda e: e.tensor_tensor(out=dst[:, :, 32:64], in0=t[2], in1=t[3], op=ALU.add), reads=[R_tq[2], R_tq[3]], writes=[R_dst])

    def stage_a(i):
        slot = i % 2
        par = i % 2
        QaT_, R_QaT_ = QaTs[par], R_QaTs[par]
        gs_, R_gs_, zs_, R_zs_ = gss[par], R_gss[par], zss[par], R_zss[par]
        if i == 0:
            load_xT(k, 0, 0, cast=False)
        if i + 1 < NT:
            load_xT(k, i + 1, (i + 1) % 2, cast=False)
        k.xcast = "act"
        load_xT(k, i, slot, dma=False)
        yield
        yield
        xb = k.xTb[slot]
        for gi, (c0, c1) in enumerate(groups):
            b = gi % 2
            for c in range(8):
                P.op("pe", lambda e, b=b, c=c, c0=c0, c1=c1: e.matmul(ps[b][:, 0:c1 - c0], lhsT=xb[:, c, :], rhs=W1[:, c, c0:c1], start=(c == 0), stop=(c == 7)),
                     reads=[k.R_xTb[slot], R_W1[gi]], writes=[Rps[b]])
                if c == 3:
                    yield
            P.op("dve", lambda e, b=b, c0=c0, c1=c1: e.tensor_tensor(out=h[:, c0:c1], in0=ps[b][:, 0:c1 - c0], in1=b1bc[:, c0:c1], op=ALU.add),
                 reads=[Rps[b], RI], writes=[R_h[gi]])
            yield
        cosb8 = bc(cos[:, i:i + 1, :], [128, 8, 32])
        sinb8 = bc(sin[:, i:i + 1, :], [128, 8, 32])
        cosb4 = bc(cos[:, i:i + 1, :], [128, 4, 32])
        sinb4 = bc(sin[:, i:i + 1, :], [128, 4, 32])
        hq = h[:, 0:512].rearrange("p (h t j) -> p h t j", h=8, t=2, j=32)
        hk = h[:, 512:768].rearrange("p (h t j) -> p h t j", h=4, t=2, j=32)
        P.op("act", lambda e: e.copy(out=qn, in_=h[:, 0:512]), reads=[R_h[0]], writes=[R_qn])
        P.op("act", lambda e: e.copy(out=kvc, in_=h[:, 768:1024]), reads=[R_h[1]], writes=[R_kvc])
        rope(hk, 4, cosb4, sinb4, kr, R_h[1], R_kr, "pool")
        rope(hq, 8, cosb8, sinb8, Qaug, R_h[0], R_Qaug, "pool")
        ws = i % 6
        P.op("pool", lambda e: e.tensor_copy(out=Vsel[:, i, :, 0:64], in_=h[:, 1024:1152].rearrange("p (g d) -> p g d", g=2)), reads=[R_h[2]], writes=[R_Vsel[i]])
        P.op("pool", lambda e: e.tensor_copy(out=Vwin[:, ws, :, 0:64], in_=h[:, 1152:1280].rearrange("p (g d) -> p g d", g=2)), reads=[R_h[2]], writes=[R_Vwin[ws]])
        P.op("act", lambda e: e.activation(out=gth, in_=h[:, 1280:1304], func=AF.Tanh, scale=0.5), reads=[R_h[2]], writes=[R_gs_])
        P.op("dve", lambda e: e.tensor_scalar(out=gs_[:].rearrange("p a b -> p (a b)"), in0=gth, scalar1=0.5, scalar2=0.5, op0=ALU.mult, op1=ALU.add), reads=[R_gs_], writes=[R_gs_])
        P.op("act", lambda e: e.activation(out=zs_, in_=h[:, 1304:1816], func=AF.Tanh, scale=0.5), reads=[R_h[3]], writes=[R_zs_])
        P.op("dve", lambda e: e.scalar_tensor_tensor(out=zs_, in0=zs_, scalar=1.0, in1=h[:, 1304:1816], op0=ALU.add, op1=ALU.mult), reads=[R_zs_, R_h[3]], writes=[R_zs_])
        yield
        yield
        for hh in range(8):
            P.op("pe", lambda e, hh=hh: e.transpose(out=psb[2][0:64, hh * 128:(hh + 1) * 128], in_=qn[:, hh * 64:(hh + 1) * 64], identity=k.ident), reads=[R_qn, RI], writes=[Rps[2]])
        P.op("dve", lambda e: e.tensor_copy(out=QnT[0:64].rearrange("p a b -> p (a b)"), in_=psb[2][0:64, 0:1024]), reads=[Rps[2]], writes=[R_QnT])
        yield
        for j in range(4):
            P.op("pe", lambda e, j=j: e.transpose(out=psb[3][0:64, (4 + j) * 128:(5 + j) * 128], in_=kvc[:, j * 64:(j + 1) * 64], identity=k.ident), reads=[R_kvc, RI], writes=[Rps[3]])
        P.op("dve", lambda e: e.tensor_copy(out=kvcT[0:64, :, 0:16], in_=kvcT[0:64, :, 128:144]), reads=[R_kvcT], writes=[R_kvcT])
        P.op("dve", lambda e: e.tensor_copy(out=kvcT[0:64, :, 16:144], in_=psb[3][0:64, 512:1024].rearrange("p (j t) -> p j t", j=4)), reads=[Rps[3], R_kvcT], writes=[R_kvcT])
        yield
        yield
        if i == 0:
            emit_ck()
        m0 = 1 if i == 0 else 0
        nb = 8 - m0
        n0 = 8 * i - 1 + m0
        for (w1_, j0, col0) in ((wk1, 0, 0), (wv1, 2, 16)):
            o_ap = ps[3][:, col0:col0 + 16]
            for l in range(32):
                P.op("pe", lambda e, w1_=w1_, l=l, j0=j0, o_ap=o_ap: e.matmul(o_ap, lhsT=w1_[0:64, l, :], rhs=kvcT[0:64, j0:j0 + 2, l:l + 16 * 7 + 1:16], start=(l == 0), stop=(l == 31)),
                     reads=[R_kvcT, RI], writes=[Rps[3]])
                if l == 15:
                    yield
            yield
        for col0, cc in ((0, 0), (16, 1)):
            P.op("dve", lambda e, col0=col0, cc=cc: e.tensor_scalar(out=u[:, col0:col0 + 16], in0=ps[3][:, col0:col0 + 16], scalar1=ck[:, cc:cc + 1], scalar2=None, op0=ALU.add), reads=[Rps[3], R_ck], writes=[R_u])
        P.op("act", lambda e: e.activation(out=th, in_=u, func=AF.Tanh, scale=0.5), reads=[R_u], writes=[R_th])
        P.op("dve", lambda e: e.scalar_tensor_tensor(out=hsf, in0=th, scalar=1.0, in1=u, op0=ALU.add, op1=ALU.mult), reads=[R_th, R_u], writes=[R_hsf])
        P.op("dve", lambda e: e.tensor_scalar(out=hsk, in0=hsf[:, 0:16], scalar1=0.5, scalar2=None, op0=ALU.mult), reads=[R_hsf], writes=[R_hsk])
        P.op("dve", lambda e: e.tensor_scalar(out=hsTv[:, :, n0:n0 + nb], in0=hsf[:, 16:32].rearrange("p (g m) -> p g m", g=2)[:, :, m0:8], scalar1=0.5, scalar2=None, op0=ALU.mult), reads=[R_hsf, R_hsv], writes=[R_hsv])
        for j in range(4):
            P.op("pe", lambda e, j=j: e.transpose(out=psb[2][0:64, j * 128:(j + 1) * 128], in_=kr[:, j, :], identity=k.ident), reads=[R_kr, RI], writes=[Rps[2]])
        P.op("act", lambda e: e.copy(out=KaT[0:64, :, i * 128:(i + 1) * 128], in_=psb[2][0:64, 0:256].rearrange("p (g t) -> p g t", g=2)), reads=[Rps[2]], writes=[R_KaT[i]])
        P.op("act", lambda e: e.copy(out=KwT[0:64, ws, :, :], in_=psb[2][0:64, 256:512].rearrange("p (g t) -> p g t", g=2)), reads=[Rps[2]], writes=[R_KwT[ws]])
        yield
        yield
        yield
        P.op("pe", lambda e: e.matmul(ps[3][0:64, 32:48], lhsT=wk2, rhs=hsk, start=True, stop=True), reads=[R_hsk, RI], writes=[Rps[3]])
        kc_ps = ps[3][0:64, 32:48].rearrange("p (g m) -> p g m", g=2)[:, :, m0:8]
        P.op("act", lambda e: e.copy(out=kcT[0:64, :, n0:n0 + nb], in_=kc_ps), reads=[Rps[3], R_kc], writes=[R_kc])
        for nt in sorted({n0 // 128, (8 * i + 6) // 128}):
            for g in range(2):
                P.op("pe", lambda e, nt=nt, g=g: e.matmul(ps[3][:, 64:128], lhsT=hsTv[:, g, nt * 128:(nt + 1) * 128], rhs=wv2, start=True, stop=True), reads=[R_hsv, RI], writes=[Rps[3]])
                P.op("act", lambda e, nt=nt, g=g: e.copy(out=vca[:, nt, g, 0:64], in_=ps[3][:, 64:128]), reads=[Rps[3], R_vca], writes=[R_vca])
        yield
        yield
        nts = [0] if 8 * i + 6 < 128 else [0, 1]

        def cmp_mask(nt):
            if nt == 0 and i <= 16:
                return maskc[:, i, :]
            if nt == 1 and i >= 16:
                return maskc[:, 17 + i - 16, :]
            return None

        imp_ps = ps[3][:, 128:388].rearrange("p (h e) -> p h e", h=4)

        def cmp_group(g):
            def extra(it, pt, pi):
                nt = it["kt"]
                for hh in range(4):
                    P.op("pe", lambda e, hh=hh: e.matmul(imp_ps[:, hh, :], lhsT=pt[:, hh * 128:(hh + 1) * 128], rhs=ovl[:, nt, :], start=(nt == nts[0] and hh == 0), stop=(nt == nts[-1]), skip_group_check=True),
                         reads=[R_Pb[pi], RI], writes=[Rps[3]])

            def done(pvb, pv):
                combine(par, g, 0, pvb, pv, True, 0)
                P.op("dve", lambda e: e.tensor_scalar_max(out=rd, in0=imp_ps[:, :, 64], scalar1=TINY), reads=[Rps[3]], writes=[R_rd])
                P.op("dve", lambda e: e.reciprocal(out=rd, in_=rd), reads=[R_rd], writes=[R_rd])
                P.op("dve", lambda e: e.tensor_scalar(out=imp[:, g, :], in0=imp_ps[:, 0, 0:64], scalar1=rd[:, 0:1], scalar2=None, op0=ALU.mult), reads=[Rps[3], R_rd], writes=[R_imp])
                for hh in range(1, 4):
                    P.op("dve", lambda e, hh=hh: e.scalar_tensor_tensor(out=imp[:, g, :], in0=imp_ps[:, hh, 0:64], scalar=rd[:, hh:hh + 1], in1=imp[:, g, :], op0=ALU.mult, op1=ALU.add),
                         reads=[Rps[3], R_rd, R_imp], writes=[R_imp])

            items = []
            add_branch(items, nts, lambda nt: (kcT[0:64, g, nt * 128:(nt + 1) * 128], [R_kc]), QnT[0:64, 4 * g:4 * g + 4, :], [R_QnT],
                       lambda nt: (vca[:, nt, g, :], [R_vca]), cmp_mask, [0, 1], [2], extra=extra, done=done, ci=0)
            flush(items)

        for g in range(2):
            cmp_group(g)
            yield
        if i < 8:
            P.op("pool", lambda e: e.memset(Qaug[:, :, 64:128], 0.0), reads=[R_Qaug], writes=[R_Qaug])
        else:
            c0, c1 = 2 * i, 2 * i + 1
            P.op("pool", lambda e: e.memset(imp[0:64, :, c0 - 1:64], -1.0), reads=[R_imp], writes=[R_imp])
            P.op("pool", lambda e: e.memset(imp[64:128, :, c1 - 1:64], -1.0), reads=[R_imp], writes=[R_imp])
            P.op("pool", lambda e: e.memset(imp[:, :, 0:1], -1.0), reads=[R_imp], writes=[R_imp])
            for g in range(2):
                P.op("dve", lambda e, g=g: e.max(out=m8a, in_=imp[:, g, :]), reads=[R_imp], writes=[R_m8])
                P.op("dve", lambda e, g=g: e.match_replace(out=impt, in_to_replace=m8a, in_values=imp[:, g, :], imm_value=-2.0), reads=[R_imp, R_m8], writes=[R_m8])
                P.op("dve", lambda e: e.max(out=m8b, in_=impt), reads=[R_m8], writes=[R_m8])
                P.op("dve", lambda e, g=g: e.tensor_scalar(out=negm[:, g, :], in0=imp[:, g, :], scalar1=m8b[:, 4:5], scalar2=NEG, op0=ALU.is_lt, op1=ALU.mult), reads=[R_imp, R_m8], writes=[R_negm])
            P.op("pool", lambda e: e.memset(negm[0:64, :, c0 - 1:c0 + 1], 0.0), reads=[R_negm], writes=[R_negm])
            P.op("pool", lambda e: e.memset(negm[64:128, :, c1 - 1:c1 + 1], 0.0), reads=[R_negm], writes=[R_negm])
            P.op("pool", lambda e: e.memset(negm[:, :, 0:1], 0.0), reads=[R_negm], writes=[R_negm])
            for g in range(2):
                P.op("pool", lambda e, g=g: e.tensor_copy(out=Qaug[:, 4 * g:4 * g + 4, 64:128], in_=bc(negm[:, g:g + 1, :], [128, 4, 64])), reads=[R_negm, R_Qaug], writes=[R_Qaug])
        yield
        yield
        yield
        yield
        for hh in range(8):
            P.op("pe", lambda e, hh=hh: e.transpose(out=psb[2][:, hh * 128:(hh + 1) * 128], in_=Qaug[:, hh, :], identity=k.ident), reads=[R_Qaug, RI], writes=[Rps[2]])
        P.op("dve", lambda e: e.tensor_copy(out=QaT_[:].rearrange("p a b -> p (a b)"), in_=psb[2][:, 0:1024]), reads=[Rps[2]], writes=[R_QaT_])
        yield

    N_A_STEPS = 33

    def stage_b(i, agen):
        par = i % 2
        QaT_, R_QaT_ = QaTs[par], R_QaTs[par]
        wkts = list(range(max(0, i - 4), i + 1))

        def win_mask(kt):
            if kt == i:
                return k.tri
            if kt == i - 4:
                return k.win2
            return None

        items = []
        for g in range(2):
            add_branch(items, list(range(i + 1)), (lambda g: lambda kt: (KaT[:, g, kt * 128:(kt + 1) * 128], [R_KaT[kt], RI]))(g), QaT_[:, 4 * g:4 * g + 4, :], [R_QaT_],
                       (lambda g: lambda kt: (Vsel[:, kt, g, :], [R_Vsel[kt]]))(g), lambda kt: k.tri if kt == i else None, [4, 5, 6], [7],
                       done=(lambda g: lambda pvb, pv: combine(par, g, 1, pvb, pv, False, 1))(g))
        for g in range(2):
            add_branch(items, wkts, (lambda g: lambda kt: (KwT[0:64, kt % 6, g, :], [R_KwT[kt % 6]]))(g), QaT_[0:64, 4 * g:4 * g + 4, :], [R_QaT_],
                       (lambda g: lambda kt: (Vwin[:, kt % 6, g, :], [R_Vwin[kt % 6]]))(g), win_mask, [4, 5, 6], [7],
                       done=(lambda g: lambda pvb, pv: combine(par, g, 2, pvb, pv, False, 1))(g))
        n = len(items)
        taken = [0]

        def hook(j):
            if agen is None:
                return
            want = ((j + 1) * N_A_STEPS + n - 1) // n
            while taken[0] < want:
                taken[0] += 1
                next(agen, None)

        flush(items, hook)
        if agen is not None:
            for _ in agen:
                pass
        acc, R_acc, zs_, R_zs_ = accs[par], R_accs[par], zss[par], R_zss[par]
        P.op("dve", lambda e: e.scalar_tensor_tensor(out=oab, in0=acc[:].rearrange("p a b -> p (a b)"), scalar=0.5, in1=zs_, op0=ALU.mult, op1=ALU.mult), reads=[R_acc[0], R_acc[1], R_zs_], writes=[R_oab])
        for c in range(4):
            P.op("pe", lambda e, c=c: e.transpose(out=psb[4][:, c * 128:(c + 1) * 128], in_=oab[:, c * 128:(c + 1) * 128], identity=k.ident), reads=[R_oab, RI], writes=[Rps[4]])
        P.op("dve", lambda e: e.tensor_copy(out=k.oaT[:, :, i * 128:(i + 1) * 128], in_=psb[4][:, 0:512].rearrange("p (c t) -> p c t", c=4)), reads=[Rps[4]], writes=[k.R_oaT])

    for _ in stage_a(0):
        pass
    for i in range(NT):
        stage_b(i, stage_a(i + 1) if i + 1 < NT else None)


def alloc_xT(k):
    A = k.A
    k.xTs = [A.alloc([8, 128]), A.alloc([8, 128])]
    k.xTb = [A.alloc([8, 128], BF16), A.alloc([8, 128], BF16)]
    k.R_xTs = [Res(), Res()]
    k.R_xTb = [Res(), Res()]


def alloc_obT(k):
    k.obT = k.A.alloc([4, SEQ], BF16)
    if not hasattr(k, "R_obT"):
        k.R_obT = Res("obT")


def pass2(k):
    P, A, NT = k.P, k.A, k.NT
    k.xcast = "act"
    ps, psb, Rps = k.ps, k.psb, k.Rps
    RI = k.R_init
    alloc_obT(k)
    W2 = A.alloc([8, 2048], BF16)
    R_W2 = load_weight_groups(k, "W2g", W2, k.w2_d, 8, [(0, 512), (512, 1024), (1024, 1536), (1536, 2048)])
    b2bc = A.alloc([2048])
    P.dma("sp", lambda e: e.dma_start(out=b2bc, in_=k.b2_d.broadcast_to([128, 2048])), "init", writes=[RI])
    lbA = A.alloc([512])
    lbB = A.alloc([512])
    ghalf = A.alloc([512])
    P.dma("sp", lambda e: e.dma_start(out=lbA, in_=k.lbl_d[0:1, :].broadcast_to([128, 512])), "init", writes=[RI])
    P.dma("sp", lambda e: e.dma_start(out=lbB, in_=k.lbl_d[1:2, :].broadcast_to([128, 512])), "init", writes=[RI])
    P.dma("sp", lambda e: e.dma_start(out=ghalf, in_=k.hg_d.broadcast_to([128, 512])), "init", writes=[RI])
    P.op("dve", lambda e: e.tensor_tensor(out=lbA, in0=lbA, in1=lbB, op=ALU.subtract), reads=[RI], writes=[RI])
    P.op("act", lambda e: e.activation(out=lbA, in_=lbA, func=AF.Tanh, scale=0.5), reads=[RI], writes=[RI])
    P.op("dve", lambda e: e.tensor_scalar(out=lbB, in0=lbA, scalar1=0.25, scalar2=0.75, op0=ALU.mult, op1=ALU.add), reads=[RI], writes=[RI])
    P.op("dve", lambda e: e.tensor_scalar(out=lbA, in0=lbA, scalar1=-0.25, scalar2=0.25, op0=ALU.mult, op1=ALU.add), reads=[RI], writes=[RI])
    P.op("dve", lambda e: e.tensor_scalar(out=ghalf, in0=ghalf, scalar1=0.5, scalar2=None, op0=ALU.mult), reads=[RI], writes=[RI])
    St = A.alloc([4, 128])
    R_St = Res()
    Sb0 = [A.alloc([4, 128], BF16), A.alloc([4, 128], BF16)]
    R_Sb0 = [Res(), Res()]
    Sb1 = A.alloc([4, 128], BF16)
    R_Sb1 = Res()
    P.op("pool", lambda e: e.memset(St, 0.0), writes=[R_St])
    P.op("pool", lambda e: e.memset(Sb0[0], 0.0), writes=[R_Sb0[0]])
    alloc_xT(k)
    h2s = [A.alloc([2048]), A.alloc([2048])]
    R_h2s = [[Res() for _ in range(4)] for _ in range(2)]
    groups = [(0, 512), (512, 1024), (1024, 1536), (1536, 2048)]
    tqz, tff, tzz, logf, kk = (A.alloc([512]) for _ in range(5))
    R_tqz, R_tff, R_tzz, R_logf, R_kk = (Res() for _ in range(5))
    tzzs = [tzz, A.alloc([512])]
    R_tzzs = [R_tzz, Res()]
    eb, enb, erev = (A.alloc([512]) for _ in range(3))
    R_eb, R_enb, R_erev = (Res() for _ in range(3))
    qe_b, ke_b, kd_b, v_b, ob_b = (A.alloc([512], BF16) for _ in range(5))
    R_qe, R_ke, R_kd, R_v, R_ob = (Res() for _ in range(5))
    qeT, qeT0, qeT1, keT, attn_b = (A.alloc([4, 128], BF16) for _ in range(5))
    R_qeT, R_qeT0, R_qeT1, R_keT, R_attn = (Res() for _ in range(5))
    P.op("pool", lambda e: e.memset(qeT0, 0.0), writes=[R_qeT0])
    kd1_b = A.alloc([512], BF16)
    P.op("pool", lambda e: e.memset(kd_b, 0.0), writes=[R_kd])
    P.op("pool", lambda e: e.memset(kd1_b, 0.0), writes=[R_kd])
    P.op("pool", lambda e: e.memset(qeT1, 0.0), writes=[R_qeT1])
    dl = A.alloc([8])
    R_dl = Res()
    ssq, lnv, rstd = (A.alloc([4]) for _ in range(3))
    R_ssq = Res()
    junk = A.alloc([128])
    R_junk = Res()

    def s1(i):
        slot = i % 2
        load_xT(k, i, slot)
        project(k, slot, W2, b2bc, groups, h2s[slot], R_h2s[slot], R_W=R_W2)

    def tail_a(i):
        tzz, R_tzz = tzzs[i % 2], R_tzzs[i % 2]
        for hh in range(4):
            P.op("act", lambda e, hh=hh: e.activation(out=junk, in_=ps[7][:, hh * 128:(hh + 1) * 128], func=AF.Square, accum_out=ssq[:, hh:hh + 1]), reads=[Rps[7]], writes=[R_junk, R_ssq])
        P.op("act", lambda e: e.activation(out=lnv, in_=ssq, func=AF.Ln, scale=1.0 / 128.0, bias=1e-5), reads=[R_ssq], writes=[R_ssq])
        P.op("act", lambda e: e.activation(out=rstd, in_=lnv, func=AF.Exp, scale=-0.5), reads=[R_ssq], writes=[R_ssq])
        for hh in range(4):
            P.op("dve", lambda e, hh=hh: e.scalar_tensor_tensor(out=ob_b[:, hh * 128:(hh + 1) * 128], in0=ps[7][:, hh * 128:(hh + 1) * 128], scalar=rstd[:, hh:hh + 1], in1=tzz[:, hh * 128:(hh + 1) * 128], op0=ALU.mult, op1=ALU.mult),
                 reads=[Rps[7], R_ssq, R_tzz], writes=[R_ob])

    def tail_b(i):
        for c in range(4):
            P.op("pe", lambda e, c=c: e.transpose(out=psb[4][:, c * 128:(c + 1) * 128], in_=ob_b[:, c * 128:(c + 1) * 128], identity=k.ident), reads=[R_ob, RI], writes=[Rps[4]])
        P.op("act", lambda e: e.copy(out=k.obT[:, :, i * 128:(i + 1) * 128], in_=psb[4][:, 0:512].rearrange("p (c t) -> p c t", c=4)), reads=[Rps[4]], writes=[k.R_obT])

    def tile(i):
        slot = i % 2
        tzz, R_tzz = tzzs[i % 2], R_tzzs[i % 2]
        h2, R_h2 = h2s[slot], R_h2s[slot]
        hq, hf, hi, hz = (h2[:, a:a + 512] for a in (0, 512, 1024, 1536))
        if getattr(k, 'stop', 99) <= 1:
            return
        P.op("act", lambda e: e.activation(out=tqz, in_=hq, func=AF.Tanh, scale=0.5), reads=[R_h2[0]], writes=[R_tqz])
        P.op("act", lambda e: e.activation(out=tff, in_=hf, func=AF.Tanh, scale=0.5), reads=[R_h2[1]], writes=[R_tff])
        P.op("act", lambda e: e.activation(out=tzz, in_=hz, func=AF.Tanh, scale=0.5), reads=[R_h2[3]], writes=[R_tzz])
        P.op("act", lambda e: e.copy(out=v_b, in_=hi), reads=[R_h2[2]], writes=[R_v])
        P.op("dve", lambda e: e.scalar_tensor_tensor(out=tqz, in0=tqz, scalar=1.0, in1=hq, op0=ALU.add, op1=ALU.mult), reads=[R_tqz, R_h2[0]], writes=[R_tqz])
        P.op("pool", lambda e: e.tensor_tensor(out=tff, in0=tff, in1=lbA, op=ALU.mult), reads=[R_tff, RI], writes=[R_tff])
        P.op("pool", lambda e: e.tensor_tensor(out=tff, in0=tff, in1=lbB, op=ALU.add), reads=[R_tff, RI], writes=[R_tff])
        P.op("dve", lambda e: e.scalar_tensor_tensor(out=tzz, in0=tzz, scalar=1.0, in1=hz, op0=ALU.add, op1=ALU.mult), reads=[R_tzz, R_h2[3]], writes=[R_tzz])
        P.op("pool", lambda e: e.tensor_tensor(out=tzz, in0=tzz, in1=ghalf, op=ALU.mult), reads=[R_tzz, RI], writes=[R_tzz])
        P.op("act", lambda e: e.activation(out=logf, in_=tff, func=AF.Ln), reads=[R_tff], writes=[R_logf])
        P.op("pool", lambda e: e.tensor_scalar(out=kk, in0=tff, scalar1=-1.0, scalar2=1.0, op0=ALU.mult, op1=ALU.add), reads=[R_tff], writes=[R_kk])
        if getattr(k, 'stop', 99) <= 2:
            return
        P.op("pe", lambda e: e.matmul(ps[2][:, :], lhsT=k.mintra_f, rhs=logf, start=True, stop=True), reads=[R_logf, RI], writes=[Rps[2]])
        P.op("pe", lambda e: e.matmul(ps[3][:, :], lhsT=k.mrev_f, rhs=logf, start=True, stop=True), reads=[R_logf, RI], writes=[Rps[3]])
        for hh in range(4):
            P.op("pe", lambda e, hh=hh: e.matmul(ps[4][:, 2 * hh:2 * hh + 2], lhsT=logf[:, hh * 128:(hh + 1) * 128], rhs=k.cind_f, start=True, stop=True), reads=[R_logf, RI], writes=[Rps[4]])
        if getattr(k, 'stop', 99) <= 3:
            return
        P.op("act", lambda e: e.activation(out=eb, in_=ps[2][:, :], func=AF.Exp), reads=[Rps[2]], writes=[R_eb])
        P.op("act", lambda e: e.activation(out=enb, in_=ps[2][:, :], func=AF.Exp, scale=-1.0), reads=[Rps[2]], writes=[R_enb])
        P.op("act", lambda e: e.activation(out=erev, in_=ps[3][:, :], func=AF.Exp), reads=[Rps[3]], writes=[R_erev])
        P.op("act", lambda e: e.activation(out=dl, in_=ps[4][:, 0:8], func=AF.Exp), reads=[Rps[4]], writes=[R_dl])
        if i > 0:
            tail_a(i - 1)
        P.op("dve", lambda e: e.scalar_tensor_tensor(out=qe_b, in0=tqz, scalar=0.5, in1=eb, op0=ALU.mult, op1=ALU.mult), reads=[R_tqz, R_eb], writes=[R_qe])
        P.op("pool", lambda e: e.tensor_tensor(out=ke_b, in0=kk, in1=enb, op=ALU.mult), reads=[R_kk, R_enb], writes=[R_ke])
        P.op("pool", lambda e: e.tensor_tensor(out=kd_b[0:64, :], in0=kk[0:64, :], in1=erev[0:64, :], op=ALU.mult), reads=[R_kk, R_erev], writes=[R_kd])
        P.op("pool", lambda e: e.tensor_tensor(out=kd1_b[64:128, :], in0=kk[64:128, :], in1=erev[64:128, :], op=ALU.mult), reads=[R_kk, R_erev], writes=[R_kd])
        if getattr(k, 'stop', 99) <= 4:
            return
        for hh in range(4):
            P.op("pe", lambda e, hh=hh: e.transpose(out=psb[5][:, hh * 128:(hh + 1) * 128], in_=qe_b[:, hh * 128:(hh + 1) * 128], identity=k.ident), reads=[R_qe, RI], writes=[Rps[5]])
        for hh in range(4):
            P.op("pe", lambda e, hh=hh: e.transpose(out=psb[5][:, (4 + hh) * 128:(5 + hh) * 128], in_=ke_b[:, hh * 128:(hh + 1) * 128], identity=k.ident), reads=[R_ke, RI], writes=[Rps[5]])
        if k.stop <= 4.2:
            return
        q3 = psb[5][:, 0:512].rearrange("p (h t) -> p h t", h=4)
        P.op("dve", lambda e: e.tensor_copy(out=qeT, in_=q3), reads=[Rps[5]], writes=[R_qeT])
        if k.stop <= 4.4:
            return
        P.op("dve", lambda e: e.tensor_copy(out=qeT0[:, :, 0:64], in_=q3[:, :, 0:64]), reads=[Rps[5]], writes=[R_qeT0])
        P.op("dve", lambda e: e.tensor_copy(out=qeT1[:, :, 64:128], in_=q3[:, :, 64:128]), reads=[Rps[5]], writes=[R_qeT1])
        if k.stop <= 4.6:
            return
        P.op("dve", lambda e: e.tensor_copy(out=keT, in_=psb[5][:, 512:1024].rearrange("p (h t) -> p h t", h=4)), reads=[Rps[5]], writes=[R_keT])
        if getattr(k, 'stop', 99) <= 5:
            return
        for hh in range(4):
            P.op("pe", lambda e, hh=hh: e.matmul(ps[6][:, hh * 128:(hh + 1) * 128], lhsT=keT[:, hh, :], rhs=qeT[:, hh, :], start=True, stop=True), reads=[R_keT, R_qeT], writes=[Rps[6]])
        if i > 0:
            tail_b(i - 1)
        P.op("dve", lambda e: e.tensor_tensor(out=attn_b, in0=ps[6][:, :].rearrange("p (h t) -> p h t", h=4), in1=bc(k.mintra_b[:, None, :], [128, 4, 128]), op=ALU.mult), reads=[Rps[6], RI], writes=[R_attn])
        if getattr(k, 'stop', 99) <= 6:
            return
        def ub(hh, cc):
            b = 2 if hh < 2 else 3
            o = ((hh % 2) * 2 + cc) * 128
            return b, ps[b][:, o:o + 128]
        for hh in range(4):
            for cc in range(2):
                b, o = ub(hh, cc)
                P.op("pe", lambda e, hh=hh, cc=cc, o=o: e.matmul(o, lhsT=(kd_b, kd1_b)[cc][:, hh * 128:(hh + 1) * 128], rhs=v_b[:, hh * 128:(hh + 1) * 128], start=True, stop=True),
                     reads=[R_kd, R_v], writes=[Rps[b]])
        if getattr(k, 'stop', 99) <= 7:
            return
        nxt = (i + 1) % 2
        for cc in range(2):
            for hh in range(4):
                b, o = ub(hh, cc)
                P.op("dve", lambda e, hh=hh, cc=cc, o=o: e.scalar_tensor_tensor(out=St[:, hh, :], in0=St[:, hh, :], scalar=dl[:, 2 * hh + cc:2 * hh + cc + 1], in1=o, op0=ALU.mult, op1=ALU.add),
                     reads=[R_St, R_dl, Rps[b]], writes=[R_St])
            if cc == 0:
                P.op("act", lambda e: e.copy(out=Sb1, in_=St), reads=[R_St], writes=[R_Sb1])
            else:
                P.op("act", lambda e: e.copy(out=Sb0[nxt], in_=St), reads=[R_St], writes=[R_Sb0[nxt]])
        if getattr(k, 'stop', 99) <= 8:
            return
        cur = i % 2
        for hh in range(4):
            o = ps[7][:, hh * 128:(hh + 1) * 128]
            P.op("pe", lambda e, hh=hh, o=o: e.matmul(o, lhsT=attn_b[:, hh, :], rhs=v_b[:, hh * 128:(hh + 1) * 128], start=True, stop=False), reads=[R_attn, R_v], writes=[Rps[7]])
            P.op("pe", lambda e, hh=hh, o=o: e.matmul(o, lhsT=qeT0[:, hh, :], rhs=Sb0[cur][:, hh, :], start=False, stop=False), reads=[R_qeT0, R_Sb0[cur]], writes=[Rps[7]])
            P.op("pe", lambda e, hh=hh, o=o: e.matmul(o, lhsT=qeT1[:, hh, :], rhs=Sb1[:, hh, :], start=False, stop=True), reads=[R_qeT1, R_Sb1], writes=[Rps[7]])

    s1(0)
    for i in range(NT):
        if i + 1 < NT:
            s1(i + 1)
        tile(i)
    tail_a(NT - 1)
    tail_b(NT - 1)
    print("pass2 arena words", A.off)


def pass3(k):
    P, A, NT = k.P, k.A, k.NT
    k.xcast = "act"
    ps, psb, Rps = k.ps, k.psb, k.Rps
    RI = k.R_init
    alloc_obT(k)
    W3 = A.alloc([8, 2048], BF16)
    R_W3 = load_weight_groups(k, "W3g", W3, k.w3_d, 8, [(0, 512), (512, 1024), (1024, 1536), (1536, 2048)])
    b3bc = A.alloc([2048])
    P.dma("sp", lambda e: e.dma_start(out=b3bc, in_=k.b3_d.broadcast_to([128, 2048])), "init", writes=[RI])
    wba = A.alloc([4, 1024], BF16)
    wbb = A.alloc([4, 1024], BF16)
    wo = A.alloc([8, 1024], BF16)
    R_wba = load_weight_groups(k, "wbag", wba, k.wba_d, 4, [(0, 512), (512, 1024)])
    R_wbb = load_weight_groups(k, "wbbg", wbb, k.wbb_d, 4, [(0, 512), (512, 1024)])
    R_wo = load_weight_groups(k, "wog", wo, k.wo_d, 8, [(0, 512), (512, 1024)])
    lng = A.alloc([1024])
    lnb = A.alloc([1024])
    P.dma("sp", lambda e: e.dma_start(out=lng, in_=k.lng_d.broadcast_to([128, 1024])), "init", writes=[RI])
    P.dma("sp", lambda e: e.dma_start(out=lnb, in_=k.lnb_d.broadcast_to([128, 1024])), "init", writes=[RI])
    alloc_xT(k)
    xt = [A.alloc([1024]), A.alloc([1024]), A.alloc([1024])]
    R_xt = [Res(), Res(), Res()]
    hg = A.alloc([1024])
    R_hg = [Res(), Res()]
    t1 = k.stage[0][:, 0:512]
    t2 = k.stage[0][:, 512:1024]
    R_t1 = R_t2 = k.R_stage[0]
    y2 = [A.alloc([1024], BF16), A.alloc([1024], BF16)]
    R_y2 = [Res(), Res()]
    yT = A.alloc([8, 128], BF16)
    R_yT = Res()
    r = k.stage[1]
    R_r = k.R_stage[1]
    ot = [A.alloc([1024]), A.alloc([1024])]
    R_ot = [Res(), Res()]
    st6 = A.alloc([12])
    mv = A.alloc([2])
    lnv = A.alloc([1])
    rstd = A.alloc([1])
    nb = A.alloc([1])
    R_st = Res()

    def s1_load(i):
        load_xT(k, i, i % 2, cast=False)
        P.dma("sp", lambda e: e.dma_start(out=xt[i % 3], in_=k.x_d[i * 128:(i + 1) * 128, :]), f"xt{i % 3}", writes=[R_xt[i % 3]])

    def s1(i):
        slot = i % 2
        load_xT(k, i, slot, dma=False)
        ts = slice(i * 128, (i + 1) * 128)
        for half in range(2):
            hs = slice(half * 512, (half + 1) * 512)
            project(k, slot, W3, b3bc, [(half * 512, half * 512 + 512), (1024 + half * 512, 1536 + half * 512)], _HG(hg, half), R_hg, R_W=R_W3)
            P.op("act", lambda e: e.activation(out=hg, in_=hg, func=AF.Tanh, scale=0.5), reads=R_hg, writes=R_hg)
            for c in range(4):
                P.op("pe", lambda e, c=c, hs=hs: e.matmul(ps[2][:, :], lhsT=k.oaT[:, c, ts], rhs=wba[:, c, hs], start=(c == 0), stop=(c == 3)), reads=[k.R_oaT, R_wba[half]], writes=[Rps[2]])
            for c in range(4):
                P.op("pe", lambda e, c=c, hs=hs: e.matmul(ps[3][:, :], lhsT=k.obT[:, c, ts], rhs=wbb[:, c, hs], start=(c == 0), stop=(c == 3)), reads=[k.R_obT, R_wbb[half]], writes=[Rps[3]])
            P.op("dve", lambda e: e.scalar_tensor_tensor(out=t1, in0=hg[:, 0:512], scalar=1.0, in1=ps[2][:, :], op0=ALU.add, op1=ALU.mult), reads=[R_hg[0], Rps[2]], writes=[R_t1])
            P.op("dve", lambda e: e.scalar_tensor_tensor(out=t2, in0=hg[:, 512:1024], scalar=1.0, in1=ps[3][:, :], op0=ALU.add, op1=ALU.mult), reads=[R_hg[1], Rps[3]], writes=[R_t2])
            P.op("pool", lambda e, hs=hs: e.tensor_tensor(out=y2[slot][:, hs], in0=t1, in1=t2, op=ALU.add), reads=[R_t1, R_t2], writes=[R_y2[slot]])

    def s2(i):
        slot = i % 2
        for c in range(8):
            P.op("pe", lambda e, c=c: e.transpose(out=psb[4][:, c * 128:(c + 1) * 128], in_=y2[slot][:, c * 128:(c + 1) * 128], identity=k.ident), reads=[R_y2[slot], RI], writes=[Rps[4]])
        P.op("act", lambda e: e.copy(out=yT[:].rearrange("p a b -> p (a b)"), in_=psb[4][:, 0:1024]), reads=[Rps[4]], writes=[R_yT])
        for half in range(2):
            hs = slice(half * 512, (half + 1) * 512)
            b = 5 + half
            for c in range(8):
                P.op("pe", lambda e, c=c, hs=hs, b=b: e.matmul(ps[b][:, :], lhsT=yT[:, c, :], rhs=wo[:, c, hs], start=(c == 0), stop=(c == 7)), reads=[R_yT, R_wo[half]], writes=[Rps[b]])
            P.op("dve", lambda e, hs=hs, b=b: e.scalar_tensor_tensor(out=r[:, hs], in0=ps[b][:, :], scalar=0.5 / ALPHA, in1=xt[i % 3][:, hs], op0=ALU.mult, op1=ALU.add), reads=[Rps[b], R_xt[i % 3]], writes=[R_r])
            P.op("dve", lambda e, hs=hs, half=half: e.bn_stats(out=st6[:, half * 6:half * 6 + 6], in_=r[:, hs]), reads=[R_r], writes=[R_st])
        P.op("dve", lambda e: e.bn_aggr(out=mv, in_=st6), reads=[R_st], writes=[R_st])
        P.op("act", lambda e: e.activation(out=lnv, in_=mv[:, 1:2], func=AF.Ln, bias=1e-5 / (ALPHA * ALPHA)), reads=[R_st], writes=[R_st])
        P.op("act", lambda e: e.activation(out=rstd, in_=lnv, func=AF.Exp, scale=-0.5), reads=[R_st], writes=[R_st])
        P.op("dve", lambda e: e.scalar_tensor_tensor(out=nb, in0=mv[:, 0:1], scalar=-1.0, in1=rstd, op0=ALU.mult, op1=ALU.mult), reads=[R_st], writes=[R_st])
        o = ot[slot]
        P.op("dve", lambda e: e.tensor_scalar(out=o, in0=r, scalar1=rstd[:, 0:1], scalar2=nb[:, 0:1], op0=ALU.mult, op1=ALU.add), reads=[R_r, R_st], writes=[R_ot[slot]])
        P.op("pool", lambda e: e.tensor_tensor(out=o, in0=o, in1=lng, op=ALU.mult), reads=[R_ot[slot], RI], writes=[R_ot[slot]])
        P.op("pool", lambda e: e.tensor_tensor(out=o, in0=o, in1=lnb, op=ALU.add), reads=[R_ot[slot], RI], writes=[R_ot[slot]])
        P.dma("sp", lambda e: e.dma_start(out=k.out_d[i * 128:(i + 1) * 128, :], in_=o), f"out{slot}", reads=[R_ot[slot]])

    s1_load(0)
    if NT > 1:
        s1_load(1)
    s1(0)
    for i in range(NT):
        if i + 2 < NT:
            s1_load(i + 2)
        if i + 1 < NT:
            s1(i + 1)
        s2(i)
    print("pass3 arena words", A.off)


class _HG:
    def __init__(self, hg, half):
        self.hg = hg
        self.half = half

    def __getitem__(self, key):
        _, cs = key
        c0 = cs.start
        o = 0 if c0 < 1024 else 512
        return self.hg[:, o:o + (cs.stop - cs.start)]


def _consts():
    p = np.arange(128)
    cst = np.zeros((128, 642), np.float32)
    cst[:, 0:128] = np.eye(128)
    cst[:, 128:256] = (p[:, None] <= p[None, :])
    cst[:, 256:384] = (p[:, None] > p[None, :])
    same = (p[:, None] // 64) == (p[None, :] // 64)
    cst[:, 384:512] = same & (p[:, None] <= p[None, :])
    cst[:, 512:640] = same & (p[:, None] > p[None, :])
    cst[:, 640] = p < 64
    cst[:, 641] = p >= 64
    maskc = np.zeros((128, 33, 128), np.float32)
    q = np.arange(128)
    for idx in range(33):
        if idx <= 16:
            i, nt = idx, 0
        else:
            i, nt = idx - 17 + 16, 1
        n = nt * 128 + p
        maskc[:, idx, :] = (16 * n[:, None] + 31) <= (128 * i + q[None, :])
    ovl = np.zeros((128, 2, 65), np.float32)
    n = np.arange(256)
    cs = n * 16
    js = np.arange(64) * 64
    ov = ((cs[:, None] < js[None, :] + 64) & (cs[:, None] + 32 > js[None, :])).astype(np.float32)
    ov[255] = 0.0
    ovl[:, :, 0:64] = ov.reshape(2, 128, 64).transpose(1, 0, 2)
    ovl[:, :, 64] = 1.0
    ovl[127, 1, 64] = 0.0
    inv = np.float32(10000.0) ** (-(np.arange(0, 64, 2, dtype=np.float32)) / np.float32(64))
    pos = np.arange(SEQ, dtype=np.float32)
    ang = (pos[:, None] * inv[None, :]).astype(np.float32)
    cos = np.cos(ang).astype(np.float32).reshape(32, 128, 32).transpose(1, 0, 2)
    sin = np.sin(ang).astype(np.float32).reshape(32, 128, 32).transpose(1, 0, 2)
    et = (np.arange(SEQ)[None, :] // 64 == np.arange(64)[:, None]).astype(np.float32)
    return dict(cst=cst, maskc=np.ascontiguousarray(maskc.reshape(128, -1)), ovl=np.ascontiguousarray(ovl.reshape(128, -1)),
                cos=np.ascontiguousarray(cos.reshape(128, -1)), sin=np.ascontiguousarray(sin.reshape(128, -1)), et=et)


def prep_shared(w_in, b_in, pe_cmp_k, w_cmp_k1, w_cmp_k2, pe_cmp_v, w_cmp_v1, w_cmp_v2,
                hgrn_lb_logits, hgrn_norm_g, w_branch_a, w_branch_b, w_out, ln_g, ln_b):
    f = lambda a: np.ascontiguousarray(np.asarray(a, dtype=np.float32))
    w, b = np.asarray(w_in[0]), np.asarray(b_in[0])
    perm1 = np.concatenate([np.arange(0, 512), np.arange(768, 896), np.arange(1024, 1152), np.arange(512, 640), np.arange(640, 768),
                            np.arange(896, 1024), np.arange(1152, 1280), np.arange(1280, 1304), np.arange(1304, 1816)])
    d = dict(
        w1=f(w[:, perm1]), b1=f(b[perm1][None, :]),
        w2=f(w[:, 1816:3864]), b2=f(b[None, 1816:3864]),
        w3=f(w[:, 3864:5912]), b3=f(b[None, 3864:5912]),
        wk1=f(np.asarray(w_cmp_k1[0]).reshape(32, 64, 128).transpose(1, 0, 2).reshape(64, 4096)),
        wv1=f(np.asarray(w_cmp_v1[0]).reshape(32, 64, 128).transpose(1, 0, 2).reshape(64, 4096)),
        wk2=f(w_cmp_k2[0]), wv2=f(w_cmp_v2[0]),
        pek=f(np.asarray(pe_cmp_k[0]).T), pev=f(np.asarray(pe_cmp_v[0]).T),
        lbl=f(hgrn_lb_logits), hg=f(hgrn_norm_g),
        wba=f(w_branch_a[0]), wbb=f(w_branch_b[0]), wo=f(w_out[0]), lng=f(ln_g), lnb=f(ln_b),
    )
    d.update(_consts())
    return d


def kernel(x, **params):
    x = np.asarray(x, dtype=np.float32)
    shared = prep_shared(**params)
    nc = build()
    in_maps = []
    for b in range(8):
        m = dict(shared)
        m["x"] = np.ascontiguousarray(x[b])
        m["xT"] = np.ascontiguousarray(x[b].T.reshape(8, 128, SEQ))
        in_maps.append(m)
    res = run_bass_kernel_spmd(nc, in_maps, core_ids=list(range(8)))
    return np.stack([np.asarray(r["out"]) for r in res.results], axis=0)
```

```python
import numpy as np
from contextlib import ExitStack
import concourse.bass as bass
import concourse.mybir as mybir
from concourse.bass_utils import run_bass_kernel_spmd

F32 = mybir.dt.float32
BF16 = mybir.dt.bfloat16
AF = mybir.ActivationFunctionType
ALU = mybir.AluOpType

NTILES = 32
SEQ = 4096
NEG = -30000.0
TINY = 1e-30
ALPHA = 2.0 ** 0.25
KEEPWARM = False


class Res:
    __slots__ = ("name", "w", "r")

    def __init__(self, name=""):
        self.name = name
        self.w = None
        self.r = {}


class Prog:
    ENG = ("pe", "act", "dve", "pool", "sp")
    CAP = 30000

    def __init__(self, nc, same_engine_sync=True):
        self.nc = nc
        self.ops = {e: [] for e in self.ENG}
        self.waited = {e: {} for e in self.ENG}
        self.dma_count = {}
        self.same_engine_sync = same_engine_sync

    def _deps(self, eng, reads, writes):
        toks = {}

        def add(ch, idx):
            if toks.get(ch, -1) < idx:
                toks[ch] = idx

        for r in reads:
            if r.w is not None:
                add(*r.w)
        for w in writes:
            if w.w is not None:
                add(*w.w)
            for ch, idx in w.r.items():
                add(ch, idx)
        waits = []
        for ch, idx in toks.items():
            if ch == eng and (eng == "pe" or not self.same_engine_sync):
                continue
            if self.waited[eng].get(ch, -1) >= idx:
                continue
            self.waited[eng][ch] = idx
            waits.append((ch, idx))
        return waits

    def _finish(self, tok, reads, writes):
        ch, idx = tok
        for r in reads:
            if r.r.get(ch, -1) < idx:
                r.r[ch] = idx
        for w in writes:
            w.w = tok
            w.r = {}

    def op(self, eng, fn, reads=(), writes=()):
        waits = self._deps(eng, reads, writes)
        idx = len(self.ops[eng])
        self.ops[eng].append(dict(fn=fn, waits=waits, ms=False, dma=None))
        self._finish((eng, idx), reads, writes)

    def dma(self, eng, fn, chan, reads=(), writes=()):
        waits = self._deps(eng, reads, writes)
        k = self.dma_count.get(chan, 0)
        self.dma_count[chan] = k + 1
        self.ops[eng].append(dict(fn=fn, waits=waits, ms=False, dma=chan))
        self._finish((("dma", chan), k), reads, writes)

    def _all_waits(self, eng, engines=True):
        waits = []
        if engines:
            for ch in self.ENG:
                if ch == eng:
                    continue
                last = -1
                for i in range(len(self.ops[ch]) - 1, -1, -1):
                    o = self.ops[ch][i]
                    if o["fn"] is not None and o["dma"] is None:
                        last = i
                        break
                if last >= 0 and self.waited[eng].get(ch, -1) < last:
                    self.waited[eng][ch] = last
                    waits.append((ch, last))
        for chan, k in self.dma_count.items():
            ch = ("dma", chan)
            if self.waited[eng].get(ch, -1) < k - 1:
                self.waited[eng][ch] = k - 1
                waits.append((ch, k - 1))
        return waits

    def barrier(self):
        allw = {e: self._all_waits(e) for e in self.ENG}
        for e in self.ENG:
            self.ops[e].append(dict(fn=None, waits=allw[e], ms=False, dma=None))

    def wait_all_dma(self, eng):
        self.ops[eng].append(dict(fn=None, waits=self._all_waits(eng, engines=False), ms=False, dma=None))

    def emit(self, stack):
        nc = self.nc
        for e in self.ENG:
            for o in self.ops[e]:
                for ch, idx in o["waits"]:
                    if isinstance(ch, str):
                        self.ops[ch][idx]["ms"] = True
        msnum = {}
        nsem = {}
        for e in self.ENG:
            c = 0
            for i, o in enumerate(self.ops[e]):
                if o["ms"]:
                    c += 1
                    msnum[(e, i)] = c
            nsem[e] = max(1, (c + self.CAP - 1) // self.CAP)
        esems = {e: [stack.enter_context(nc.semaphore(f"s_{e}_{j}")) for j in range(nsem[e])] for e in self.ENG}
        dsems = {ch: stack.enter_context(nc.semaphore(f"d_{ch}")) for ch in self.dma_count}
        CAP = self.CAP

        def semval(ch, idx):
            if isinstance(ch, str):
                m = msnum[(ch, idx)]
                return esems[ch][(m - 1) // CAP], (m - 1) % CAP + 1
            return dsems[ch[1]], 16 * (idx + 1)

        block = stack.enter_context(nc.Block())

        def run(e):
            def body(eng):
                for i, o in enumerate(self.ops[e]):
                    for ch, idx in o["waits"]:
                        s, v = semval(ch, idx)
                        eng.wait_ge(s, v)
                    if o["fn"] is None:
                        continue
                    ins = o["fn"](eng)
                    if o["dma"] is not None:
                        ins.then_inc(dsems[o["dma"]], 16)
                    elif o["ms"]:
                        m = msnum[(e, i)]
                        ins.then_inc(esems[e][(m - 1) // CAP], 1)
            return body

        block.tensor(run("pe"))
        block.scalar(run("act"))
        block.vector(run("dve"))
        block.gpsimd(run("pool"))
        block.sync(run("sp"))


class Arena:
    def __init__(self, handle, nwords):
        self.h = handle
        self.n = nwords
        self.off = 0

    def alloc(self, shape, dtype=F32, parts=128):
        nel = int(np.prod(shape))
        nw = (nel * (2 if dtype == BF16 else 4) + 3) // 4
        nw = (nw + 1) // 2 * 2
        assert self.off + nw <= self.n, f"SBUF arena overflow: {self.off}+{nw} > {self.n}"
        a = self.h[0:parts, self.off:self.off + nw]
        self.off += nw
        if dtype == BF16:
            a = a.bitcast(BF16)
        a = a[:, 0:nel]
        if len(shape) == 2:
            a = a.rearrange("p (a b) -> p a b", a=shape[0], b=shape[1])
        elif len(shape) == 3:
            a = a.rearrange("p (a b c) -> p a b c", a=shape[0], b=shape[1], c=shape[2])
        elif len(shape) != 1:
            raise ValueError(shape)
        return a


def bc(ap, shape):
    return ap.to_broadcast(list(shape))


class K:
    pass


def build(nt_tiles=NTILES, dbg=False, passes=(1, 2, 3), stop=99):
    nc = bass.Bass("TRN2", target_bir_lowering=False)
    k = K()
    k.stop = stop
    k.nc = nc
    k.NT = nt_tiles
    k.dbg = dbg
    S = SEQ

    def din(name, shape):
        return nc.dram_tensor(name, shape, F32, kind="ExternalInput").ap()

    k.xT_d = din("xT", [8, 128, S])
    k.x_d = din("x", [S, 1024])
    k.w1_d = din("w1", [1024, 1816])
    k.b1_d = din("b1", [1, 1816])
    k.w2_d = din("w2", [1024, 2048])
    k.b2_d = din("b2", [1, 2048])
    k.w3_d = din("w3", [1024, 2048])
    k.b3_d = din("b3", [1, 2048])
    k.wk1_d = din("wk1", [64, 32 * 128])
    k.wv1_d = din("wv1", [64, 32 * 128])
    k.wk2_d = din("wk2", [128, 64])
    k.wv2_d = din("wv2", [128, 64])
    k.pek_d = din("pek", [64, 32])
    k.pev_d = din("pev", [64, 32])
    k.lbl_d = din("lbl", [2, 512])
    k.hg_d = din("hg", [1, 512])
    k.wba_d = din("wba", [512, 1024])
    k.wbb_d = din("wbb", [512, 1024])
    k.wo_d = din("wo", [1024, 1024])
    k.lng_d = din("lng", [1, 1024])
    k.lnb_d = din("lnb", [1, 1024])
    k.cst_d = din("cst", [128, 642])
    k.maskc_d = din("maskc", [128, 33 * 128])
    k.ovl_d = din("ovl", [128, 2 * 65])
    k.cos_d = din("cos", [128, 32 * 32])
    k.sin_d = din("sin", [128, 32 * 32])
    k.et_d = din("et", [64, S])
    k.out_d = nc.dram_tensor("out", [S, 1024], F32, kind="ExternalOutput").ap()
    if dbg:
        k.dbg_oa = nc.dram_tensor("dbg_oa", [128, 4 * S], BF16, kind="ExternalOutput").ap()
        k.dbg_ob = nc.dram_tensor("dbg_ob", [128, 4 * S], BF16, kind="ExternalOutput").ap()

    P = Prog(nc)
    k.P = P
    with ExitStack() as st:
        ARENA_WORDS = 50 * 1024
        arena_h = st.enter_context(nc.sbuf_tensor("arena", [128, ARENA_WORDS], F32))
        k.A = Arena(arena_h, ARENA_WORDS)
        k.ps = [st.enter_context(nc.psum_tensor(f"ps{i}", [128, 512], F32)) for i in range(8)]
        k.psb = [p.bitcast(BF16) for p in k.ps]
        k.Rps = [Res(f"ps{i}") for i in range(8)]
        setup_persistent(k)
        if 1 in passes:
            mark = k.A.off
            pass1(k)
            P.barrier()
            k.A.off = mark
        if dbg:
            P.dma("sp", lambda e: e.dma_start(out=k.dbg_oa, in_=k.oaT[:].rearrange("p a b -> p (a b)")), "dbg", reads=[k.R_oaT])
        if 2 in passes:
            mark = k.A.off
            pass2(k)
            P.barrier()
            if dbg:
                P.dma("sp", lambda e: e.dma_start(out=k.dbg_ob, in_=k.obT[:].rearrange("p a b -> p (a b)")), "dbg", reads=[k.R_obT])
                P.barrier()
            k.A.off = mark
        if 3 in passes:
            pass3(k)
        P.wait_all_dma("sp")
        P.emit(st)
    return nc


def stage_cast(k, dst, src_d, parts, ncols, eng_cycle=("act", "dve", "pool"), p0=0):
    P = k.P
    c0 = 0
    while c0 < ncols:
        n = min(1024, ncols - c0)
        s = k.stage_i % 2
        k.stage_i += 1
        stg = k.stage[s]
        Rs = k.R_stage[s]
        P.dma("sp", lambda e, stg=stg, c0=c0, n=n: e.dma_start(out=stg[p0:p0 + parts, 0:n], in_=src_d[:, c0:c0 + n]), f"stg{s}", writes=[Rs])
        eng = eng_cycle[k.stage_i % len(eng_cycle)]
        d = dst[:, c0:c0 + n]
        if eng == "act":
            P.op("act", lambda e, d=d, stg=stg, n=n: e.copy(out=d, in_=stg[p0:p0 + parts, 0:n]), reads=[Rs], writes=[k.R_init])
        else:
            P.op(eng, lambda e, d=d, stg=stg, n=n: e.tensor_copy(out=d, in_=stg[p0:p0 + parts, 0:n]), reads=[Rs], writes=[k.R_init])
        c0 += n


def load_weight_groups(k, name, W, w_d, nchunks, colgroups):
    P = k.P
    res = []
    for gi, (c0, c1) in enumerate(colgroups):
        r = Res(f"{name}{gi}")
        src = w_d[:, c0:c1].rearrange("(c p) n -> p c n", p=128)
        P.dma("pool", lambda e, c0=c0, c1=c1, src=src: e.dma_start(out=W[:, :, c0:c1], in_=src), f"{name}{gi}", writes=[r])
        res.append(r)
    return res


def load_cast(k, dst, src_d):
    k.P.dma("pool", lambda e: e.dma_start(out=dst, in_=src_d), "initc", writes=[k.R_initc])


def join_init(k):
    k.P.op("pool", lambda e: e.memset(k.joinbuf, 0.0), reads=[k.R_init, k.R_initc], writes=[k.R_init])


def setup_persistent(k):
    P, A = k.P, k.A
    k.R_init = Res("init")
    k.R_initc = Res("initc")
    k.joinbuf = A.alloc([2])
    k.stage = [A.alloc([1024]), A.alloc([1024])]
    k.R_stage = [Res("stg0"), Res("stg1")]
    k.stage_i = 0
    k.cstf = A.alloc([642])
    P.dma("sp", lambda e: e.dma_start(out=k.cstf, in_=k.cst_d), "init", writes=[k.R_init])
    k.cstb = A.alloc([512], BF16)
    P.op("dve", lambda e: e.tensor_copy(out=k.cstb, in_=k.cstf[:, 0:512]), reads=[k.R_init], writes=[k.R_init])
    k.ident = k.cstb[:, 0:128]
    k.tri = k.cstb[:, 128:256]
    k.win2 = k.cstb[:, 256:384]
    k.mintra_b = k.cstb[:, 384:512]
    k.mintra_f = k.cstf[:, 384:512]
    k.mrev_f = k.cstf[:, 512:640]
    k.cind_f = k.cstf[:, 640:642]
    k.oaT = A.alloc([4, SEQ], BF16)
    k.R_oaT = Res("oaT")


def load_xT(k, i, slot, dma=True, cast=True):
    P = k.P
    xs = k.xTs[slot]
    xb = k.xTb[slot]
    src = k.xT_d[:, :, i * 128:(i + 1) * 128].rearrange("c p t -> p c t")
    if dma:
        P.dma("sp", lambda e: e.dma_start(out=xs, in_=src), f"xT{slot}", writes=[k.R_xTs[slot]])
    if not cast:
        return
    if getattr(k, "xcast", "pool") == "act":
        P.op("act", lambda e: e.copy(out=xb, in_=xs), reads=[k.R_xTs[slot]], writes=[k.R_xTb[slot]])
    else:
        P.op("pool", lambda e: e.tensor_copy(out=xb, in_=xs), reads=[k.R_xTs[slot]], writes=[k.R_xTb[slot]])


def project(k, slot, W, bbc, groups, h, R_h, banks=(0, 1), R_W=None):
    P = k.P
    xb = k.xTb[slot]
    for gi, (c0, c1) in enumerate(groups):
        b = banks[gi % len(banks)]
        bank = k.ps[b]
        for c in range(8):
            P.op("pe", lambda e, bank=bank, c=c, c0=c0, c1=c1: e.matmul(bank[:, 0:c1 - c0], lhsT=xb[:, c, :], rhs=W[:, c, c0:c1], start=(c == 0), stop=(c == 7)),
                 reads=[k.R_xTb[slot], k.R_init if R_W is None else R_W[c0 // 512]], writes=[k.Rps[b]])
        P.op("dve", lambda e, bank=bank, c0=c0, c1=c1: e.tensor_tensor(out=h[:, c0:c1], in0=bank[:, 0:c1 - c0], in1=bbc[:, c0:c1], op=ALU.add),
             reads=[k.Rps[b], k.R_init], writes=[R_h[gi]])


def pass1(k):
    P, A, NT = k.P, k.A, k.NT
    ps, psb, Rps = k.ps, k.psb, k.Rps
    RI = k.R_init
    W1 = A.alloc([8, 1816], BF16)
    R_W1 = load_weight_groups(k, "W1g", W1, k.w1_d, 8, [(0, 512), (512, 1024), (1024, 1304), (1304, 1816)])
    b1bc = A.alloc([1816])
    P.dma("sp", lambda e: e.dma_start(out=b1bc, in_=k.b1_d.broadcast_to([128, 1816])), "init", writes=[RI])
    cos = A.alloc([32, 32])
    sin = A.alloc([32, 32])
    P.dma("sp", lambda e: e.dma_start(out=cos[:].rearrange("p a b -> p (a b)"), in_=k.cos_d), "init", writes=[RI])
    P.dma("sp", lambda e: e.dma_start(out=sin[:].rearrange("p a b -> p (a b)"), in_=k.sin_d), "init", writes=[RI])
    maskc = A.alloc([33, 128], BF16)
    load_cast(k, maskc[:].rearrange("p a b -> p (a b)"), k.maskc_d)
    ovl = A.alloc([2, 65], BF16)
    load_cast(k, ovl[:].rearrange("p a b -> p (a b)"), k.ovl_d)
    wk1 = A.alloc([32, 128], BF16)
    wv1 = A.alloc([32, 128], BF16)
    load_cast(k, wk1[0:64].rearrange("p a b -> p (a b)"), k.wk1_d)
    load_cast(k, wv1[0:64].rearrange("p a b -> p (a b)"), k.wv1_d)
    wk2 = A.alloc([64], BF16)
    wv2 = A.alloc([64], BF16)
    load_cast(k, wk2, k.wk2_d)
    load_cast(k, wv2, k.wv2_d)
    pek = A.alloc([32], BF16)
    pev = A.alloc([32], BF16)
    load_cast(k, pek[0:64], k.pek_d)
    load_cast(k, pev[0:64], k.pev_d)
    KaT = A.alloc([2, SEQ], BF16)
    for g in range(2):
        load_cast(k, KaT[64:128, g, :], k.et_d)
    R_KaT = [Res() for _ in range(NT)]
    KwT = A.alloc([6, 2, 128], BF16)
    R_KwT = [Res() for _ in range(6)]
    Vsel = A.alloc([32, 2, 65], BF16)
    R_Vsel = [Res() for _ in range(NT)]
    Vwin = A.alloc([6, 2, 65], BF16)
    R_Vwin = [Res() for _ in range(6)]
    kcT = A.alloc([2, 256], BF16)
    hsTv = A.alloc([2, 256], BF16)
    vca = A.alloc([2, 2, 65], BF16)
    R_kc, R_hsv, R_vca = Res("kc"), Res("hsv"), Res("vca")
    P.op("pool", lambda e: e.memset(kcT, 0.0), writes=[R_kc])
    P.op("pool", lambda e: e.memset(hsTv, 0.0), writes=[R_hsv])
    P.op("pool", lambda e: e.memset(vca, 0.0), writes=[R_vca])
    P.op("pool", lambda e: e.memset(vca[:, :, :, 64:65], 1.0), reads=[R_vca], writes=[R_vca])
    P.op("pool", lambda e: e.memset(Vsel[:, :, :, 64:65], 1.0), writes=R_Vsel)
    P.op("pool", lambda e: e.memset(Vwin[:, :, :, 64:65], 1.0), writes=R_Vwin)
    kvcT = A.alloc([4, 144], BF16)
    R_kvcT = Res("kvcT")
    P.op("pool", lambda e: e.memset(kvcT, 0.0), writes=[R_kvcT])
    ck = A.alloc([2])
    R_ck = Res("ck")

    def emit_ck():
        for (w1_, pe_, col) in ((wk1, pek, 0), (wv1, pev, 1)):
            for l in range(32):
                P.op("pe", lambda e, w1_=w1_, pe_=pe_, l=l, col=col: e.matmul(ps[3][:, col:col + 1], lhsT=w1_[0:64, l, :], rhs=pe_[0:64, l:l + 1], start=(l == 0), stop=(l == 31)),
                     reads=[RI], writes=[Rps[3]])
        P.op("dve", lambda e: e.tensor_copy(out=ck, in_=ps[3][:, 0:2]), reads=[Rps[3]], writes=[R_ck])

    join_init(k)
    k.xTs = [A.alloc([8, 128]), A.alloc([8, 128])]
    k.xTb = [A.alloc([8, 128], BF16), A.alloc([8, 128], BF16)]
    k.R_xTs = [Res(), Res()]
    k.R_xTb = [Res(), Res()]
    h = A.alloc([1816])
    R_h = [Res() for _ in range(4)]
    groups = [(0, 512), (512, 1024), (1024, 1304), (1304, 1816)]
    tq = [A.alloc([8, 32]) for _ in range(4)]
    R_tq = [Res() for _ in range(4)]
    Qaug = A.alloc([8, 128], BF16)
    R_Qaug = Res()
    qn = A.alloc([512], BF16)
    R_qn = Res()
    kr = A.alloc([4, 64], BF16)
    R_kr = Res()
    kvc = A.alloc([256], BF16)
    R_kvc = Res()
    QnT = A.alloc([8, 128], BF16)
    R_QnT = Res()
    QaT = A.alloc([8, 128], BF16)
    R_QaT = Res()
    gth = A.alloc([24])
    gs = A.alloc([8, 3])
    R_gs = Res()
    zs = A.alloc([512])
    R_zs = Res()
    u = A.alloc([32])
    th = A.alloc([32])
    hsf = A.alloc([32])
    hsk = A.alloc([16], BF16)
    R_u, R_th, R_hsf, R_hsk = Res(), Res(), Res(), Res()
    NPB = 6
    Pb = [A.alloc([512], BF16) for _ in range(NPB)]
    R_Pb = [Res() for _ in range(NPB)]
    pb_i = [0]
    rd = A.alloc([4])
    R_rd = Res()
    imp = A.alloc([2, 64])
    R_imp = Res()
    m8a = A.alloc([8])
    m8b = A.alloc([8])
    impt = A.alloc([64])
    R_m8 = Res()
    negm = A.alloc([2, 64])
    R_negm = Res()
    cfs = [A.alloc([4]), A.alloc([4])]
    R_cfs = [Res(), Res()]
    tmpc = A.alloc([4, 64])
    R_tmpc = Res()
    oab = A.alloc([512], BF16)
    R_oab = Res()
    QaTs = [QaT, A.alloc([8, 128], BF16)]
    R_QaTs = [Res(), Res()]
    accs = [A.alloc([8, 64]), A.alloc([8, 64])]
    R_accs = [[Res(), Res()], [Res(), Res()]]
    gss = [gs, A.alloc([8, 3])]
    R_gss = [Res(), Res()]
    zss = [zs, A.alloc([512])]
    R_zss = [Res(), Res()]
    sc_cnt = {}
    pv_cnt = {}
    pvs = A.alloc([260])
    R_pvs = Res()
    print("pass1 arena words", A.off)

    def add_branch(items, kts, lhs_of, rhs_q, qres, v_of, masks, sbanks, pvbanks, extra=None, done=None, ci=1):
        key = tuple(pvbanks)
        cnt = pv_cnt.get(key, 0)
        pv_cnt[key] = cnt + 1
        pvb = pvbanks[cnt % len(pvbanks)]
        pv = ps[pvb][:, 0:260].rearrange("p (h e) -> p h e", h=4)
        for idx, kt in enumerate(kts):
            lhsT, lres = lhs_of(kt)
            va, vres = v_of(kt)
            items.append(dict(kt=kt, idx=idx, n=len(kts), lhsT=lhsT, lres=lres, rhs=rhs_q, qres=qres, va=va, vres=vres, mask=masks(kt),
                              sbanks=sbanks, pvb=pvb, pv=pv, extra=extra, done=done, ci=ci))

    def flush(items, hook=None):
        def score(it):
            assert len(it["sbanks"]) >= 2
            key = tuple(it["sbanks"])
            cnt = sc_cnt.get(key, 0)
            sc_cnt[key] = cnt + 1
            sb = it["sbanks"][cnt % len(it["sbanks"])]
            it["sb"] = sb
            P.op("pe", lambda e: e.matmul(ps[sb][:, :], lhsT=it["lhsT"], rhs=it["rhs"], start=True, stop=True),
                 reads=it["lres"] + it["qres"], writes=[Rps[sb]])

        def rest(it):
            sb = it["sb"]
            pi = pb_i[0] % NPB
            pb_i[0] += 1
            pt = Pb[pi]
            P.op("act", lambda e: e.activation(out=pt, in_=ps[sb][:, :], func=AF.Exp, scale=0.125), reads=[Rps[sb]], writes=[R_Pb[pi]])
            m = it["mask"]
            if m is not None:
                pt3 = pt.rearrange("p (h q) -> p h q", h=4)
                P.op("dve", lambda e: e.tensor_tensor(out=pt3, in0=pt3, in1=bc(m[:, None, :], [128, 4, 128]), op=ALU.mult),
                     reads=[R_Pb[pi], RI], writes=[R_Pb[pi]])
            pv, pvb, idx, n, va = it["pv"], it["pvb"], it["idx"], it["n"], it["va"]
            for hh in range(4):
                P.op("pe", lambda e, hh=hh: e.matmul(pv[:, hh, :], lhsT=pt[:, hh * 128:(hh + 1) * 128], rhs=va, start=(idx == 0 and hh == 0), stop=(idx == n - 1), skip_group_check=True),
                     reads=[R_Pb[pi]] + it["vres"], writes=[Rps[pvb]])
            if KEEPWARM:
                P.op("pe", lambda e: e.matmul(ps[pvb][:, 260:512], lhsT=pt[:, 384:512], rhs=pt[:, 0:252], start=False, stop=False, skip_group_check=True),
                     reads=[R_Pb[pi]], writes=[Rps[pvb]])
            if it["extra"] is not None:
                it["extra"](it, pt, pi)
            if idx == n - 1 and it["done"] is not None:
                if it["ci"] == 1:
                    P.op("dve", lambda e: e.tensor_copy(out=pvs, in_=ps[pvb][:, 0:260]), reads=[Rps[pvb]], writes=[R_pvs])
                    it["done"](None, pvs.rearrange("p (h e) -> p h e", h=4))
                else:
                    it["done"](pvb, pv)

        if not items:
            return
        look = len(items[0]["sbanks"]) - 1
        for j in range(min(look, len(items))):
            score(items[j])
        for j, it in enumerate(items):
            if j + look < len(items):
                score(items[j + look])
            rest(it)
            if hook is not None:
                hook(j)

    def combine(par, g, br, pvb, pv, first, ci):
        cf, R_cf = cfs[ci], R_cfs[ci]
        acc, R_acc, gs_, R_gs_ = accs[par], R_accs[par], gss[par], R_gss[par]
        R_src = R_pvs if pvb is None else Rps[pvb]
        P.op("dve", lambda e: e.tensor_scalar_max(out=cf, in0=pv[:, :, 64], scalar1=TINY), reads=[R_src], writes=[R_cf])
        P.op("dve", lambda e: e.reciprocal(out=cf, in_=cf), reads=[R_cf], writes=[R_cf])
        P.op("dve", lambda e: e.tensor_tensor(out=cf, in0=cf, in1=gs_[:, 4 * g:4 * g + 4, br], op=ALU.mult), reads=[R_cf, R_gs_], writes=[R_cf])
        accg = acc[:, 4 * g:4 * g + 4, :]
        if first:
            P.op("dve", lambda e: e.tensor_tensor(out=accg, in0=pv[:, :, 0:64], in1=bc(cf[:, :, None], [128, 4, 64]), op=ALU.mult),
                 reads=[R_src, R_cf], writes=[R_acc[g]])
        else:
            P.op("dve", lambda e: e.tensor_tensor(out=tmpc, in0=pv[:, :, 0:64], in1=bc(cf[:, :, None], [128, 4, 64]), op=ALU.mult),
                 reads=[R_src, R_cf], writes=[R_tmpc])
            P.op("pool", lambda e: e.tensor_tensor(out=accg, in0=accg, in1=tmpc, op=ALU.add), reads=[R_tmpc, R_acc[g]], writes=[R_acc[g]])

    def rope(src, nh, cb, sb_, dst, R_src, R_dst, eng):
        t = [x[:, 0:nh, :] for x in tq]
        P.op(eng, lambda e: e.tensor_tensor(out=t[0], in0=src[:, :, 0, :], in1=cb, op=ALU.mult), reads=[R_src, RI], writes=[R_tq[0]])
        P.op(eng, lambda e: e.tensor_tensor(out=t[1], in0=src[:, :, 1, :], in1=sb_, op=ALU.mult), reads=[R_src, RI], writes=[R_tq[1]])
        P.op(eng, lambda e: e.tensor_tensor(out=t[2], in0=src[:, :, 1, :], in1=cb, op=ALU.mult), reads=[R_src, RI], writes=[R_tq[2]])
        P.op(eng, lambda e: e.tensor_tensor(out=t[3], in0=src[:, :, 0, :], in1=sb_, op=ALU.mult), reads=[R_src, RI], writes=[R_tq[3]])
        P.op(eng, lambda e: e.tensor_tensor(out=dst[:, :, 0:32], in0=t[0], in1=t[1], op=ALU.subtract), reads=[R_tq[0], R_tq[1]], writes=[R_dst])
        P.op(eng, lambda e: e.tensor_tensor(out=dst[:, :, 32:64], in0=t[2], in1=t[3], op=ALU.add), reads=[R_tq[2], R_tq[3]], writes=[R_dst])

    def stage_a(i):
        slot = i % 2
        par = i % 2
        QaT_, R_QaT_ = QaTs[par], R_QaTs[par]
        gs_, R_gs_, zs_, R_zs_ = gss[par], R_gss[par], zss[par], R_zss[par]
        if i == 0:
            load_xT(k, 0, 0, cast=False)
        if i + 1 < NT:
            load_xT(k, i + 1, (i + 1) % 2, cast=False)
        k.xcast = "act"
        load_xT(k, i, slot, dma=False)
        yield
        yield
        xb = k.xTb[slot]
        for gi, (c0, c1) in enumerate(groups):
            b = gi % 2
            for c in range(8):
                P.op("pe", lambda e, b=b, c=c, c0=c0, c1=c1: e.matmul(ps[b][:, 0:c1 - c0], lhsT=xb[:, c, :], rhs=W1[:, c, c0:c1], start=(c == 0), stop=(c == 7)),
                     reads=[k.R_xTb[slot], R_W1[gi]], writes=[Rps[b]])
                if c == 3:
                    yield
            P.op("dve", lambda e, b=b, c0=c0, c1=c1: e.tensor_tensor(out=h[:, c0:c1], in0=ps[b][:, 0:c1 - c0], in1=b1bc[:, c0:c1], op=ALU.add),
                 reads=[Rps[b], RI], writes=[R_h[gi]])
            yield
        cosb8 = bc(cos[:, i:i + 1, :], [128, 8, 32])
        sinb8 = bc(sin[:, i:i + 1, :], [128, 8, 32])
        cosb4 = bc(cos[:, i:i + 1, :], [128, 4, 32])
        sinb4 = bc(sin[:, i:i + 1, :], [128, 4, 32])
        hq = h[:, 0:512].rearrange("p (h t j) -> p h t j", h=8, t=2, j=32)
        hk = h[:, 512:768].rearrange("p (h t j) -> p h t j", h=4, t=2, j=32)
        P.op("act", lambda e: e.copy(out=qn, in_=h[:, 0:512]), reads=[R_h[0]], writes=[R_qn])
        P.op("act", lambda e: e.copy(out=kvc, in_=h[:, 768:1024]), reads=[R_h[1]], writes=[R_kvc])
        rope(hk, 4, cosb4, sinb4, kr, R_h[1], R_kr, "pool")
        rope(hq, 8, cosb8, sinb8, Qaug, R_h[0], R_Qaug, "pool")
        ws = i % 6
        P.op("pool", lambda e: e.tensor_copy(out=Vsel[:, i, :, 0:64], in_=h[:, 1024:1152].rearrange("p (g d) -> p g d", g=2)), reads=[R_h[2]], writes=[R_Vsel[i]])
        P.op("pool", lambda e: e.tensor_copy(out=Vwin[:, ws, :, 0:64], in_=h[:, 1152:1280].rearrange("p (g d) -> p g d", g=2)), reads=[R_h[2]], writes=[R_Vwin[ws]])
        P.op("act", lambda e: e.activation(out=gth, in_=h[:, 1280:1304], func=AF.Tanh, scale=0.5), reads=[R_h[2]], writes=[R_gs_])
        P.op("dve", lambda e: e.tensor_scalar(out=gs_[:].rearrange("p a b -> p (a b)"), in0=gth, scalar1=0.5, scalar2=0.5, op0=ALU.mult, op1=ALU.add), reads=[R_gs_], writes=[R_gs_])
        P.op("act", lambda e: e.activation(out=zs_, in_=h[:, 1304:1816], func=AF.Tanh, scale=0.5), reads=[R_h[3]], writes=[R_zs_])
        P.op("dve", lambda e: e.scalar_tensor_tensor(out=zs_, in0=zs_, scalar=1.0, in1=h[:, 1304:1816], op0=ALU.add, op1=ALU.mult), reads=[R_zs_, R_h[3]], writes=[R_zs_])
        yield
        yield
        for hh in range(8):
            P.op("pe", lambda e, hh=hh: e.transpose(out=psb[2][0:64, hh * 128:(hh + 1) * 128], in_=qn[:, hh * 64:(hh + 1) * 64], identity=k.ident), reads=[R_qn, RI], writes=[Rps[2]])
        P.op("dve", lambda e: e.tensor_copy(out=QnT[0:64].rearrange("p a b -> p (a b)"), in_=psb[2][0:64, 0:1024]), reads=[Rps[2]], writes=[R_QnT])
        yield
        for j in range(4):
            P.op("pe", lambda e, j=j: e.transpose(out=psb[3][0:64, (4 + j) * 128:(5 + j) * 128], in_=kvc[:, j * 64:(j + 1) * 64], identity=k.ident), reads=[R_kvc, RI], writes=[Rps[3]])
        P.op("dve", lambda e: e.tensor_copy(out=kvcT[0:64, :, 0:16], in_=kvcT[0:64, :, 128:144]), reads=[R_kvcT], writes=[R_kvcT])
        P.op("dve", lambda e: e.tensor_copy(out=kvcT[0:64, :, 16:144], in_=psb[3][0:64, 512:1024].rearrange("p (j t) -> p j t", j=4)), reads=[Rps[3], R_kvcT], writes=[R_kvcT])
        yield
        yield
        if i == 0:
            emit_ck()
        m0 = 1 if i == 0 else 0
        nb = 8 - m0
        n0 = 8 * i - 1 + m0
        for (w1_, j0, col0) in ((wk1, 0, 0), (wv1, 2, 16)):
            o_ap = ps[3][:, col0:col0 + 16]
            for l in range(32):
                P.op("pe", lambda e, w1_=w1_, l=l, j0=j0, o_ap=o_ap: e.matmul(o_ap, lhsT=w1_[0:64, l, :], rhs=kvcT[0:64, j0:j0 + 2, l:l + 16 * 7 + 1:16], start=(l == 0), stop=(l == 31)),
                     reads=[R_kvcT, RI], writes=[Rps[3]])
                if l == 15:
                    yield
            yield
        for col0, cc in ((0, 0), (16, 1)):
            P.op("dve", lambda e, col0=col0, cc=cc: e.tensor_scalar(out=u[:, col0:col0 + 16], in0=ps[3][:, col0:col0 + 16], scalar1=ck[:, cc:cc + 1], scalar2=None, op0=ALU.add), reads=[Rps[3], R_ck], writes=[R_u])
        P.op("act", lambda e: e.activation(out=th, in_=u, func=AF.Tanh, scale=0.5), reads=[R_u], writes=[R_th])
        P.op("dve", lambda e: e.scalar_tensor_tensor(out=hsf, in0=th, scalar=1.0, in1=u, op0=ALU.add, op1=ALU.mult), reads=[R_th, R_u], writes=[R_hsf])
        P.op("dve", lambda e: e.tensor_scalar(out=hsk, in0=hsf[:, 0:16], scalar1=0.5, scalar2=None, op0=ALU.mult), reads=[R_hsf], writes=[R_hsk])
        P.op("dve", lambda e: e.tensor_scalar(out=hsTv[:, :, n0:n0 + nb], in0=hsf[:, 16:32].rearrange("p (g m) -> p g m", g=2)[:, :, m0:8], scalar1=0.5, scalar2=None, op0=ALU.mult), reads=[R_hsf, R_hsv], writes=[R_hsv])
        for j in range(4):
            P.op("pe", lambda e, j=j: e.transpose(out=psb[2][0:64, j * 128:(j + 1) * 128], in_=kr[:, j, :], identity=k.ident), reads=[R_kr, RI], writes=[Rps[2]])
        P.op("act", lambda e: e.copy(out=KaT[0:64, :, i * 128:(i + 1) * 128], in_=psb[2][0:64, 0:256].rearrange("p (g t) -> p g t", g=2)), reads=[Rps[2]], writes=[R_KaT[i]])
        P.op("act", lambda e: e.copy(out=KwT[0:64, ws, :, :], in_=psb[2][0:64, 256:512].rearrange("p (g t) -> p g t", g=2)), reads=[Rps[2]], writes=[R_KwT[ws]])
        yield
        yield
        yield
        P.op("pe", lambda e: e.matmul(ps[3][0:64, 32:48], lhsT=wk2, rhs=hsk, start=True, stop=True), reads=[R_hsk, RI], writes=[Rps[3]])
        kc_ps = ps[3][0:64, 32:48].rearrange("p (g m) -> p g m", g=2)[:, :, m0:8]
        P.op("act", lambda e: e.copy(out=kcT[0:64, :, n0:n0 + nb], in_=kc_ps), reads=[Rps[3], R_kc], writes=[R_kc])
        for nt in sorted({n0 // 128, (8 * i + 6) // 128}):
            for g in range(2):
                P.op("pe", lambda e, nt=nt, g=g: e.matmul(ps[3][:, 64:128], lhsT=hsTv[:, g, nt * 128:(nt + 1) * 128], rhs=wv2, start=True, stop=True), reads=[R_hsv, RI], writes=[Rps[3]])
                P.op("act", lambda e, nt=nt, g=g: e.copy(out=vca[:, nt, g, 0:64], in_=ps[3][:, 64:128]), reads=[Rps[3], R_vca], writes=[R_vca])
        yield
        yield
        nts = [0] if 8 * i + 6 < 128 else [0, 1]

        def cmp_mask(nt):
            if nt == 0 and i <= 16:
                return maskc[:, i, :]
            if nt == 1 and i >= 16:
                return maskc[:, 17 + i - 16, :]
            return None

        imp_ps = ps[3][:, 128:388].rearrange("p (h e) -> p h e", h=4)

        def cmp_group(g):
            def extra(it, pt, pi):
                nt = it["kt"]
                for hh in range(4):
                    P.op("pe", lambda e, hh=hh: e.matmul(imp_ps[:, hh, :], lhsT=pt[:, hh * 128:(hh + 1) * 128], rhs=ovl[:, nt, :], start=(nt == nts[0] and hh == 0), stop=(nt == nts[-1]), skip_group_check=True),
                         reads=[R_Pb[pi], RI], writes=[Rps[3]])

            def done(pvb, pv):
                combine(par, g, 0, pvb, pv, True, 0)
                P.op("dve", lambda e: e.tensor_scalar_max(out=rd, in0=imp_ps[:, :, 64], scalar1=TINY), reads=[Rps[3]], writes=[R_rd])
                P.op("dve", lambda e: e.reciprocal(out=rd, in_=rd), reads=[R_rd], writes=[R_rd])
                P.op("dve", lambda e: e.tensor_scalar(out=imp[:, g, :], in0=imp_ps[:, 0, 0:64], scalar1=rd[:, 0:1], scalar2=None, op0=ALU.mult), reads=[Rps[3], R_rd], writes=[R_imp])
                for hh in range(1, 4):
                    P.op("dve", lambda e, hh=hh: e.scalar_tensor_tensor(out=imp[:, g, :], in0=imp_ps[:, hh, 0:64], scalar=rd[:, hh:hh + 1], in1=imp[:, g, :], op0=ALU.mult, op1=ALU.add),
                         reads=[Rps[3], R_rd, R_imp], writes=[R_imp])

            items = []
            add_branch(items, nts, lambda nt: (kcT[0:64, g, nt * 128:(nt + 1) * 128], [R_kc]), QnT[0:64, 4 * g:4 * g + 4, :], [R_QnT],
                       lambda nt: (vca[:, nt, g, :], [R_vca]), cmp_mask, [0, 1], [2], extra=extra, done=done, ci=0)
            flush(items)

        for g in range(2):
            cmp_group(g)
            yield
        if i < 8:
            P.op("pool", lambda e: e.memset(Qaug[:, :, 64:128], 0.0), reads=[R_Qaug], writes=[R_Qaug])
        else:
            c0, c1 = 2 * i, 2 * i + 1
            P.op("pool", lambda e: e.memset(imp[0:64, :, c0 - 1:64], -1.0), reads=[R_imp], writes=[R_imp])
            P.op("pool", lambda e: e.memset(imp[64:128, :, c1 - 1:64], -1.0), reads=[R_imp], writes=[R_imp])
            P.op("pool", lambda e: e.memset(imp[:, :, 0:1], -1.0), reads=[R_imp], writes=[R_imp])
            for g in range(2):
                P.op("dve", lambda e, g=g: e.max(out=m8a, in_=imp[:, g, :]), reads=[R_imp], writes=[R_m8])
                P.op("dve", lambda e, g=g: e.match_replace(out=impt, in_to_replace=m8a, in_values=imp[:, g, :], imm_value=-2.0), reads=[R_imp, R_m8], writes=[R_m8])
                P.op("dve", lambda e: e.max(out=m8b, in_=impt), reads=[R_m8], writes=[R_m8])
                P.op("dve", lambda e, g=g: e.tensor_scalar(out=negm[:, g, :], in0=imp[:, g, :], scalar1=m8b[:, 4:5], scalar2=NEG, op0=ALU.is_lt, op1=ALU.mult), reads=[R_imp, R_m8], writes=[R_negm])
            P.op("pool", lambda e: e.memset(negm[0:64, :, c0 - 1:c0 + 1], 0.0), reads=[R_negm], writes=[R_negm])
            P.op("pool", lambda e: e.memset(negm[64:128, :, c1 - 1:c1 + 1], 0.0), reads=[R_negm], writes=[R_negm])
            P.op("pool", lambda e: e.memset(negm[:, :, 0:1], 0.0), reads=[R_negm], writes=[R_negm])
            for g in range(2):
                P.op("pool", lambda e, g=g: e.tensor_copy(out=Qaug[:, 4 * g:4 * g + 4, 64:128], in_=bc(negm[:, g:g + 1, :], [128, 4, 64])), reads=[R_negm, R_Qaug], writes=[R_Qaug])
        yield
        yield
        yield
        yield
        for hh in range(8):
            P.op("pe", lambda e, hh=hh: e.transpose(out=psb[2][:, hh * 128:(hh + 1) * 128], in_=Qaug[:, hh, :], identity=k.ident), reads=[R_Qaug, RI], writes=[Rps[2]])
        P.op("dve", lambda e: e.tensor_copy(out=QaT_[:].rearrange("p a b -> p (a b)"), in_=psb[2][:, 0:1024]), reads=[Rps[2]], writes=[R_QaT_])
        yield

    N_A_STEPS = 33

    def stage_b(i, agen, prev_tail=None):
        par = i % 2
        QaT_, R_QaT_ = QaTs[par], R_QaTs[par]
        wkts = list(range(max(0, i - 4), i + 1))

        def win_mask(kt):
            if kt == i:
                return k.tri
            if kt == i - 4:
                return k.win2
            return None

        items = []
        for g in range(2):
            add_branch(items, list(range(i + 1)), (lambda g: lambda kt: (KaT[:, g, kt * 128:(kt + 1) * 128], [R_KaT[kt], RI]))(g), QaT_[:, 4 * g:4 * g + 4, :], [R_QaT_],
                       (lambda g: lambda kt: (Vsel[:, kt, g, :], [R_Vsel[kt]]))(g), lambda kt: k.tri if kt == i else None, [4, 5, 6], [7],
                       done=(lambda g: lambda pvb, pv: combine(par, g, 1, pvb, pv, False, 1))(g))
        for g in range(2):
            add_branch(items, wkts, (lambda g: lambda kt: (KwT[0:64, kt % 6, g, :], [R_KwT[kt % 6]]))(g), QaT_[0:64, 4 * g:4 * g + 4, :], [R_QaT_],
                       (lambda g: lambda kt: (Vwin[:, kt % 6, g, :], [R_Vwin[kt % 6]]))(g), win_mask, [4, 5, 6], [7],
                       done=(lambda g: lambda pvb, pv: combine(par, g, 2, pvb, pv, False, 1))(g))
        n = len(items)
        taken = [0]

        pt_ = [prev_tail, None]

        def hook(j):
            want = ((j + 1) * N_A_STEPS + n - 1) // n
            if pt_[0] is not None and (j >= 2 or want > 6 or j == n - 1):
                pt_[0][0]()
                pt_[1] = pt_[0][1]
                pt_[0] = None
            if pt_[1] is not None and items[j]["idx"] == items[j]["n"] - 1:
                pt_[1]()
                pt_[1] = None
            if agen is None:
                return
            while taken[0] < want:
                taken[0] += 1
                next(agen, None)

        flush(items, hook)
        if pt_[0] is not None:
            pt_[0][0]()
            pt_[1] = pt_[0][1]
            pt_[0] = None
        if pt_[1] is not None:
            pt_[1]()
            pt_[1] = None
        if agen is not None:
            for _ in agen:
                pass
        acc, R_acc, zs_, R_zs_ = accs[par], R_accs[par], zss[par], R_zss[par]

        def tail_dve():
            P.op("dve", lambda e: e.scalar_tensor_tensor(out=oab, in0=acc[:].rearrange("p a b -> p (a b)"), scalar=0.5, in1=zs_, op0=ALU.mult, op1=ALU.mult), reads=[R_acc[0], R_acc[1], R_zs_], writes=[R_oab])

        def tail_pe():
            for half in range(2):
                for c in range(2):
                    cc = half * 2 + c
                    P.op("pe", lambda e, c=c, cc=cc: e.transpose(out=psb[7][:, 520 + c * 128:520 + (c + 1) * 128], in_=oab[:, cc * 128:(cc + 1) * 128], identity=k.ident), reads=[R_oab, RI], writes=[Rps[7]])
                P.op("dve", lambda e, half=half: e.tensor_copy(out=k.oaT[:, 2 * half:2 * half + 2, i * 128:(i + 1) * 128], in_=psb[7][:, 520:776].rearrange("p (c t) -> p c t", c=2)), reads=[Rps[7]], writes=[k.R_oaT])
        return (tail_dve, tail_pe)

    for _ in stage_a(0):
        pass
    tail_fn = None
    for i in range(NT):
        tail_fn = stage_b(i, stage_a(i + 1) if i + 1 < NT else None, tail_fn)
    tail_fn[0]()
    tail_fn[1]()


def alloc_xT(k):
    A = k.A
    k.xTs = [A.alloc([8, 128]), A.alloc([8, 128])]
    k.xTb = [A.alloc([8, 128], BF16), A.alloc([8, 128], BF16)]
    k.R_xTs = [Res(), Res()]
    k.R_xTb = [Res(), Res()]


def alloc_obT(k):
    k.obT = k.A.alloc([4, SEQ], BF16)
    if not hasattr(k, "R_obT"):
        k.R_obT = Res("obT")


def pass2(k):
    P, A, NT = k.P, k.A, k.NT
    k.xcast = "act"
    ps, psb, Rps = k.ps, k.psb, k.Rps
    RI = k.R_init
    alloc_obT(k)
    W2 = A.alloc([8, 2048], BF16)
    R_W2 = load_weight_groups(k, "W2g", W2, k.w2_d, 8, [(0, 512), (512, 1024), (1024, 1536), (1536, 2048)])
    b2bc = A.alloc([2048])
    P.dma("sp", lambda e: e.dma_start(out=b2bc, in_=k.b2_d.broadcast_to([128, 2048])), "init", writes=[RI])
    lbA = A.alloc([512])
    lbB = A.alloc([512])
    ghalf = A.alloc([512])
    P.dma("sp", lambda e: e.dma_start(out=lbA, in_=k.lbl_d[0:1, :].broadcast_to([128, 512])), "init", writes=[RI])
    P.dma("sp", lambda e: e.dma_start(out=lbB, in_=k.lbl_d[1:2, :].broadcast_to([128, 512])), "init", writes=[RI])
    P.dma("sp", lambda e: e.dma_start(out=ghalf, in_=k.hg_d.broadcast_to([128, 512])), "init", writes=[RI])
    P.op("dve", lambda e: e.tensor_tensor(out=lbA, in0=lbA, in1=lbB, op=ALU.subtract), reads=[RI], writes=[RI])
    P.op("act", lambda e: e.activation(out=lbA, in_=lbA, func=AF.Tanh, scale=0.5), reads=[RI], writes=[RI])
    P.op("dve", lambda e: e.tensor_scalar(out=lbB, in0=lbA, scalar1=0.25, scalar2=0.75, op0=ALU.mult, op1=ALU.add), reads=[RI], writes=[RI])
    P.op("dve", lambda e: e.tensor_scalar(out=lbA, in0=lbA, scalar1=-0.25, scalar2=0.25, op0=ALU.mult, op1=ALU.add), reads=[RI], writes=[RI])
    P.op("dve", lambda e: e.tensor_scalar(out=ghalf, in0=ghalf, scalar1=0.5, scalar2=None, op0=ALU.mult), reads=[RI], writes=[RI])
    St = A.alloc([4, 128])
    R_St = Res()
    Sb0 = [A.alloc([4, 128], BF16), A.alloc([4, 128], BF16)]
    R_Sb0 = [Res(), Res()]
    Sb1 = A.alloc([4, 128], BF16)
    R_Sb1 = Res()
    P.op("pool", lambda e: e.memset(St, 0.0), writes=[R_St])
    P.op("pool", lambda e: e.memset(Sb0[0], 0.0), writes=[R_Sb0[0]])
    alloc_xT(k)
    h2s = [A.alloc([2048]), A.alloc([2048])]
    R_h2s = [[Res() for _ in range(4)] for _ in range(2)]
    groups = [(0, 512), (512, 1024), (1024, 1536), (1536, 2048)]
    tqz, tff, tzz, logf, kk = (A.alloc([512]) for _ in range(5))
    R_tqz, R_tff, R_tzz, R_logf, R_kk = (Res() for _ in range(5))
    tzzs = [tzz, A.alloc([512])]
    R_tzzs = [R_tzz, Res()]
    eb, enb, erev = (A.alloc([512]) for _ in range(3))
    R_eb, R_enb, R_erev = (Res() for _ in range(3))
    qe_b, ke_b, kd_b, v_b, ob_b = (A.alloc([512], BF16) for _ in range(5))
    R_qe, R_ke, R_kd, R_v, R_ob = (Res() for _ in range(5))
    qeT, qeT0, qeT1, keT, attn_b = (A.alloc([4, 128], BF16) for _ in range(5))
    R_qeT, R_qeT0, R_qeT1, R_keT, R_attn = (Res() for _ in range(5))
    P.op("pool", lambda e: e.memset(qeT0, 0.0), writes=[R_qeT0])
    kd1_b = A.alloc([512], BF16)
    P.op("pool", lambda e: e.memset(kd_b, 0.0), writes=[R_kd])
    P.op("pool", lambda e: e.memset(kd1_b, 0.0), writes=[R_kd])
    P.op("pool", lambda e: e.memset(qeT1, 0.0), writes=[R_qeT1])
    dl = A.alloc([8])
    R_dl = Res()
    ssq, lnv, rstd = (A.alloc([4]) for _ in range(3))
    R_ssq = Res()
    junk = A.alloc([128])
    R_junk = Res()

    def s1(i):
        slot = i % 2
        load_xT(k, i, slot)
        project(k, slot, W2, b2bc, groups, h2s[slot], R_h2s[slot], R_W=R_W2)

    def tail_a(i):
        tzz, R_tzz = tzzs[i % 2], R_tzzs[i % 2]
        for hh in range(4):
            P.op("act", lambda e, hh=hh: e.activation(out=junk, in_=ps[7][:, hh * 128:(hh + 1) * 128], func=AF.Square, accum_out=ssq[:, hh:hh + 1]), reads=[Rps[7]], writes=[R_junk, R_ssq])
        P.op("act", lambda e: e.activation(out=lnv, in_=ssq, func=AF.Ln, scale=1.0 / 128.0, bias=1e-5), reads=[R_ssq], writes=[R_ssq])
        P.op("act", lambda e: e.activation(out=rstd, in_=lnv, func=AF.Exp, scale=-0.5), reads=[R_ssq], writes=[R_ssq])
        for hh in range(4):
            P.op("dve", lambda e, hh=hh: e.scalar_tensor_tensor(out=ob_b[:, hh * 128:(hh + 1) * 128], in0=ps[7][:, hh * 128:(hh + 1) * 128], scalar=rstd[:, hh:hh + 1], in1=tzz[:, hh * 128:(hh + 1) * 128], op0=ALU.mult, op1=ALU.mult),
                 reads=[Rps[7], R_ssq, R_tzz], writes=[R_ob])

    def tail_b(i):
        for c in range(4):
            P.op("pe", lambda e, c=c: e.transpose(out=psb[4][:, c * 128:(c + 1) * 128], in_=ob_b[:, c * 128:(c + 1) * 128], identity=k.ident), reads=[R_ob, RI], writes=[Rps[4]])
        P.op("act", lambda e: e.copy(out=k.obT[:, :, i * 128:(i + 1) * 128], in_=psb[4][:, 0:512].rearrange("p (c t) -> p c t", c=4)), reads=[Rps[4]], writes=[k.R_obT])

    def tile(i):
        slot = i % 2
        tzz, R_tzz = tzzs[i % 2], R_tzzs[i % 2]
        h2, R_h2 = h2s[slot], R_h2s[slot]
        hq, hf, hi, hz = (h2[:, a:a + 512] for a in (0, 512, 1024, 1536))
        if getattr(k, 'stop', 99) <= 1:
            return
        P.op("act", lambda e: e.activation(out=tqz, in_=hq, func=AF.Tanh, scale=0.5), reads=[R_h2[0]], writes=[R_tqz])
        P.op("act", lambda e: e.activation(out=tff, in_=hf, func=AF.Tanh, scale=0.5), reads=[R_h2[1]], writes=[R_tff])
        P.op("act", lambda e: e.activation(out=tzz, in_=hz, func=AF.Tanh, scale=0.5), reads=[R_h2[3]], writes=[R_tzz])
        P.op("act", lambda e: e.copy(out=v_b, in_=hi), reads=[R_h2[2]], writes=[R_v])
        P.op("dve", lambda e: e.scalar_tensor_tensor(out=tqz, in0=tqz, scalar=1.0, in1=hq, op0=ALU.add, op1=ALU.mult), reads=[R_tqz, R_h2[0]], writes=[R_tqz])
        P.op("pool", lambda e: e.tensor_tensor(out=tff, in0=tff, in1=lbA, op=ALU.mult), reads=[R_tff, RI], writes=[R_tff])
        P.op("pool", lambda e: e.tensor_tensor(out=tff, in0=tff, in1=lbB, op=ALU.add), reads=[R_tff, RI], writes=[R_tff])
        P.op("dve", lambda e: e.scalar_tensor_tensor(out=tzz, in0=tzz, scalar=1.0, in1=hz, op0=ALU.add, op1=ALU.mult), reads=[R_tzz, R_h2[3]], writes=[R_tzz])
        P.op("pool", lambda e: e.tensor_tensor(out=tzz, in0=tzz, in1=ghalf, op=ALU.mult), reads=[R_tzz, RI], writes=[R_tzz])
        P.op("act", lambda e: e.activation(out=logf, in_=tff, func=AF.Ln), reads=[R_tff], writes=[R_logf])
        P.op("pool", lambda e: e.tensor_scalar(out=kk, in0=tff, scalar1=-1.0, scalar2=1.0, op0=ALU.mult, op1=ALU.add), reads=[R_tff], writes=[R_kk])
        if getattr(k, 'stop', 99) <= 2:
            return
        P.op("pe", lambda e: e.matmul(ps[2][:, :], lhsT=k.mintra_f, rhs=logf, start=True, stop=True), reads=[R_logf, RI], writes=[Rps[2]])
        P.op("pe", lambda e: e.matmul(ps[3][:, :], lhsT=k.mrev_f, rhs=logf, start=True, stop=True), reads=[R_logf, RI], writes=[Rps[3]])
        for hh in range(4):
            P.op("pe", lambda e, hh=hh: e.matmul(ps[4][:, 2 * hh:2 * hh + 2], lhsT=logf[:, hh * 128:(hh + 1) * 128], rhs=k.cind_f, start=True, stop=True), reads=[R_logf, RI], writes=[Rps[4]])
        if getattr(k, 'stop', 99) <= 3:
            return
        P.op("act", lambda e: e.activation(out=eb, in_=ps[2][:, :], func=AF.Exp), reads=[Rps[2]], writes=[R_eb])
        P.op("act", lambda e: e.activation(out=enb, in_=ps[2][:, :], func=AF.Exp, scale=-1.0), reads=[Rps[2]], writes=[R_enb])
        P.op("act", lambda e: e.activation(out=erev, in_=ps[3][:, :], func=AF.Exp), reads=[Rps[3]], writes=[R_erev])
        P.op("act", lambda e: e.activation(out=dl, in_=ps[4][:, 0:8], func=AF.Exp), reads=[Rps[4]], writes=[R_dl])
        if i > 0:
            tail_a(i - 1)
        P.op("dve", lambda e: e.scalar_tensor_tensor(out=qe_b, in0=tqz, scalar=0.5, in1=eb, op0=ALU.mult, op1=ALU.mult), reads=[R_tqz, R_eb], writes=[R_qe])
        P.op("pool", lambda e: e.tensor_tensor(out=ke_b, in0=kk, in1=enb, op=ALU.mult), reads=[R_kk, R_enb], writes=[R_ke])
        P.op("pool", lambda e: e.tensor_tensor(out=kd_b[0:64, :], in0=kk[0:64, :], in1=erev[0:64, :], op=ALU.mult), reads=[R_kk, R_erev], writes=[R_kd])
        P.op("pool", lambda e: e.tensor_tensor(out=kd1_b[64:128, :], in0=kk[64:128, :], in1=erev[64:128, :], op=ALU.mult), reads=[R_kk, R_erev], writes=[R_kd])
        if getattr(k, 'stop', 99) <= 4:
            return
        for hh in range(4):
            P.op("pe", lambda e, hh=hh: e.transpose(out=psb[5][:, hh * 128:(hh + 1) * 128], in_=qe_b[:, hh * 128:(hh + 1) * 128], identity=k.ident), reads=[R_qe, RI], writes=[Rps[5]])
        for hh in range(4):
            P.op("pe", lambda e, hh=hh: e.transpose(out=psb[5][:, (4 + hh) * 128:(5 + hh) * 128], in_=ke_b[:, hh * 128:(hh + 1) * 128], identity=k.ident), reads=[R_ke, RI], writes=[Rps[5]])
        if k.stop <= 4.2:
            return
        q3 = psb[5][:, 0:512].rearrange("p (h t) -> p h t", h=4)
        P.op("dve", lambda e: e.tensor_copy(out=qeT, in_=q3), reads=[Rps[5]], writes=[R_qeT])
        if k.stop <= 4.4:
            return
        P.op("dve", lambda e: e.tensor_copy(out=qeT0[:, :, 0:64], in_=q3[:, :, 0:64]), reads=[Rps[5]], writes=[R_qeT0])
        P.op("dve", lambda e: e.tensor_copy(out=qeT1[:, :, 64:128], in_=q3[:, :, 64:128]), reads=[Rps[5]], writes=[R_qeT1])
        if k.stop <= 4.6:
            return
        P.op("dve", lambda e: e.tensor_copy(out=keT, in_=psb[5][:, 512:1024].rearrange("p (h t) -> p h t", h=4)), reads=[Rps[5]], writes=[R_keT])
        if getattr(k, 'stop', 99) <= 5:
            return
        for hh in range(4):
            P.op("pe", lambda e, hh=hh: e.matmul(ps[6][:, hh * 128:(hh + 1) * 128], lhsT=keT[:, hh, :], rhs=qeT[:, hh, :], start=True, stop=True), reads=[R_keT, R_qeT], writes=[Rps[6]])
        if i > 0:
            tail_b(i - 1)
        P.op("dve", lambda e: e.tensor_tensor(out=attn_b, in0=ps[6][:, :].rearrange("p (h t) -> p h t", h=4), in1=bc(k.mintra_b[:, None, :], [128, 4, 128]), op=ALU.mult), reads=[Rps[6], RI], writes=[R_attn])
        if getattr(k, 'stop', 99) <= 6:
            return
        def ub(hh, cc):
            b = 2 if hh < 2 else 3
            o = ((hh % 2) * 2 + cc) * 128
            return b, ps[b][:, o:o + 128]
        for hh in range(4):
            for cc in range(2):
                b, o = ub(hh, cc)
                P.op("pe", lambda e, hh=hh, cc=cc, o=o: e.matmul(o, lhsT=(kd_b, kd1_b)[cc][:, hh * 128:(hh + 1) * 128], rhs=v_b[:, hh * 128:(hh + 1) * 128], start=True, stop=True),
                     reads=[R_kd, R_v], writes=[Rps[b]])
        if getattr(k, 'stop', 99) <= 7:
            return
        nxt = (i + 1) % 2
        for cc in range(2):
            for hh in range(4):
                b, o = ub(hh, cc)
                P.op("dve", lambda e, hh=hh, cc=cc, o=o: e.scalar_tensor_tensor(out=St[:, hh, :], in0=St[:, hh, :], scalar=dl[:, 2 * hh + cc:2 * hh + cc + 1], in1=o, op0=ALU.mult, op1=ALU.add),
                     reads=[R_St, R_dl, Rps[b]], writes=[R_St])
            if cc == 0:
                P.op("act", lambda e: e.copy(out=Sb1, in_=St), reads=[R_St], writes=[R_Sb1])
            else:
                P.op("act", lambda e: e.copy(out=Sb0[nxt], in_=St), reads=[R_St], writes=[R_Sb0[nxt]])
        if getattr(k, 'stop', 99) <= 8:
            return
        cur = i % 2
        for hh in range(4):
            o = ps[7][:, hh * 128:(hh + 1) * 128]
            P.op("pe", lambda e, hh=hh, o=o: e.matmul(o, lhsT=attn_b[:, hh, :], rhs=v_b[:, hh * 128:(hh + 1) * 128], start=True, stop=False), reads=[R_attn, R_v], writes=[Rps[7]])
            P.op("pe", lambda e, hh=hh, o=o: e.matmul(o, lhsT=qeT0[:, hh, :], rhs=Sb0[cur][:, hh, :], start=False, stop=False), reads=[R_qeT0, R_Sb0[cur]], writes=[Rps[7]])
            P.op("pe", lambda e, hh=hh, o=o: e.matmul(o, lhsT=qeT1[:, hh, :], rhs=Sb1[:, hh, :], start=False, stop=True), reads=[R_qeT1, R_Sb1], writes=[Rps[7]])

    s1(0)
    for i in range(NT):
        if i + 1 < NT:
            s1(i + 1)
        tile(i)
    tail_a(NT - 1)
    tail_b(NT - 1)
    print("pass2 arena words", A.off)


def pass3(k):
    P, A, NT = k.P, k.A, k.NT
    k.xcast = "act"
    ps, psb, Rps = k.ps, k.psb, k.Rps
    RI = k.R_init
    alloc_obT(k)
    W3 = A.alloc([8, 2048], BF16)
    R_W3 = load_weight_groups(k, "W3g", W3, k.w3_d, 8, [(0, 512), (512, 1024), (1024, 1536), (1536, 2048)])
    b3bc = A.alloc([2048])
    P.dma("sp", lambda e: e.dma_start(out=b3bc, in_=k.b3_d.broadcast_to([128, 2048])), "init", writes=[RI])
    wba = A.alloc([4, 1024], BF16)
    wbb = A.alloc([4, 1024], BF16)
    wo = A.alloc([8, 1024], BF16)
    R_wba = load_weight_groups(k, "wbag", wba, k.wba_d, 4, [(0, 512), (512, 1024)])
    R_wbb = load_weight_groups(k, "wbbg", wbb, k.wbb_d, 4, [(0, 512), (512, 1024)])
    R_wo = load_weight_groups(k, "wog", wo, k.wo_d, 8, [(0, 512), (512, 1024)])
    lng = A.alloc([1024])
    lnb = A.alloc([1024])
    P.dma("sp", lambda e: e.dma_start(out=lng, in_=k.lng_d.broadcast_to([128, 1024])), "init", writes=[RI])
    P.dma("sp", lambda e: e.dma_start(out=lnb, in_=k.lnb_d.broadcast_to([128, 1024])), "init", writes=[RI])
    alloc_xT(k)
    xt = [A.alloc([1024]), A.alloc([1024]), A.alloc([1024])]
    R_xt = [Res(), Res(), Res()]
    hg = A.alloc([1024])
    R_hg = [Res(), Res()]
    t1 = k.stage[0][:, 0:512]
    t2 = k.stage[0][:, 512:1024]
    R_t1 = R_t2 = k.R_stage[0]
    y2 = [A.alloc([1024], BF16), A.alloc([1024], BF16)]
    R_y2 = [Res(), Res()]
    yT = A.alloc([8, 128], BF16)
    R_yT = Res()
    r = k.stage[1]
    R_r = k.R_stage[1]
    ot = [A.alloc([1024]), A.alloc([1024])]
    R_ot = [Res(), Res()]
    st6 = A.alloc([12])
    mv = A.alloc([2])
    lnv = A.alloc([1])
    rstd = A.alloc([1])
    nb = A.alloc([1])
    R_st = Res()

    def s1_load(i):
        load_xT(k, i, i % 2, cast=False)
        P.dma("sp", lambda e: e.dma_start(out=xt[i % 3], in_=k.x_d[i * 128:(i + 1) * 128, :]), f"xt{i % 3}", writes=[R_xt[i % 3]])

    def s1(i):
        slot = i % 2
        load_xT(k, i, slot, dma=False)
        ts = slice(i * 128, (i + 1) * 128)
        for half in range(2):
            hs = slice(half * 512, (half + 1) * 512)
            project(k, slot, W3, b3bc, [(half * 512, half * 512 + 512), (1024 + half * 512, 1536 + half * 512)], _HG(hg, half), R_hg, R_W=R_W3)
            P.op("act", lambda e: e.activation(out=hg, in_=hg, func=AF.Tanh, scale=0.5), reads=R_hg, writes=R_hg)
            for c in range(4):
                P.op("pe", lambda e, c=c, hs=hs: e.matmul(ps[2][:, :], lhsT=k.oaT[:, c, ts], rhs=wba[:, c, hs], start=(c == 0), stop=(c == 3)), reads=[k.R_oaT, R_wba[half]], writes=[Rps[2]])
            for c in range(4):
                P.op("pe", lambda e, c=c, hs=hs: e.matmul(ps[3][:, :], lhsT=k.obT[:, c, ts], rhs=wbb[:, c, hs], start=(c == 0), stop=(c == 3)), reads=[k.R_obT, R_wbb[half]], writes=[Rps[3]])
            P.op("dve", lambda e: e.scalar_tensor_tensor(out=t1, in0=hg[:, 0:512], scalar=1.0, in1=ps[2][:, :], op0=ALU.add, op1=ALU.mult), reads=[R_hg[0], Rps[2]], writes=[R_t1])
            P.op("dve", lambda e: e.scalar_tensor_tensor(out=t2, in0=hg[:, 512:1024], scalar=1.0, in1=ps[3][:, :], op0=ALU.add, op1=ALU.mult), reads=[R_hg[1], Rps[3]], writes=[R_t2])
            P.op("pool", lambda e, hs=hs: e.tensor_tensor(out=y2[slot][:, hs], in0=t1, in1=t2, op=ALU.add), reads=[R_t1, R_t2], writes=[R_y2[slot]])

    def s2(i):
        slot = i % 2
        for c in range(8):
            P.op("pe", lambda e, c=c: e.transpose(out=psb[4][:, c * 128:(c + 1) * 128], in_=y2[slot][:, c * 128:(c + 1) * 128], identity=k.ident), reads=[R_y2[slot], RI], writes=[Rps[4]])
        P.op("act", lambda e: e.copy(out=yT[:].rearrange("p a b -> p (a b)"), in_=psb[4][:, 0:1024]), reads=[Rps[4]], writes=[R_yT])
        for half in range(2):
            hs = slice(half * 512, (half + 1) * 512)
            b = 5 + half
            for c in range(8):
                P.op("pe", lambda e, c=c, hs=hs, b=b: e.matmul(ps[b][:, :], lhsT=yT[:, c, :], rhs=wo[:, c, hs], start=(c == 0), stop=(c == 7)), reads=[R_yT, R_wo[half]], writes=[Rps[b]])
            P.op("dve", lambda e, hs=hs, b=b: e.scalar_tensor_tensor(out=r[:, hs], in0=ps[b][:, :], scalar=0.5 / ALPHA, in1=xt[i % 3][:, hs], op0=ALU.mult, op1=ALU.add), reads=[Rps[b], R_xt[i % 3]], writes=[R_r])
            P.op("dve", lambda e, hs=hs, half=half: e.bn_stats(out=st6[:, half * 6:half * 6 + 6], in_=r[:, hs]), reads=[R_r], writes=[R_st])
        P.op("dve", lambda e: e.bn_aggr(out=mv, in_=st6), reads=[R_st], writes=[R_st])
        P.op("act", lambda e: e.activation(out=lnv, in_=mv[:, 1:2], func=AF.Ln, bias=1e-5 / (ALPHA * ALPHA)), reads=[R_st], writes=[R_st])
        P.op("act", lambda e: e.activation(out=rstd, in_=lnv, func=AF.Exp, scale=-0.5), reads=[R_st], writes=[R_st])
        P.op("dve", lambda e: e.scalar_tensor_tensor(out=nb, in0=mv[:, 0:1], scalar=-1.0, in1=rstd, op0=ALU.mult, op1=ALU.mult), reads=[R_st], writes=[R_st])
        o = ot[slot]
        P.op("dve", lambda e: e.tensor_scalar(out=o, in0=r, scalar1=rstd[:, 0:1], scalar2=nb[:, 0:1], op0=ALU.mult, op1=ALU.add), reads=[R_r, R_st], writes=[R_ot[slot]])
        P.op("pool", lambda e: e.tensor_tensor(out=o, in0=o, in1=lng, op=ALU.mult), reads=[R_ot[slot], RI], writes=[R_ot[slot]])
        P.op("pool", lambda e: e.tensor_tensor(out=o, in0=o, in1=lnb, op=ALU.add), reads=[R_ot[slot], RI], writes=[R_ot[slot]])
        P.dma("sp", lambda e: e.dma_start(out=k.out_d[i * 128:(i + 1) * 128, :], in_=o), f"out{slot}", reads=[R_ot[slot]])

    s1_load(0)
    if NT > 1:
        s1_load(1)
    s1(0)
    for i in range(NT):
        if i + 2 < NT:
            s1_load(i + 2)
        if i + 1 < NT:
            s1(i + 1)
        s2(i)
    print("pass3 arena words", A.off)


class _HG:
    def __init__(self, hg, half):
        self.hg = hg
        self.half = half

    def __getitem__(self, key):
        _, cs = key
        c0 = cs.start
        o = 0 if c0 < 1024 else 512
        return self.hg[:, o:o + (cs.stop - cs.start)]


def _consts():
    p = np.arange(128)
    cst = np.zeros((128, 642), np.float32)
    cst[:, 0:128] = np.eye(128)
    cst[:, 128:256] = (p[:, None] <= p[None, :])
    cst[:, 256:384] = (p[:, None] > p[None, :])
    same = (p[:, None] // 64) == (p[None, :] // 64)
    cst[:, 384:512] = same & (p[:, None] <= p[None, :])
    cst[:, 512:640] = same & (p[:, None] > p[None, :])
    cst[:, 640] = p < 64
    cst[:, 641] = p >= 64
    maskc = np.zeros((128, 33, 128), np.float32)
    q = np.arange(128)
    for idx in range(33):
        if idx <= 16:
            i, nt = idx, 0
        else:
            i, nt = idx - 17 + 16, 1
        n = nt * 128 + p
        maskc[:, idx, :] = (16 * n[:, None] + 31) <= (128 * i + q[None, :])
    ovl = np.zeros((128, 2, 65), np.float32)
    n = np.arange(256)
    cs = n * 16
    js = np.arange(64) * 64
    ov = ((cs[:, None] < js[None, :] + 64) & (cs[:, None] + 32 > js[None, :])).astype(np.float32)
    ov[255] = 0.0
    ovl[:, :, 0:64] = ov.reshape(2, 128, 64).transpose(1, 0, 2)
    ovl[:, :, 64] = 1.0
    ovl[127, 1, 64] = 0.0
    inv = np.float32(10000.0) ** (-(np.arange(0, 64, 2, dtype=np.float32)) / np.float32(64))
    pos = np.arange(SEQ, dtype=np.float32)
    ang = (pos[:, None] * inv[None, :]).astype(np.float32)
    cos = np.cos(ang).astype(np.float32).reshape(32, 128, 32).transpose(1, 0, 2)
    sin = np.sin(ang).astype(np.float32).reshape(32, 128, 32).transpose(1, 0, 2)
    et = (np.arange(SEQ)[None, :] // 64 == np.arange(64)[:, None]).astype(np.float32)
    return dict(cst=cst, maskc=np.ascontiguousarray(maskc.reshape(128, -1)), ovl=np.ascontiguousarray(ovl.reshape(128, -1)),
                cos=np.ascontiguousarray(cos.reshape(128, -1)), sin=np.ascontiguousarray(sin.reshape(128, -1)), et=et)


def prep_shared(w_in, b_in, pe_cmp_k, w_cmp_k1, w_cmp_k2, pe_cmp_v, w_cmp_v1, w_cmp_v2,
                hgrn_lb_logits, hgrn_norm_g, w_branch_a, w_branch_b, w_out, ln_g, ln_b):
    f = lambda a: np.ascontiguousarray(np.asarray(a, dtype=np.float32))
    w, b = np.asarray(w_in[0]), np.asarray(b_in[0])
    perm1 = np.concatenate([np.arange(0, 512), np.arange(768, 896), np.arange(1024, 1152), np.arange(512, 640), np.arange(640, 768),
                            np.arange(896, 1024), np.arange(1152, 1280), np.arange(1280, 1304), np.arange(1304, 1816)])
    d = dict(
        w1=f(w[:, perm1]), b1=f(b[perm1][None, :]),
        w2=f(w[:, 1816:3864]), b2=f(b[None, 1816:3864]),
        w3=f(w[:, 3864:5912]), b3=f(b[None, 3864:5912]),
        wk1=f(np.asarray(w_cmp_k1[0]).reshape(32, 64, 128).transpose(1, 0, 2).reshape(64, 4096)),
        wv1=f(np.asarray(w_cmp_v1[0]).reshape(32, 64, 128).transpose(1, 0, 2).reshape(64, 4096)),
        wk2=f(w_cmp_k2[0]), wv2=f(w_cmp_v2[0]),
        pek=f(np.asarray(pe_cmp_k[0]).T), pev=f(np.asarray(pe_cmp_v[0]).T),
        lbl=f(hgrn_lb_logits), hg=f(hgrn_norm_g),
        wba=f(w_branch_a[0]), wbb=f(w_branch_b[0]), wo=f(w_out[0]), lng=f(ln_g), lnb=f(ln_b),
    )
    d.update(_consts())
    return d


def kernel(x, **params):
    x = np.asarray(x, dtype=np.float32)
    shared = prep_shared(**params)
    nc = build()
    in_maps = []
    for b in range(8):
        m = dict(shared)
        m["x"] = np.ascontiguousarray(x[b])
        m["xT"] = np.ascontiguousarray(x[b].T.reshape(8, 128, SEQ))
        in_maps.append(m)
    res = run_bass_kernel_spmd(nc, in_maps, core_ids=list(range(8)))
    return np.stack([np.asarray(r["out"]) for r in res.results], axis=0)
```

```python
import numpy as np
from contextlib import ExitStack
import concourse.bass as bass
import concourse.mybir as mybir
from concourse.bass_utils import run_bass_kernel_spmd

F32 = mybir.dt.float32
BF16 = mybir.dt.bfloat16
AF = mybir.ActivationFunctionType
ALU = mybir.AluOpType

NTILES = 32
SEQ = 4096
NEG = -30000.0
TINY = 1e-30
ALPHA = 2.0 ** 0.25
KEEPWARM = False
FUSE_WAIT = True


class Res:
    __slots__ = ("name", "w", "r")

    def __init__(self, name=""):
        self.name = name
        self.w = None
        self.r = {}


class Prog:
    ENG = ("pe", "act", "dve", "pool", "sp")
    CAP = 30000

    def __init__(self, nc, same_engine_sync=True):
        self.nc = nc
        self.ops = {e: [] for e in self.ENG}
        self.waited = {e: {} for e in self.ENG}
        self.dma_count = {}
        self.same_engine_sync = same_engine_sync

    def _deps(self, eng, reads, writes):
        toks = {}

        def add(ch, idx):
            if toks.get(ch, -1) < idx:
                toks[ch] = idx

        for r in reads:
            if r.w is not None:
                add(*r.w)
        for w in writes:
            if w.w is not None:
                add(*w.w)
            for ch, idx in w.r.items():
                add(ch, idx)
        waits = []
        for ch, idx in toks.items():
            if ch == eng and (eng == "pe" or not self.same_engine_sync):
                continue
            if self.waited[eng].get(ch, -1) >= idx:
                continue
            self.waited[eng][ch] = idx
            waits.append((ch, idx))
        return waits

    def _finish(self, tok, reads, writes):
        ch, idx = tok
        for r in reads:
            if r.r.get(ch, -1) < idx:
                r.r[ch] = idx
        for w in writes:
            w.w = tok
            w.r = {}

    def op(self, eng, fn, reads=(), writes=()):
        waits = self._deps(eng, reads, writes)
        idx = len(self.ops[eng])
        self.ops[eng].append(dict(fn=fn, waits=waits, ms=False, dma=None))
        self._finish((eng, idx), reads, writes)

    def dma(self, eng, fn, chan, reads=(), writes=()):
        waits = self._deps(eng, reads, writes)
        k = self.dma_count.get(chan, 0)
        self.dma_count[chan] = k + 1
        self.ops[eng].append(dict(fn=fn, waits=waits, ms=False, dma=chan))
        self._finish((("dma", chan), k), reads, writes)

    def _all_waits(self, eng, engines=True):
        waits = []
        if engines:
            for ch in self.ENG:
                if ch == eng:
                    continue
                last = -1
                for i in range(len(self.ops[ch]) - 1, -1, -1):
                    o = self.ops[ch][i]
                    if o["fn"] is not None and o["dma"] is None:
                        last = i
                        break
                if last >= 0 and self.waited[eng].get(ch, -1) < last:
                    self.waited[eng][ch] = last
                    waits.append((ch, last))
        for chan, k in self.dma_count.items():
            ch = ("dma", chan)
            if self.waited[eng].get(ch, -1) < k - 1:
                self.waited[eng][ch] = k - 1
                waits.append((ch, k - 1))
        return waits

    def barrier(self):
        allw = {e: self._all_waits(e) for e in self.ENG}
        for e in self.ENG:
            self.ops[e].append(dict(fn=None, waits=allw[e], ms=False, dma=None))

    def wait_all_dma(self, eng):
        self.ops[eng].append(dict(fn=None, waits=self._all_waits(eng, engines=False), ms=False, dma=None))

    def emit(self, stack):
        nc = self.nc
        for e in self.ENG:
            for o in self.ops[e]:
                for ch, idx in o["waits"]:
                    if isinstance(ch, str):
                        self.ops[ch][idx]["ms"] = True
        msnum = {}
        nsem = {}
        for e in self.ENG:
            c = 0
            for i, o in enumerate(self.ops[e]):
                if o["ms"]:
                    c += 1
                    msnum[(e, i)] = c
            nsem[e] = max(1, (c + self.CAP - 1) // self.CAP)
        esems = {e: [stack.enter_context(nc.semaphore(f"s_{e}_{j}")) for j in range(nsem[e])] for e in self.ENG}
        dsems = {ch: stack.enter_context(nc.semaphore(f"d_{ch}")) for ch in self.dma_count}
        CAP = self.CAP

        def semval(ch, idx):
            if isinstance(ch, str):
                m = msnum[(ch, idx)]
                return esems[ch][(m - 1) // CAP], (m - 1) % CAP + 1
            return dsems[ch[1]], 16 * (idx + 1)

        block = stack.enter_context(nc.Block())

        def run(e):
            def body(eng):
                for i, o in enumerate(self.ops[e]):
                    waits = list(o["waits"])
                    fused = None
                    if FUSE_WAIT and waits and o["fn"] is not None and o["dma"] is None:
                        fused = waits.pop()
                    for ch, idx in waits:
                        s, v = semval(ch, idx)
                        eng.wait_ge(s, v)
                    if o["fn"] is None:
                        continue
                    ins = o["fn"](eng)
                    if fused is not None:
                        s, v = semval(*fused)
                        ins._wait_ge(s, v)
                    if o["dma"] is not None:
                        ins.then_inc(dsems[o["dma"]], 16)
                    elif o["ms"]:
                        m = msnum[(e, i)]
                        ins.then_inc(esems[e][(m - 1) // CAP], 1)
            return body

        block.tensor(run("pe"))
        block.scalar(run("act"))
        block.vector(run("dve"))
        block.gpsimd(run("pool"))
        block.sync(run("sp"))


class Arena:
    def __init__(self, handle, nwords):
        self.h = handle
        self.n = nwords
        self.off = 0

    def alloc(self, shape, dtype=F32, parts=128):
        nel = int(np.prod(shape))
        nw = (nel * (2 if dtype == BF16 else 4) + 3) // 4
        nw = (nw + 1) // 2 * 2
        assert self.off + nw <= self.n, f"SBUF arena overflow: {self.off}+{nw} > {self.n}"
        a = self.h[0:parts, self.off:self.off + nw]
        self.off += nw
        if dtype == BF16:
            a = a.bitcast(BF16)
        a = a[:, 0:nel]
        if len(shape) == 2:
            a = a.rearrange("p (a b) -> p a b", a=shape[0], b=shape[1])
        elif len(shape) == 3:
            a = a.rearrange("p (a b c) -> p a b c", a=shape[0], b=shape[1], c=shape[2])
        elif len(shape) != 1:
            raise ValueError(shape)
        return a


def bc(ap, shape):
    return ap.to_broadcast(list(shape))


class K:
    pass


def build(nt_tiles=NTILES, dbg=False, passes=(1, 2, 3), stop=99):
    nc = bass.Bass("TRN2", target_bir_lowering=False)
    k = K()
    k.stop = stop
    k.nc = nc
    k.NT = nt_tiles
    k.dbg = dbg
    S = SEQ

    def din(name, shape):
        return nc.dram_tensor(name, shape, F32, kind="ExternalInput").ap()

    k.xT_d = din("xT", [8, 128, S])
    k.x_d = din("x", [S, 1024])
    k.w1_d = din("w1", [1024, 1816])
    k.b1_d = din("b1", [1, 1816])
    k.w2_d = din("w2", [1024, 2048])
    k.b2_d = din("b2", [1, 2048])
    k.w3_d = din("w3", [1024, 2048])
    k.b3_d = din("b3", [1, 2048])
    k.wk1_d = din("wk1", [64, 32 * 128])
    k.wv1_d = din("wv1", [64, 32 * 128])
    k.wk2_d = din("wk2", [128, 64])
    k.wv2_d = din("wv2", [128, 64])
    k.pek_d = din("pek", [64, 32])
    k.pev_d = din("pev", [64, 32])
    k.lbl_d = din("lbl", [2, 512])
    k.hg_d = din("hg", [1, 512])
    k.wba_d = din("wba", [512, 1024])
    k.wbb_d = din("wbb", [512, 1024])
    k.wo_d = din("wo", [1024, 1024])
    k.lng_d = din("lng", [1, 1024])
    k.lnb_d = din("lnb", [1, 1024])
    k.cst_d = din("cst", [128, 642])
    k.maskc_d = din("maskc", [128, 33 * 128])
    k.ovl_d = din("ovl", [128, 2 * 65])
    k.cos_d = din("cos", [128, 32 * 32])
    k.sin_d = din("sin", [128, 32 * 32])
    k.et_d = din("et", [64, S])
    k.out_d = nc.dram_tensor("out", [S, 1024], F32, kind="ExternalOutput").ap()
    if dbg:
        k.dbg_oa = nc.dram_tensor("dbg_oa", [128, 4 * S], BF16, kind="ExternalOutput").ap()
        k.dbg_ob = nc.dram_tensor("dbg_ob", [128, 4 * S], BF16, kind="ExternalOutput").ap()

    P = Prog(nc)
    k.P = P
    with ExitStack() as st:
        ARENA_WORDS = 50 * 1024
        arena_h = st.enter_context(nc.sbuf_tensor("arena", [128, ARENA_WORDS], F32))
        k.A = Arena(arena_h, ARENA_WORDS)
        k.ps = [st.enter_context(nc.psum_tensor(f"ps{i}", [128, 512], F32)) for i in range(8)]
        k.psb = [p.bitcast(BF16) for p in k.ps]
        k.Rps = [Res(f"ps{i}") for i in range(8)]
        setup_persistent(k)
        if 1 in passes:
            mark = k.A.off
            pass1(k)
            P.barrier()
            k.A.off = mark
        if dbg:
            P.dma("sp", lambda e: e.dma_start(out=k.dbg_oa, in_=k.oaT[:].rearrange("p a b -> p (a b)")), "dbg", reads=[k.R_oaT])
        if 2 in passes:
            mark = k.A.off
            pass2(k)
            P.barrier()
            if dbg:
                P.dma("sp", lambda e: e.dma_start(out=k.dbg_ob, in_=k.obT[:].rearrange("p a b -> p (a b)")), "dbg", reads=[k.R_obT])
                P.barrier()
            k.A.off = mark
        if 3 in passes:
            pass3(k)
        P.wait_all_dma("sp")
        P.emit(st)
    return nc


def stage_cast(k, dst, src_d, parts, ncols, eng_cycle=("act", "dve", "pool"), p0=0):
    P = k.P
    c0 = 0
    while c0 < ncols:
        n = min(1024, ncols - c0)
        s = k.stage_i % 2
        k.stage_i += 1
        stg = k.stage[s]
        Rs = k.R_stage[s]
        P.dma("sp", lambda e, stg=stg, c0=c0, n=n: e.dma_start(out=stg[p0:p0 + parts, 0:n], in_=src_d[:, c0:c0 + n]), f"stg{s}", writes=[Rs])
        eng = eng_cycle[k.stage_i % len(eng_cycle)]
        d = dst[:, c0:c0 + n]
        if eng == "act":
            P.op("act", lambda e, d=d, stg=stg, n=n: e.copy(out=d, in_=stg[p0:p0 + parts, 0:n]), reads=[Rs], writes=[k.R_init])
        else:
            P.op(eng, lambda e, d=d, stg=stg, n=n: e.tensor_copy(out=d, in_=stg[p0:p0 + parts, 0:n]), reads=[Rs], writes=[k.R_init])
        c0 += n


def load_weight_groups(k, name, W, w_d, nchunks, colgroups):
    P = k.P
    res = []
    for gi, (c0, c1) in enumerate(colgroups):
        r = Res(f"{name}{gi}")
        src = w_d[:, c0:c1].rearrange("(c p) n -> p c n", p=128)
        P.dma("pool", lambda e, c0=c0, c1=c1, src=src: e.dma_start(out=W[:, :, c0:c1], in_=src), f"{name}{gi}", writes=[r])
        res.append(r)
    return res


def load_cast(k, dst, src_d):
    k.P.dma("pool", lambda e: e.dma_start(out=dst, in_=src_d), "initc", writes=[k.R_initc])


def join_init(k):
    k.P.op("pool", lambda e: e.memset(k.joinbuf, 0.0), reads=[k.R_init, k.R_initc], writes=[k.R_init])


def setup_persistent(k):
    P, A = k.P, k.A
    k.R_init = Res("init")
    k.R_initc = Res("initc")
    k.joinbuf = A.alloc([2])
    k.stage = [A.alloc([1024]), A.alloc([1024])]
    k.R_stage = [Res("stg0"), Res("stg1")]
    k.stage_i = 0
    k.cstf = A.alloc([642])
    P.dma("sp", lambda e: e.dma_start(out=k.cstf, in_=k.cst_d), "init", writes=[k.R_init])
    k.cstb = A.alloc([512], BF16)
    P.op("dve", lambda e: e.tensor_copy(out=k.cstb, in_=k.cstf[:, 0:512]), reads=[k.R_init], writes=[k.R_init])
    k.ident = k.cstb[:, 0:128]
    k.tri = k.cstb[:, 128:256]
    k.win2 = k.cstb[:, 256:384]
    k.mintra_b = k.cstb[:, 384:512]
    k.mintra_f = k.cstf[:, 384:512]
    k.mrev_f = k.cstf[:, 512:640]
    k.cind_f = k.cstf[:, 640:642]
    k.oaT = A.alloc([4, SEQ], BF16)
    k.R_oaT = Res("oaT")


def load_xT(k, i, slot, dma=True, cast=True):
    P = k.P
    xs = k.xTs[slot]
    xb = k.xTb[slot]
    src = k.xT_d[:, :, i * 128:(i + 1) * 128].rearrange("c p t -> p c t")
    if dma:
        P.dma("sp", lambda e: e.dma_start(out=xs, in_=src), f"xT{slot}", writes=[k.R_xTs[slot]])
    if not cast:
        return
    if getattr(k, "xcast", "pool") == "act":
        P.op("act", lambda e: e.copy(out=xb, in_=xs), reads=[k.R_xTs[slot]], writes=[k.R_xTb[slot]])
    else:
        P.op("pool", lambda e: e.tensor_copy(out=xb, in_=xs), reads=[k.R_xTs[slot]], writes=[k.R_xTb[slot]])


def project(k, slot, W, bbc, groups, h, R_h, banks=(0, 1), R_W=None):
    P = k.P
    xb = k.xTb[slot]
    for gi, (c0, c1) in enumerate(groups):
        b = banks[gi % len(banks)]
        bank = k.ps[b]
        for c in range(8):
            P.op("pe", lambda e, bank=bank, c=c, c0=c0, c1=c1: e.matmul(bank[:, 0:c1 - c0], lhsT=xb[:, c, :], rhs=W[:, c, c0:c1], start=(c == 0), stop=(c == 7)),
                 reads=[k.R_xTb[slot], k.R_init if R_W is None else R_W[c0 // 512]], writes=[k.Rps[b]])
        P.op("dve", lambda e, bank=bank, c0=c0, c1=c1: e.tensor_tensor(out=h[:, c0:c1], in0=bank[:, 0:c1 - c0], in1=bbc[:, c0:c1], op=ALU.add),
             reads=[k.Rps[b], k.R_init], writes=[R_h[gi]])


def pass1(k):
    P, A, NT = k.P, k.A, k.NT
    ps, psb, Rps = k.ps, k.psb, k.Rps
    RI = k.R_init
    W1 = A.alloc([8, 1816], BF16)
    R_W1 = load_weight_groups(k, "W1g", W1, k.w1_d, 8, [(0, 512), (512, 1024), (1024, 1304), (1304, 1816)])
    b1bc = A.alloc([1816])
    P.dma("sp", lambda e: e.dma_start(out=b1bc, in_=k.b1_d.broadcast_to([128, 1816])), "init", writes=[RI])
    cos = A.alloc([32, 32])
    sin = A.alloc([32, 32])
    P.dma("sp", lambda e: e.dma_start(out=cos[:].rearrange("p a b -> p (a b)"), in_=k.cos_d), "init", writes=[RI])
    P.dma("sp", lambda e: e.dma_start(out=sin[:].rearrange("p a b -> p (a b)"), in_=k.sin_d), "init", writes=[RI])
    maskc = A.alloc([33, 128], BF16)
    load_cast(k, maskc[:].rearrange("p a b -> p (a b)"), k.maskc_d)
    ovl = A.alloc([2, 65], BF16)
    load_cast(k, ovl[:].rearrange("p a b -> p (a b)"), k.ovl_d)
    wk1 = A.alloc([32, 128], BF16)
    wv1 = A.alloc([32, 128], BF16)
    load_cast(k, wk1[0:64].rearrange("p a b -> p (a b)"), k.wk1_d)
    load_cast(k, wv1[0:64].rearrange("p a b -> p (a b)"), k.wv1_d)
    wk2 = A.alloc([64], BF16)
    wv2 = A.alloc([64], BF16)
    load_cast(k, wk2, k.wk2_d)
    load_cast(k, wv2, k.wv2_d)
    pek = A.alloc([32], BF16)
    pev = A.alloc([32], BF16)
    load_cast(k, pek[0:64], k.pek_d)
    load_cast(k, pev[0:64], k.pev_d)
    KaT = A.alloc([2, SEQ], BF16)
    for g in range(2):
        load_cast(k, KaT[64:128, g, :], k.et_d)
    R_KaT = [Res() for _ in range(NT)]
    KwT = A.alloc([6, 2, 128], BF16)
    R_KwT = [Res() for _ in range(6)]
    Vsel = A.alloc([32, 2, 65], BF16)
    R_Vsel = [Res() for _ in range(NT)]
    Vwin = A.alloc([6, 2, 65], BF16)
    R_Vwin = [Res() for _ in range(6)]
    kcT = A.alloc([2, 256], BF16)
    hsTv = A.alloc([2, 256], BF16)
    vca = A.alloc([2, 2, 65], BF16)
    R_kc, R_hsv, R_vca = Res("kc"), Res("hsv"), Res("vca")
    P.op("pool", lambda e: e.memset(kcT, 0.0), writes=[R_kc])
    P.op("pool", lambda e: e.memset(hsTv, 0.0), writes=[R_hsv])
    P.op("pool", lambda e: e.memset(vca, 0.0), writes=[R_vca])
    P.op("pool", lambda e: e.memset(vca[:, :, :, 64:65], 1.0), reads=[R_vca], writes=[R_vca])
    P.op("pool", lambda e: e.memset(Vsel[:, :, :, 64:65], 1.0), writes=R_Vsel)
    P.op("pool", lambda e: e.memset(Vwin[:, :, :, 64:65], 1.0), writes=R_Vwin)
    kvcT = A.alloc([4, 144], BF16)
    R_kvcT = Res("kvcT")
    P.op("pool", lambda e: e.memset(kvcT, 0.0), writes=[R_kvcT])
    ck = A.alloc([2])
    R_ck = Res("ck")

    def emit_ck():
        for (w1_, pe_, col) in ((wk1, pek, 0), (wv1, pev, 1)):
            for l in range(32):
                P.op("pe", lambda e, w1_=w1_, pe_=pe_, l=l, col=col: e.matmul(ps[3][:, col:col + 1], lhsT=w1_[0:64, l, :], rhs=pe_[0:64, l:l + 1], start=(l == 0), stop=(l == 31)),
                     reads=[RI], writes=[Rps[3]])
        P.op("dve", lambda e: e.tensor_copy(out=ck, in_=ps[3][:, 0:2]), reads=[Rps[3]], writes=[R_ck])

    join_init(k)
    k.xTs = [A.alloc([8, 128]), A.alloc([8, 128])]
    k.xTb = [A.alloc([8, 128], BF16), A.alloc([8, 128], BF16)]
    k.R_xTs = [Res(), Res()]
    k.R_xTb = [Res(), Res()]
    h = A.alloc([1816])
    R_h = [Res() for _ in range(4)]
    groups = [(0, 512), (512, 1024), (1024, 1304), (1304, 1816)]
    tq = [A.alloc([8, 32]) for _ in range(4)]
    R_tq = [Res() for _ in range(4)]
    Qaug = A.alloc([8, 128], BF16)
    R_Qaug = Res()
    qn = A.alloc([512], BF16)
    R_qn = Res()
    kr = A.alloc([4, 64], BF16)
    R_kr = Res()
    kvc = A.alloc([256], BF16)
    R_kvc = Res()
    QnT = A.alloc([8, 128], BF16)
    R_QnT = Res()
    QaT = A.alloc([8, 128], BF16)
    R_QaT = Res()
    gth = A.alloc([24])
    gs = A.alloc([8, 3])
    R_gs = Res()
    zs = A.alloc([512])
    R_zs = Res()
    u = A.alloc([32])
    th = A.alloc([32])
    hsf = A.alloc([32])
    hsk = A.alloc([16], BF16)
    R_u, R_th, R_hsf, R_hsk = Res(), Res(), Res(), Res()
    NPB = 6
    Pb = [A.alloc([512], BF16) for _ in range(NPB)]
    R_Pb = [Res() for _ in range(NPB)]
    pb_i = [0]
    rd = A.alloc([4])
    R_rd = Res()
    imp = A.alloc([2, 64])
    R_imp = Res()
    m8a = A.alloc([8])
    m8b = A.alloc([8])
    impt = A.alloc([64])
    R_m8 = Res()
    negm = A.alloc([2, 64])
    R_negm = Res()
    cfs = [A.alloc([4]), A.alloc([4])]
    R_cfs = [Res(), Res()]
    tmpc = A.alloc([4, 64])
    R_tmpc = Res()
    oab = A.alloc([512], BF16)
    R_oab = Res()
    QaTs = [QaT, A.alloc([8, 128], BF16)]
    R_QaTs = [Res(), Res()]
    accs = [A.alloc([8, 64]), A.alloc([8, 64])]
    R_accs = [[Res(), Res()], [Res(), Res()]]
    gss = [gs, A.alloc([8, 3])]
    R_gss = [Res(), Res()]
    zss = [zs, A.alloc([512])]
    R_zss = [Res(), Res()]
    sc_cnt = {}
    pv_cnt = {}
    pvs = A.alloc([260])
    R_pvs = Res()
    print("pass1 arena words", A.off)

    def add_branch(items, kts, lhs_of, rhs_q, qres, v_of, masks, sbanks, pvbanks, extra=None, done=None, ci=1):
        key = tuple(pvbanks)
        cnt = pv_cnt.get(key, 0)
        pv_cnt[key] = cnt + 1
        pvb = pvbanks[cnt % len(pvbanks)]
        pv = ps[pvb][:, 0:260].rearrange("p (h e) -> p h e", h=4)
        for idx, kt in enumerate(kts):
            lhsT, lres = lhs_of(kt)
            va, vres = v_of(kt)
            items.append(dict(kt=kt, idx=idx, n=len(kts), lhsT=lhsT, lres=lres, rhs=rhs_q, qres=qres, va=va, vres=vres, mask=masks(kt),
                              sbanks=sbanks, pvb=pvb, pv=pv, extra=extra, done=done, ci=ci))

    def flush(items, hook=None):
        def score(it):
            assert len(it["sbanks"]) >= 2
            key = tuple(it["sbanks"])
            cnt = sc_cnt.get(key, 0)
            sc_cnt[key] = cnt + 1
            sb = it["sbanks"][cnt % len(it["sbanks"])]
            it["sb"] = sb
            P.op("pe", lambda e: e.matmul(ps[sb][:, :], lhsT=it["lhsT"], rhs=it["rhs"], start=True, stop=True),
                 reads=it["lres"] + it["qres"], writes=[Rps[sb]])

        def rest(it):
            sb = it["sb"]
            pi = pb_i[0] % NPB
            pb_i[0] += 1
            pt = Pb[pi]
            P.op("act", lambda e: e.activation(out=pt, in_=ps[sb][:, :], func=AF.Exp, scale=0.125), reads=[Rps[sb]], writes=[R_Pb[pi]])
            m = it["mask"]
            if m is not None:
                pt3 = pt.rearrange("p (h q) -> p h q", h=4)
                P.op("dve", lambda e: e.tensor_tensor(out=pt3, in0=pt3, in1=bc(m[:, None, :], [128, 4, 128]), op=ALU.mult),
                     reads=[R_Pb[pi], RI], writes=[R_Pb[pi]])
            pv, pvb, idx, n, va = it["pv"], it["pvb"], it["idx"], it["n"], it["va"]
            for hh in range(4):
                P.op("pe", lambda e, hh=hh: e.matmul(pv[:, hh, :], lhsT=pt[:, hh * 128:(hh + 1) * 128], rhs=va, start=(idx == 0 and hh == 0), stop=(idx == n - 1), skip_group_check=True),
                     reads=[R_Pb[pi]] + it["vres"], writes=[Rps[pvb]])
            if KEEPWARM:
                P.op("pe", lambda e: e.matmul(ps[pvb][:, 260:512], lhsT=pt[:, 384:512], rhs=pt[:, 0:252], start=False, stop=False, skip_group_check=True),
                     reads=[R_Pb[pi]], writes=[Rps[pvb]])
            if it["extra"] is not None:
                it["extra"](it, pt, pi)
            if idx == n - 1 and it["done"] is not None:
                if it["ci"] == 1:
                    P.op("dve", lambda e: e.tensor_copy(out=pvs, in_=ps[pvb][:, 0:260]), reads=[Rps[pvb]], writes=[R_pvs])
                    it["done"](None, pvs.rearrange("p (h e) -> p h e", h=4))
                else:
                    it["done"](pvb, pv)

        if not items:
            return
        look = len(items[0]["sbanks"]) - 1
        for j in range(min(look, len(items))):
            score(items[j])
        for j, it in enumerate(items):
            if j + look < len(items):
                score(items[j + look])
            rest(it)
            if hook is not None:
                hook(j)

    def combine(par, g, br, pvb, pv, first, ci):
        cf, R_cf = cfs[ci], R_cfs[ci]
        acc, R_acc, gs_, R_gs_ = accs[par], R_accs[par], gss[par], R_gss[par]
        R_src = R_pvs if pvb is None else Rps[pvb]
        P.op("dve", lambda e: e.tensor_scalar_max(out=cf, in0=pv[:, :, 64], scalar1=TINY), reads=[R_src], writes=[R_cf])
        P.op("dve", lambda e: e.reciprocal(out=cf, in_=cf), reads=[R_cf], writes=[R_cf])
        P.op("dve", lambda e: e.tensor_tensor(out=cf, in0=cf, in1=gs_[:, 4 * g:4 * g + 4, br], op=ALU.mult), reads=[R_cf, R_gs_], writes=[R_cf])
        accg = acc[:, 4 * g:4 * g + 4, :]
        if first:
            P.op("dve", lambda e: e.tensor_tensor(out=accg, in0=pv[:, :, 0:64], in1=bc(cf[:, :, None], [128, 4, 64]), op=ALU.mult),
                 reads=[R_src, R_cf], writes=[R_acc[g]])
        else:
            P.op("dve", lambda e: e.tensor_tensor(out=tmpc, in0=pv[:, :, 0:64], in1=bc(cf[:, :, None], [128, 4, 64]), op=ALU.mult),
                 reads=[R_src, R_cf], writes=[R_tmpc])
            P.op("pool", lambda e: e.tensor_tensor(out=accg, in0=accg, in1=tmpc, op=ALU.add), reads=[R_tmpc, R_acc[g]], writes=[R_acc[g]])

    def rope(src, nh, cb, sb_, dst, R_src, R_dst, eng):
        t = [x[:, 0:nh, :] for x in tq]
        P.op(eng, lambda e: e.tensor_tensor(out=t[0], in0=src[:, :, 0, :], in1=cb, op=ALU.mult), reads=[R_src, RI], writes=[R_tq[0]])
        P.op(eng, lambda e: e.tensor_tensor(out=t[1], in0=src[:, :, 1, :], in1=sb_, op=ALU.mult), reads=[R_src, RI], writes=[R_tq[1]])
        P.op(eng, lambda e: e.tensor_tensor(out=t[2], in0=src[:, :, 1, :], in1=cb, op=ALU.mult), reads=[R_src, RI], writes=[R_tq[2]])
        P.op(eng, lambda e: e.tensor_tensor(out=t[3], in0=src[:, :, 0, :], in1=sb_, op=ALU.mult), reads=[R_src, RI], writes=[R_tq[3]])
        P.op(eng, lambda e: e.tensor_tensor(out=dst[:, :, 0:32], in0=t[0], in1=t[1], op=ALU.subtract), reads=[R_tq[0], R_tq[1]], writes=[R_dst])
        P.op(eng, lambda e: e.tensor_tensor(out=dst[:, :, 32:64], in0=t[2], in1=t[3], op=ALU.add), reads=[R_tq[2], R_tq[3]], writes=[R_dst])

    def stage_a(i):
        slot = i % 2
        par = i % 2
        QaT_, R_QaT_ = QaTs[par], R_QaTs[par]
        gs_, R_gs_, zs_, R_zs_ = gss[par], R_gss[par], zss[par], R_zss[par]
        if i == 0:
            load_xT(k, 0, 0, cast=False)
        if i + 1 < NT:
            load_xT(k, i + 1, (i + 1) % 2, cast=False)
        k.xcast = "act"
        load_xT(k, i, slot, dma=False)
        yield
        yield
        xb = k.xTb[slot]
        for gi, (c0, c1) in enumerate(groups):
            b = gi % 2
            for c in range(8):
                P.op("pe", lambda e, b=b, c=c, c0=c0, c1=c1: e.matmul(ps[b][:, 0:c1 - c0], lhsT=xb[:, c, :], rhs=W1[:, c, c0:c1], start=(c == 0), stop=(c == 7)),
                     reads=[k.R_xTb[slot], R_W1[gi]], writes=[Rps[b]])
                if c == 3:
                    yield
            P.op("dve", lambda e, b=b, c0=c0, c1=c1: e.tensor_tensor(out=h[:, c0:c1], in0=ps[b][:, 0:c1 - c0], in1=b1bc[:, c0:c1], op=ALU.add),
                 reads=[Rps[b], RI], writes=[R_h[gi]])
            yield
        cosb8 = bc(cos[:, i:i + 1, :], [128, 8, 32])
        sinb8 = bc(sin[:, i:i + 1, :], [128, 8, 32])
        cosb4 = bc(cos[:, i:i + 1, :], [128, 4, 32])
        sinb4 = bc(sin[:, i:i + 1, :], [128, 4, 32])
        hq = h[:, 0:512].rearrange("p (h t j) -> p h t j", h=8, t=2, j=32)
        hk = h[:, 512:768].rearrange("p (h t j) -> p h t j", h=4, t=2, j=32)
        P.op("act", lambda e: e.copy(out=qn, in_=h[:, 0:512]), reads=[R_h[0]], writes=[R_qn])
        P.op("act", lambda e: e.copy(out=kvc, in_=h[:, 768:1024]), reads=[R_h[1]], writes=[R_kvc])
        rope(hk, 4, cosb4, sinb4, kr, R_h[1], R_kr, "pool")
        rope(hq, 8, cosb8, sinb8, Qaug, R_h[0], R_Qaug, "pool")
        ws = i % 6
        P.op("pool", lambda e: e.tensor_copy(out=Vsel[:, i, :, 0:64], in_=h[:, 1024:1152].rearrange("p (g d) -> p g d", g=2)), reads=[R_h[2]], writes=[R_Vsel[i]])
        P.op("pool", lambda e: e.tensor_copy(out=Vwin[:, ws, :, 0:64], in_=h[:, 1152:1280].rearrange("p (g d) -> p g d", g=2)), reads=[R_h[2]], writes=[R_Vwin[ws]])
        P.op("act", lambda e: e.activation(out=gth, in_=h[:, 1280:1304], func=AF.Tanh, scale=0.5), reads=[R_h[2]], writes=[R_gs_])
        P.op("dve", lambda e: e.tensor_scalar(out=gs_[:].rearrange("p a b -> p (a b)"), in0=gth, scalar1=0.5, scalar2=0.5, op0=ALU.mult, op1=ALU.add), reads=[R_gs_], writes=[R_gs_])
        P.op("act", lambda e: e.activation(out=zs_, in_=h[:, 1304:1816], func=AF.Tanh, scale=0.5), reads=[R_h[3]], writes=[R_zs_])
        P.op("dve", lambda e: e.scalar_tensor_tensor(out=zs_, in0=zs_, scalar=1.0, in1=h[:, 1304:1816], op0=ALU.add, op1=ALU.mult), reads=[R_zs_, R_h[3]], writes=[R_zs_])
        yield
        yield
        for hh in range(8):
            P.op("pe", lambda e, hh=hh: e.transpose(out=psb[2][0:64, hh * 128:(hh + 1) * 128], in_=qn[:, hh * 64:(hh + 1) * 64], identity=k.ident), reads=[R_qn, RI], writes=[Rps[2]])
        P.op("dve", lambda e: e.tensor_copy(out=QnT[0:64].rearrange("p a b -> p (a b)"), in_=psb[2][0:64, 0:1024]), reads=[Rps[2]], writes=[R_QnT])
        yield
        for j in range(4):
            P.op("pe", lambda e, j=j: e.transpose(out=psb[3][0:64, (4 + j) * 128:(5 + j) * 128], in_=kvc[:, j * 64:(j + 1) * 64], identity=k.ident), reads=[R_kvc, RI], writes=[Rps[3]])
        P.op("dve", lambda e: e.tensor_copy(out=kvcT[0:64, :, 0:16], in_=kvcT[0:64, :, 128:144]), reads=[R_kvcT], writes=[R_kvcT])
        P.op("dve", lambda e: e.tensor_copy(out=kvcT[0:64, :, 16:144], in_=psb[3][0:64, 512:1024].rearrange("p (j t) -> p j t", j=4)), reads=[Rps[3], R_kvcT], writes=[R_kvcT])
        yield
        yield
        if i == 0:
            emit_ck()
        m0 = 1 if i == 0 else 0
        nb = 8 - m0
        n0 = 8 * i - 1 + m0
        for (w1_, j0, col0) in ((wk1, 0, 0), (wv1, 2, 16)):
            o_ap = ps[3][:, col0:col0 + 16]
            for l in range(32):
                P.op("pe", lambda e, w1_=w1_, l=l, j0=j0, o_ap=o_ap: e.matmul(o_ap, lhsT=w1_[0:64, l, :], rhs=kvcT[0:64, j0:j0 + 2, l:l + 16 * 7 + 1:16], start=(l == 0), stop=(l == 31)),
                     reads=[R_kvcT, RI], writes=[Rps[3]])
                if l == 15:
                    yield
            yield
        for col0, cc in ((0, 0), (16, 1)):
            P.op("dve", lambda e, col0=col0, cc=cc: e.tensor_scalar(out=u[:, col0:col0 + 16], in0=ps[3][:, col0:col0 + 16], scalar1=ck[:, cc:cc + 1], scalar2=None, op0=ALU.add), reads=[Rps[3], R_ck], writes=[R_u])
        P.op("act", lambda e: e.activation(out=th, in_=u, func=AF.Tanh, scale=0.5), reads=[R_u], writes=[R_th])
        P.op("dve", lambda e: e.scalar_tensor_tensor(out=hsf, in0=th, scalar=1.0, in1=u, op0=ALU.add, op1=ALU.mult), reads=[R_th, R_u], writes=[R_hsf])
        P.op("dve", lambda e: e.tensor_scalar(out=hsk, in0=hsf[:, 0:16], scalar1=0.5, scalar2=None, op0=ALU.mult), reads=[R_hsf], writes=[R_hsk])
        P.op("dve", lambda e: e.tensor_scalar(out=hsTv[:, :, n0:n0 + nb], in0=hsf[:, 16:32].rearrange("p (g m) -> p g m", g=2)[:, :, m0:8], scalar1=0.5, scalar2=None, op0=ALU.mult), reads=[R_hsf, R_hsv], writes=[R_hsv])
        for j in range(4):
            P.op("pe", lambda e, j=j: e.transpose(out=psb[2][0:64, j * 128:(j + 1) * 128], in_=kr[:, j, :], identity=k.ident), reads=[R_kr, RI], writes=[Rps[2]])
        P.op("act", lambda e: e.copy(out=KaT[0:64, :, i * 128:(i + 1) * 128], in_=psb[2][0:64, 0:256].rearrange("p (g t) -> p g t", g=2)), reads=[Rps[2]], writes=[R_KaT[i]])
        P.op("act", lambda e: e.copy(out=KwT[0:64, ws, :, :], in_=psb[2][0:64, 256:512].rearrange("p (g t) -> p g t", g=2)), reads=[Rps[2]], writes=[R_KwT[ws]])
        yield
        yield
        yield
        P.op("pe", lambda e: e.matmul(ps[3][0:64, 32:48], lhsT=wk2, rhs=hsk, start=True, stop=True), reads=[R_hsk, RI], writes=[Rps[3]])
        kc_ps = ps[3][0:64, 32:48].rearrange("p (g m) -> p g m", g=2)[:, :, m0:8]
        P.op("act", lambda e: e.copy(out=kcT[0:64, :, n0:n0 + nb], in_=kc_ps), reads=[Rps[3], R_kc], writes=[R_kc])
        for nt in sorted({n0 // 128, (8 * i + 6) // 128}):
            for g in range(2):
                P.op("pe", lambda e, nt=nt, g=g: e.matmul(ps[3][:, 64:128], lhsT=hsTv[:, g, nt * 128:(nt + 1) * 128], rhs=wv2, start=True, stop=True), reads=[R_hsv, RI], writes=[Rps[3]])
                P.op("act", lambda e, nt=nt, g=g: e.copy(out=vca[:, nt, g, 0:64], in_=ps[3][:, 64:128]), reads=[Rps[3], R_vca], writes=[R_vca])
        yield
        yield
        nts = [0] if 8 * i + 6 < 128 else [0, 1]

        def cmp_mask(nt):
            if nt == 0 and i <= 16:
                return maskc[:, i, :]
            if nt == 1 and i >= 16:
                return maskc[:, 17 + i - 16, :]
            return None

        imp_ps = ps[3][:, 128:388].rearrange("p (h e) -> p h e", h=4)

        def cmp_group(g):
            def extra(it, pt, pi):
                nt = it["kt"]
                for hh in range(4):
                    P.op("pe", lambda e, hh=hh: e.matmul(imp_ps[:, hh, :], lhsT=pt[:, hh * 128:(hh + 1) * 128], rhs=ovl[:, nt, :], start=(nt == nts[0] and hh == 0), stop=(nt == nts[-1]), skip_group_check=True),
                         reads=[R_Pb[pi], RI], writes=[Rps[3]])

            def done(pvb, pv):
                combine(par, g, 0, pvb, pv, True, 0)
                P.op("dve", lambda e: e.tensor_scalar_max(out=rd, in0=imp_ps[:, :, 64], scalar1=TINY), reads=[Rps[3]], writes=[R_rd])
                P.op("dve", lambda e: e.reciprocal(out=rd, in_=rd), reads=[R_rd], writes=[R_rd])
                P.op("dve", lambda e: e.tensor_scalar(out=imp[:, g, :], in0=imp_ps[:, 0, 0:64], scalar1=rd[:, 0:1], scalar2=None, op0=ALU.mult), reads=[Rps[3], R_rd], writes=[R_imp])
                for hh in range(1, 4):
                    P.op("dve", lambda e, hh=hh: e.scalar_tensor_tensor(out=imp[:, g, :], in0=imp_ps[:, hh, 0:64], scalar=rd[:, hh:hh + 1], in1=imp[:, g, :], op0=ALU.mult, op1=ALU.add),
                         reads=[Rps[3], R_rd, R_imp], writes=[R_imp])

            items = []
            add_branch(items, nts, lambda nt: (kcT[0:64, g, nt * 128:(nt + 1) * 128], [R_kc]), QnT[0:64, 4 * g:4 * g + 4, :], [R_QnT],
                       lambda nt: (vca[:, nt, g, :], [R_vca]), cmp_mask, [0, 1], [2], extra=extra, done=done, ci=0)
            flush(items)

        for g in range(2):
            cmp_group(g)
            yield
        if i < 8:
            P.op("pool", lambda e: e.memset(Qaug[:, :, 64:128], 0.0), reads=[R_Qaug], writes=[R_Qaug])
        else:
            c0, c1 = 2 * i, 2 * i + 1
            P.op("pool", lambda e: e.memset(imp[0:64, :, c0 - 1:64], -1.0), reads=[R_imp], writes=[R_imp])
            P.op("pool", lambda e: e.memset(imp[64:128, :, c1 - 1:64], -1.0), reads=[R_imp], writes=[R_imp])
            P.op("pool", lambda e: e.memset(imp[:, :, 0:1], -1.0), reads=[R_imp], writes=[R_imp])
            for g in range(2):
                P.op("dve", lambda e, g=g: e.max(out=m8a, in_=imp[:, g, :]), reads=[R_imp], writes=[R_m8])
                P.op("dve", lambda e, g=g: e.match_replace(out=impt, in_to_replace=m8a, in_values=imp[:, g, :], imm_value=-2.0), reads=[R_imp, R_m8], writes=[R_m8])
                P.op("dve", lambda e: e.max(out=m8b, in_=impt), reads=[R_m8], writes=[R_m8])
                P.op("dve", lambda e, g=g: e.tensor_scalar(out=negm[:, g, :], in0=imp[:, g, :], scalar1=m8b[:, 4:5], scalar2=NEG, op0=ALU.is_lt, op1=ALU.mult), reads=[R_imp, R_m8], writes=[R_negm])
            P.op("pool", lambda e: e.memset(negm[0:64, :, c0 - 1:c0 + 1], 0.0), reads=[R_negm], writes=[R_negm])
            P.op("pool", lambda e: e.memset(negm[64:128, :, c1 - 1:c1 + 1], 0.0), reads=[R_negm], writes=[R_negm])
            P.op("pool", lambda e: e.memset(negm[:, :, 0:1], 0.0), reads=[R_negm], writes=[R_negm])
            for g in range(2):
                P.op("pool", lambda e, g=g: e.tensor_copy(out=Qaug[:, 4 * g:4 * g + 4, 64:128], in_=bc(negm[:, g:g + 1, :], [128, 4, 64])), reads=[R_negm, R_Qaug], writes=[R_Qaug])
        yield
        yield
        yield
        yield
        for hh in range(8):
            P.op("pe", lambda e, hh=hh: e.transpose(out=psb[2][:, hh * 128:(hh + 1) * 128], in_=Qaug[:, hh, :], identity=k.ident), reads=[R_Qaug, RI], writes=[Rps[2]])
        P.op("dve", lambda e: e.tensor_copy(out=QaT_[:].rearrange("p a b -> p (a b)"), in_=psb[2][:, 0:1024]), reads=[Rps[2]], writes=[R_QaT_])
        yield

    N_A_STEPS = 33

    def stage_b(i, agen, prev_tail=None):
        par = i % 2
        QaT_, R_QaT_ = QaTs[par], R_QaTs[par]
        wkts = list(range(max(0, i - 4), i + 1))

        def win_mask(kt):
            if kt == i:
                return k.tri
            if kt == i - 4:
                return k.win2
            return None

        items = []
        for g in range(2):
            add_branch(items, list(range(i + 1)), (lambda g: lambda kt: (KaT[:, g, kt * 128:(kt + 1) * 128], [R_KaT[kt], RI]))(g), QaT_[:, 4 * g:4 * g + 4, :], [R_QaT_],
                       (lambda g: lambda kt: (Vsel[:, kt, g, :], [R_Vsel[kt]]))(g), lambda kt: k.tri if kt == i else None, [4, 5, 6], [7],
                       done=(lambda g: lambda pvb, pv: combine(par, g, 1, pvb, pv, False, 1))(g))
        for g in range(2):
            add_branch(items, wkts, (lambda g: lambda kt: (KwT[0:64, kt % 6, g, :], [R_KwT[kt % 6]]))(g), QaT_[0:64, 4 * g:4 * g + 4, :], [R_QaT_],
                       (lambda g: lambda kt: (Vwin[:, kt % 6, g, :], [R_Vwin[kt % 6]]))(g), win_mask, [4, 5, 6], [7],
                       done=(lambda g: lambda pvb, pv: combine(par, g, 2, pvb, pv, False, 1))(g))
        n = len(items)
        taken = [0]

        pt_ = [prev_tail, None]

        def hook(j):
            want = ((j + 1) * N_A_STEPS + n - 1) // n
            if pt_[0] is not None and (j >= 2 or want > 6 or j == n - 1):
                pt_[0][0]()
                pt_[1] = pt_[0][1]
                pt_[0] = None
            if pt_[1] is not None and items[j]["idx"] == items[j]["n"] - 1:
                pt_[1]()
                pt_[1] = None
            if agen is None:
                return
            while taken[0] < want:
                taken[0] += 1
                next(agen, None)

        flush(items, hook)
        if pt_[0] is not None:
            pt_[0][0]()
            pt_[1] = pt_[0][1]
            pt_[0] = None
        if pt_[1] is not None:
            pt_[1]()
            pt_[1] = None
        if agen is not None:
            for _ in agen:
                pass
        acc, R_acc, zs_, R_zs_ = accs[par], R_accs[par], zss[par], R_zss[par]

        def tail_dve():
            P.op("dve", lambda e: e.scalar_tensor_tensor(out=oab, in0=acc[:].rearrange("p a b -> p (a b)"), scalar=0.5, in1=zs_, op0=ALU.mult, op1=ALU.mult), reads=[R_acc[0], R_acc[1], R_zs_], writes=[R_oab])

        def tail_pe():
            for half in range(2):
                for c in range(2):
                    cc = half * 2 + c
                    P.op("pe", lambda e, c=c, cc=cc: e.transpose(out=psb[7][:, 520 + c * 128:520 + (c + 1) * 128], in_=oab[:, cc * 128:(cc + 1) * 128], identity=k.ident), reads=[R_oab, RI], writes=[Rps[7]])
                P.op("dve", lambda e, half=half: e.tensor_copy(out=k.oaT[:, 2 * half:2 * half + 2, i * 128:(i + 1) * 128], in_=psb[7][:, 520:776].rearrange("p (c t) -> p c t", c=2)), reads=[Rps[7]], writes=[k.R_oaT])
        return (tail_dve, tail_pe)

    for _ in stage_a(0):
        pass
    tail_fn = None
    for i in range(NT):
        tail_fn = stage_b(i, stage_a(i + 1) if i + 1 < NT else None, tail_fn)
    tail_fn[0]()
    tail_fn[1]()


def alloc_xT(k):
    A = k.A
    k.xTs = [A.alloc([8, 128]), A.alloc([8, 128])]
    k.xTb = [A.alloc([8, 128], BF16), A.alloc([8, 128], BF16)]
    k.R_xTs = [Res(), Res()]
    k.R_xTb = [Res(), Res()]


def alloc_obT(k):
    k.obT = k.A.alloc([4, SEQ], BF16)
    if not hasattr(k, "R_obT"):
        k.R_obT = Res("obT")


def pass2(k):
    P, A, NT = k.P, k.A, k.NT
    k.xcast = "act"
    ps, psb, Rps = k.ps, k.psb, k.Rps
    RI = k.R_init
    alloc_obT(k)
    W2 = A.alloc([8, 2048], BF16)
    R_W2 = load_weight_groups(k, "W2g", W2, k.w2_d, 8, [(0, 512), (512, 1024), (1024, 1536), (1536, 2048)])
    b2bc = A.alloc([2048])
    P.dma("sp", lambda e: e.dma_start(out=b2bc, in_=k.b2_d.broadcast_to([128, 2048])), "init", writes=[RI])
    lbA = A.alloc([512])
    lbB = A.alloc([512])
    ghalf = A.alloc([512])
    P.dma("sp", lambda e: e.dma_start(out=lbA, in_=k.lbl_d[0:1, :].broadcast_to([128, 512])), "init", writes=[RI])
    P.dma("sp", lambda e: e.dma_start(out=lbB, in_=k.lbl_d[1:2, :].broadcast_to([128, 512])), "init", writes=[RI])
    P.dma("sp", lambda e: e.dma_start(out=ghalf, in_=k.hg_d.broadcast_to([128, 512])), "init", writes=[RI])
    P.op("dve", lambda e: e.tensor_tensor(out=lbA, in0=lbA, in1=lbB, op=ALU.subtract), reads=[RI], writes=[RI])
    P.op("act", lambda e: e.activation(out=lbA, in_=lbA, func=AF.Tanh, scale=0.5), reads=[RI], writes=[RI])
    P.op("dve", lambda e: e.tensor_scalar(out=lbB, in0=lbA, scalar1=0.25, scalar2=0.75, op0=ALU.mult, op1=ALU.add), reads=[RI], writes=[RI])
    P.op("dve", lambda e: e.tensor_scalar(out=lbA, in0=lbA, scalar1=-0.25, scalar2=0.25, op0=ALU.mult, op1=ALU.add), reads=[RI], writes=[RI])
    P.op("dve", lambda e: e.tensor_scalar(out=ghalf, in0=ghalf, scalar1=0.5, scalar2=None, op0=ALU.mult), reads=[RI], writes=[RI])
    St = A.alloc([4, 128])
    R_St = Res()
    Sb0 = [A.alloc([4, 128], BF16), A.alloc([4, 128], BF16)]
    R_Sb0 = [Res(), Res()]
    Sb1 = A.alloc([4, 128], BF16)
    R_Sb1 = Res()
    P.op("pool", lambda e: e.memset(St, 0.0), writes=[R_St])
    P.op("pool", lambda e: e.memset(Sb0[0], 0.0), writes=[R_Sb0[0]])
    alloc_xT(k)
    h2s = [A.alloc([2048]), A.alloc([2048])]
    R_h2s = [[Res() for _ in range(4)] for _ in range(2)]
    groups = [(0, 512), (512, 1024), (1024, 1536), (1536, 2048)]
    tqz, tff, tzz, logf, kk = (A.alloc([512]) for _ in range(5))
    R_tqz, R_tff, R_tzz, R_logf, R_kk = (Res() for _ in range(5))
    tzzs = [tzz, A.alloc([512])]
    R_tzzs = [R_tzz, Res()]
    eb, enb, erev = (A.alloc([512]) for _ in range(3))
    R_eb, R_enb, R_erev = (Res() for _ in range(3))
    qe_b, ke_b, kd_b, v_b, ob_b = (A.alloc([512], BF16) for _ in range(5))
    R_qe, R_ke, R_kd, R_v, R_ob = (Res() for _ in range(5))
    qeT, qeT0, qeT1, keT, attn_b = (A.alloc([4, 128], BF16) for _ in range(5))
    R_qeT, R_qeT0, R_qeT1, R_keT, R_attn = (Res() for _ in range(5))
    P.op("pool", lambda e: e.memset(qeT0, 0.0), writes=[R_qeT0])
    kd1_b = A.alloc([512], BF16)
    P.op("pool", lambda e: e.memset(kd_b, 0.0), writes=[R_kd])
    P.op("pool", lambda e: e.memset(kd1_b, 0.0), writes=[R_kd])
    P.op("pool", lambda e: e.memset(qeT1, 0.0), writes=[R_qeT1])
    dl = A.alloc([8])
    R_dl = Res()
    ssq, lnv, rstd = (A.alloc([4]) for _ in range(3))
    R_ssq = Res()
    junk = A.alloc([128])
    R_junk = Res()

    def s1(i):
        slot = i % 2
        load_xT(k, i, slot)
        project(k, slot, W2, b2bc, groups, h2s[slot], R_h2s[slot], R_W=R_W2)

    def tail_a(i):
        tzz, R_tzz = tzzs[i % 2], R_tzzs[i % 2]
        for hh in range(4):
            P.op("act", lambda e, hh=hh: e.activation(out=junk, in_=ps[7][:, hh * 128:(hh + 1) * 128], func=AF.Square, accum_out=ssq[:, hh:hh + 1]), reads=[Rps[7]], writes=[R_junk, R_ssq])
        P.op("act", lambda e: e.activation(out=lnv, in_=ssq, func=AF.Ln, scale=1.0 / 128.0, bias=1e-5), reads=[R_ssq], writes=[R_ssq])
        P.op("act", lambda e: e.activation(out=rstd, in_=lnv, func=AF.Exp, scale=-0.5), reads=[R_ssq], writes=[R_ssq])
        for hh in range(4):
            P.op("dve", lambda e, hh=hh: e.scalar_tensor_tensor(out=ob_b[:, hh * 128:(hh + 1) * 128], in0=ps[7][:, hh * 128:(hh + 1) * 128], scalar=rstd[:, hh:hh + 1], in1=tzz[:, hh * 128:(hh + 1) * 128], op0=ALU.mult, op1=ALU.mult),
                 reads=[Rps[7], R_ssq, R_tzz], writes=[R_ob])

    def tail_b(i):
        for c in range(4):
            P.op("pe", lambda e, c=c: e.transpose(out=psb[4][:, c * 128:(c + 1) * 128], in_=ob_b[:, c * 128:(c + 1) * 128], identity=k.ident), reads=[R_ob, RI], writes=[Rps[4]])
        P.op("act", lambda e: e.copy(out=k.obT[:, :, i * 128:(i + 1) * 128], in_=psb[4][:, 0:512].rearrange("p (c t) -> p c t", c=4)), reads=[Rps[4]], writes=[k.R_obT])

    def tile(i):
        slot = i % 2
        tzz, R_tzz = tzzs[i % 2], R_tzzs[i % 2]
        h2, R_h2 = h2s[slot], R_h2s[slot]
        hq, hf, hi, hz = (h2[:, a:a + 512] for a in (0, 512, 1024, 1536))
        if getattr(k, 'stop', 99) <= 1:
            return
        P.op("act", lambda e: e.activation(out=tqz, in_=hq, func=AF.Tanh, scale=0.5), reads=[R_h2[0]], writes=[R_tqz])
        P.op("act", lambda e: e.activation(out=tff, in_=hf, func=AF.Tanh, scale=0.5), reads=[R_h2[1]], writes=[R_tff])
        P.op("act", lambda e: e.activation(out=tzz, in_=hz, func=AF.Tanh, scale=0.5), reads=[R_h2[3]], writes=[R_tzz])
        P.op("act", lambda e: e.copy(out=v_b, in_=hi), reads=[R_h2[2]], writes=[R_v])
        P.op("dve", lambda e: e.scalar_tensor_tensor(out=tqz, in0=tqz, scalar=1.0, in1=hq, op0=ALU.add, op1=ALU.mult), reads=[R_tqz, R_h2[0]], writes=[R_tqz])
        P.op("pool", lambda e: e.tensor_tensor(out=tff, in0=tff, in1=lbA, op=ALU.mult), reads=[R_tff, RI], writes=[R_tff])
        P.op("pool", lambda e: e.tensor_tensor(out=tff, in0=tff, in1=lbB, op=ALU.add), reads=[R_tff, RI], writes=[R_tff])
        P.op("dve", lambda e: e.scalar_tensor_tensor(out=tzz, in0=tzz, scalar=1.0, in1=hz, op0=ALU.add, op1=ALU.mult), reads=[R_tzz, R_h2[3]], writes=[R_tzz])
        P.op("pool", lambda e: e.tensor_tensor(out=tzz, in0=tzz, in1=ghalf, op=ALU.mult), reads=[R_tzz, RI], writes=[R_tzz])
        P.op("act", lambda e: e.activation(out=logf, in_=tff, func=AF.Ln), reads=[R_tff], writes=[R_logf])
        P.op("pool", lambda e: e.tensor_scalar(out=kk, in0=tff, scalar1=-1.0, scalar2=1.0, op0=ALU.mult, op1=ALU.add), reads=[R_tff], writes=[R_kk])
        if getattr(k, 'stop', 99) <= 2:
            return
        P.op("pe", lambda e: e.matmul(ps[2][:, :], lhsT=k.mintra_f, rhs=logf, start=True, stop=True), reads=[R_logf, RI], writes=[Rps[2]])
        P.op("pe", lambda e: e.matmul(ps[3][:, :], lhsT=k.mrev_f, rhs=logf, start=True, stop=True), reads=[R_logf, RI], writes=[Rps[3]])
        for hh in range(4):
            P.op("pe", lambda e, hh=hh: e.matmul(ps[4][:, 2 * hh:2 * hh + 2], lhsT=logf[:, hh * 128:(hh + 1) * 128], rhs=k.cind_f, start=True, stop=True), reads=[R_logf, RI], writes=[Rps[4]])
        if getattr(k, 'stop', 99) <= 3:
            return
        P.op("act", lambda e: e.activation(out=eb, in_=ps[2][:, :], func=AF.Exp), reads=[Rps[2]], writes=[R_eb])
        P.op("act", lambda e: e.activation(out=enb, in_=ps[2][:, :], func=AF.Exp, scale=-1.0), reads=[Rps[2]], writes=[R_enb])
        P.op("act", lambda e: e.activation(out=erev, in_=ps[3][:, :], func=AF.Exp), reads=[Rps[3]], writes=[R_erev])
        P.op("act", lambda e: e.activation(out=dl, in_=ps[4][:, 0:8], func=AF.Exp), reads=[Rps[4]], writes=[R_dl])
        if i > 0:
            tail_a(i - 1)
        P.op("dve", lambda e: e.scalar_tensor_tensor(out=qe_b, in0=tqz, scalar=0.5, in1=eb, op0=ALU.mult, op1=ALU.mult), reads=[R_tqz, R_eb], writes=[R_qe])
        P.op("pool", lambda e: e.tensor_tensor(out=ke_b, in0=kk, in1=enb, op=ALU.mult), reads=[R_kk, R_enb], writes=[R_ke])
        P.op("pool", lambda e: e.tensor_tensor(out=kd_b[0:64, :], in0=kk[0:64, :], in1=erev[0:64, :], op=ALU.mult), reads=[R_kk, R_erev], writes=[R_kd])
        P.op("pool", lambda e: e.tensor_tensor(out=kd1_b[64:128, :], in0=kk[64:128, :], in1=erev[64:128, :], op=ALU.mult), reads=[R_kk, R_erev], writes=[R_kd])
        if getattr(k, 'stop', 99) <= 4:
            return
        for hh in range(4):
            P.op("pe", lambda e, hh=hh: e.transpose(out=psb[5][:, hh * 128:(hh + 1) * 128], in_=qe_b[:, hh * 128:(hh + 1) * 128], identity=k.ident), reads=[R_qe, RI], writes=[Rps[5]])
        for hh in range(4):
            P.op("pe", lambda e, hh=hh: e.transpose(out=psb[5][:, (4 + hh) * 128:(5 + hh) * 128], in_=ke_b[:, hh * 128:(hh + 1) * 128], identity=k.ident), reads=[R_ke, RI], writes=[Rps[5]])
        if k.stop <= 4.2:
            return
        q3 = psb[5][:, 0:512].rearrange("p (h t) -> p h t", h=4)
        P.op("dve", lambda e: e.tensor_copy(out=qeT, in_=q3), reads=[Rps[5]], writes=[R_qeT])
        if k.stop <= 4.4:
            return
        P.op("dve", lambda e: e.tensor_copy(out=qeT0[:, :, 0:64], in_=q3[:, :, 0:64]), reads=[Rps[5]], writes=[R_qeT0])
        P.op("dve", lambda e: e.tensor_copy(out=qeT1[:, :, 64:128], in_=q3[:, :, 64:128]), reads=[Rps[5]], writes=[R_qeT1])
        if k.stop <= 4.6:
            return
        P.op("dve", lambda e: e.tensor_copy(out=keT, in_=psb[5][:, 512:1024].rearrange("p (h t) -> p h t", h=4)), reads=[Rps[5]], writes=[R_keT])
        if getattr(k, 'stop', 99) <= 5:
            return
        for hh in range(4):
            P.op("pe", lambda e, hh=hh: e.matmul(ps[6][:, hh * 128:(hh + 1) * 128], lhsT=keT[:, hh, :], rhs=qeT[:, hh, :], start=True, stop=True), reads=[R_keT, R_qeT], writes=[Rps[6]])
        if i > 0:
            tail_b(i - 1)
        P.op("dve", lambda e: e.tensor_tensor(out=attn_b, in0=ps[6][:, :].rearrange("p (h t) -> p h t", h=4), in1=bc(k.mintra_b[:, None, :], [128, 4, 128]), op=ALU.mult), reads=[Rps[6], RI], writes=[R_attn])
        if getattr(k, 'stop', 99) <= 6:
            return
        def ub(hh, cc):
            b = 2 if hh < 2 else 3
            o = ((hh % 2) * 2 + cc) * 128
            return b, ps[b][:, o:o + 128]
        for hh in range(4):
            for cc in range(2):
                b, o = ub(hh, cc)
                P.op("pe", lambda e, hh=hh, cc=cc, o=o: e.matmul(o, lhsT=(kd_b, kd1_b)[cc][:, hh * 128:(hh + 1) * 128], rhs=v_b[:, hh * 128:(hh + 1) * 128], start=True, stop=True),
                     reads=[R_kd, R_v], writes=[Rps[b]])
        if getattr(k, 'stop', 99) <= 7:
            return
        nxt = (i + 1) % 2
        for cc in range(2):
            for hh in range(4):
                b, o = ub(hh, cc)
                P.op("dve", lambda e, hh=hh, cc=cc, o=o: e.scalar_tensor_tensor(out=St[:, hh, :], in0=St[:, hh, :], scalar=dl[:, 2 * hh + cc:2 * hh + cc + 1], in1=o, op0=ALU.mult, op1=ALU.add),
                     reads=[R_St, R_dl, Rps[b]], writes=[R_St])
            if cc == 0:
                P.op("act", lambda e: e.copy(out=Sb1, in_=St), reads=[R_St], writes=[R_Sb1])
            else:
                P.op("act", lambda e: e.copy(out=Sb0[nxt], in_=St), reads=[R_St], writes=[R_Sb0[nxt]])
        if getattr(k, 'stop', 99) <= 8:
            return
        cur = i % 2
        for hh in range(4):
            o = ps[7][:, hh * 128:(hh + 1) * 128]
            P.op("pe", lambda e, hh=hh, o=o: e.matmul(o, lhsT=attn_b[:, hh, :], rhs=v_b[:, hh * 128:(hh + 1) * 128], start=True, stop=False), reads=[R_attn, R_v], writes=[Rps[7]])
            P.op("pe", lambda e, hh=hh, o=o: e.matmul(o, lhsT=qeT0[:, hh, :], rhs=Sb0[cur][:, hh, :], start=False, stop=False), reads=[R_qeT0, R_Sb0[cur]], writes=[Rps[7]])
            P.op("pe", lambda e, hh=hh, o=o: e.matmul(o, lhsT=qeT1[:, hh, :], rhs=Sb1[:, hh, :], start=False, stop=True), reads=[R_qeT1, R_Sb1], writes=[Rps[7]])

    s1(0)
    for i in range(NT):
        if i + 1 < NT:
            s1(i + 1)
        tile(i)
    tail_a(NT - 1)
    tail_b(NT - 1)
    print("pass2 arena words", A.off)


def pass3(k):
    P, A, NT = k.P, k.A, k.NT
    k.xcast = "act"
    ps, psb, Rps = k.ps, k.psb, k.Rps
    RI = k.R_init
    alloc_obT(k)
    W3 = A.alloc([8, 2048], BF16)
    R_W3 = load_weight_groups(k, "W3g", W3, k.w3_d, 8, [(0, 512), (512, 1024), (1024, 1536), (1536, 2048)])
    b3bc = A.alloc([2048])
    P.dma("sp", lambda e: e.dma_start(out=b3bc, in_=k.b3_d.broadcast_to([128, 2048])), "init", writes=[RI])
    wba = A.alloc([4, 1024], BF16)
    wbb = A.alloc([4, 1024], BF16)
    wo = A.alloc([8, 1024], BF16)
    R_wba = load_weight_groups(k, "wbag", wba, k.wba_d, 4, [(0, 512), (512, 1024)])
    R_wbb = load_weight_groups(k, "wbbg", wbb, k.wbb_d, 4, [(0, 512), (512, 1024)])
    R_wo = load_weight_groups(k, "wog", wo, k.wo_d, 8, [(0, 512), (512, 1024)])
    lng = A.alloc([1024])
    lnb = A.alloc([1024])
    P.dma("sp", lambda e: e.dma_start(out=lng, in_=k.lng_d.broadcast_to([128, 1024])), "init", writes=[RI])
    P.dma("sp", lambda e: e.dma_start(out=lnb, in_=k.lnb_d.broadcast_to([128, 1024])), "init", writes=[RI])
    alloc_xT(k)
    xt = [A.alloc([1024]), A.alloc([1024]), A.alloc([1024])]
    R_xt = [Res(), Res(), Res()]
    hg = A.alloc([1024])
    R_hg = [Res(), Res()]
    t1 = k.stage[0][:, 0:512]
    t2 = k.stage[0][:, 512:1024]
    R_t1 = R_t2 = k.R_stage[0]
    y2 = [A.alloc([1024], BF16), A.alloc([1024], BF16)]
    R_y2 = [Res(), Res()]
    yT = A.alloc([8, 128], BF16)
    R_yT = Res()
    r = k.stage[1]
    R_r = k.R_stage[1]
    ot = [A.alloc([1024]), A.alloc([1024])]
    R_ot = [Res(), Res()]
    st6 = A.alloc([12])
    mv = A.alloc([2])
    lnv = A.alloc([1])
    rstd = A.alloc([1])
    nb = A.alloc([1])
    R_st = Res()

    def s1_load(i):
        load_xT(k, i, i % 2, cast=False)
        P.dma("sp", lambda e: e.dma_start(out=xt[i % 3], in_=k.x_d[i * 128:(i + 1) * 128, :]), f"xt{i % 3}", writes=[R_xt[i % 3]])

    def s1(i):
        slot = i % 2
        load_xT(k, i, slot, dma=False)
        ts = slice(i * 128, (i + 1) * 128)
        for half in range(2):
            hs = slice(half * 512, (half + 1) * 512)
            project(k, slot, W3, b3bc, [(half * 512, half * 512 + 512), (1024 + half * 512, 1536 + half * 512)], _HG(hg, half), R_hg, R_W=R_W3)
            P.op("act", lambda e: e.activation(out=hg, in_=hg, func=AF.Tanh, scale=0.5), reads=R_hg, writes=R_hg)
            for c in range(4):
                P.op("pe", lambda e, c=c, hs=hs: e.matmul(ps[2][:, :], lhsT=k.oaT[:, c, ts], rhs=wba[:, c, hs], start=(c == 0), stop=(c == 3)), reads=[k.R_oaT, R_wba[half]], writes=[Rps[2]])
            for c in range(4):
                P.op("pe", lambda e, c=c, hs=hs: e.matmul(ps[3][:, :], lhsT=k.obT[:, c, ts], rhs=wbb[:, c, hs], start=(c == 0), stop=(c == 3)), reads=[k.R_obT, R_wbb[half]], writes=[Rps[3]])
            P.op("dve", lambda e: e.scalar_tensor_tensor(out=t1, in0=hg[:, 0:512], scalar=1.0, in1=ps[2][:, :], op0=ALU.add, op1=ALU.mult), reads=[R_hg[0], Rps[2]], writes=[R_t1])
            P.op("dve", lambda e: e.scalar_tensor_tensor(out=t2, in0=hg[:, 512:1024], scalar=1.0, in1=ps[3][:, :], op0=ALU.add, op1=ALU.mult), reads=[R_hg[1], Rps[3]], writes=[R_t2])
            P.op("pool", lambda e, hs=hs: e.tensor_tensor(out=y2[slot][:, hs], in0=t1, in1=t2, op=ALU.add), reads=[R_t1, R_t2], writes=[R_y2[slot]])

    def s2(i):
        slot = i % 2
        for c in range(8):
            P.op("pe", lambda e, c=c: e.transpose(out=psb[4][:, c * 128:(c + 1) * 128], in_=y2[slot][:, c * 128:(c + 1) * 128], identity=k.ident), reads=[R_y2[slot], RI], writes=[Rps[4]])
        P.op("act", lambda e: e.copy(out=yT[:].rearrange("p a b -> p (a b)"), in_=psb[4][:, 0:1024]), reads=[Rps[4]], writes=[R_yT])
        for half in range(2):
            hs = slice(half * 512, (half + 1) * 512)
            b = 5 + half
            for c in range(8):
                P.op("pe", lambda e, c=c, hs=hs, b=b: e.matmul(ps[b][:, :], lhsT=yT[:, c, :], rhs=wo[:, c, hs], start=(c == 0), stop=(c == 7)), reads=[R_yT, R_wo[half]], writes=[Rps[b]])
            P.op("dve", lambda e, hs=hs, b=b: e.scalar_tensor_tensor(out=r[:, hs], in0=ps[b][:, :], scalar=0.5 / ALPHA, in1=xt[i % 3][:, hs], op0=ALU.mult, op1=ALU.add), reads=[Rps[b], R_xt[i % 3]], writes=[R_r])
            P.op("dve", lambda e, hs=hs, half=half: e.bn_stats(out=st6[:, half * 6:half * 6 + 6], in_=r[:, hs]), reads=[R_r], writes=[R_st])
        P.op("dve", lambda e: e.bn_aggr(out=mv, in_=st6), reads=[R_st], writes=[R_st])
        P.op("act", lambda e: e.activation(out=lnv, in_=mv[:, 1:2], func=AF.Ln, bias=1e-5 / (ALPHA * ALPHA)), reads=[R_st], writes=[R_st])
        P.op("act", lambda e: e.activation(out=rstd, in_=lnv, func=AF.Exp, scale=-0.5), reads=[R_st], writes=[R_st])
        P.op("dve", lambda e: e.scalar_tensor_tensor(out=nb, in0=mv[:, 0:1], scalar=-1.0, in1=rstd, op0=ALU.mult, op1=ALU.mult), reads=[R_st], writes=[R_st])
        o = ot[slot]
        P.op("dve", lambda e: e.tensor_scalar(out=o, in0=r, scalar1=rstd[:, 0:1], scalar2=nb[:, 0:1], op0=ALU.mult, op1=ALU.add), reads=[R_r, R_st], writes=[R_ot[slot]])
        P.op("pool", lambda e: e.tensor_tensor(out=o, in0=o, in1=lng, op=ALU.mult), reads=[R_ot[slot], RI], writes=[R_ot[slot]])
        P.op("pool", lambda e: e.tensor_tensor(out=o, in0=o, in1=lnb, op=ALU.add), reads=[R_ot[slot], RI], writes=[R_ot[slot]])
        P.dma("sp", lambda e: e.dma_start(out=k.out_d[i * 128:(i + 1) * 128, :], in_=o), f"out{slot}", reads=[R_ot[slot]])

    s1_load(0)
    if NT > 1:
        s1_load(1)
    s1(0)
    for i in range(NT):
        if i + 2 < NT:
            s1_load(i + 2)
        if i + 1 < NT:
            s1(i + 1)
        s2(i)
    print("pass3 arena words", A.off)


class _HG:
    def __init__(self, hg, half):
        self.hg = hg
        self.half = half

    def __getitem__(self, key):
        _, cs = key
        c0 = cs.start
        o = 0 if c0 < 1024 else 512
        return self.hg[:, o:o + (cs.stop - cs.start)]


def _consts():
    p = np.arange(128)
    cst = np.zeros((128, 642), np.float32)
    cst[:, 0:128] = np.eye(128)
    cst[:, 128:256] = (p[:, None] <= p[None, :])
    cst[:, 256:384] = (p[:, None] > p[None, :])
    same = (p[:, None] // 64) == (p[None, :] // 64)
    cst[:, 384:512] = same & (p[:, None] <= p[None, :])
    cst[:, 512:640] = same & (p[:, None] > p[None, :])
    cst[:, 640] = p < 64
    cst[:, 641] = p >= 64
    maskc = np.zeros((128, 33, 128), np.float32)
    q = np.arange(128)
    for idx in range(33):
        if idx <= 16:
            i, nt = idx, 0
        else:
            i, nt = idx - 17 + 16, 1
        n = nt * 128 + p
        maskc[:, idx, :] = (16 * n[:, None] + 31) <= (128 * i + q[None, :])
    ovl = np.zeros((128, 2, 65), np.float32)
    n = np.arange(256)
    cs = n * 16
    js = np.arange(64) * 64
    ov = ((cs[:, None] < js[None, :] + 64) & (cs[:, None] + 32 > js[None, :])).astype(np.float32)
    ov[255] = 0.0
    ovl[:, :, 0:64] = ov.reshape(2, 128, 64).transpose(1, 0, 2)
    ovl[:, :, 64] = 1.0
    ovl[127, 1, 64] = 0.0
    inv = np.float32(10000.0) ** (-(np.arange(0, 64, 2, dtype=np.float32)) / np.float32(64))
    pos = np.arange(SEQ, dtype=np.float32)
    ang = (pos[:, None] * inv[None, :]).astype(np.float32)
    cos = np.cos(ang).astype(np.float32).reshape(32, 128, 32).transpose(1, 0, 2)
    sin = np.sin(ang).astype(np.float32).reshape(32, 128, 32).transpose(1, 0, 2)
    et = (np.arange(SEQ)[None, :] // 64 == np.arange(64)[:, None]).astype(np.float32)
    return dict(cst=cst, maskc=np.ascontiguousarray(maskc.reshape(128, -1)), ovl=np.ascontiguousarray(ovl.reshape(128, -1)),
                cos=np.ascontiguousarray(cos.reshape(128, -1)), sin=np.ascontiguousarray(sin.reshape(128, -1)), et=et)


def prep_shared(w_in, b_in, pe_cmp_k, w_cmp_k1, w_cmp_k2, pe_cmp_v, w_cmp_v1, w_cmp_v2,
                hgrn_lb_logits, hgrn_norm_g, w_branch_a, w_branch_b, w_out, ln_g, ln_b):
    f = lambda a: np.ascontiguousarray(np.asarray(a, dtype=np.float32))
    w, b = np.asarray(w_in[0]), np.asarray(b_in[0])
    perm1 = np.concatenate([np.arange(0, 512), np.arange(768, 896), np.arange(1024, 1152), np.arange(512, 640), np.arange(640, 768),
                            np.arange(896, 1024), np.arange(1152, 1280), np.arange(1280, 1304), np.arange(1304, 1816)])
    d = dict(
        w1=f(w[:, perm1]), b1=f(b[perm1][None, :]),
        w2=f(w[:, 1816:3864]), b2=f(b[None, 1816:3864]),
        w3=f(w[:, 3864:5912]), b3=f(b[None, 3864:5912]),
        wk1=f(np.asarray(w_cmp_k1[0]).reshape(32, 64, 128).transpose(1, 0, 2).reshape(64, 4096)),
        wv1=f(np.asarray(w_cmp_v1[0]).reshape(32, 64, 128).transpose(1, 0, 2).reshape(64, 4096)),
        wk2=f(w_cmp_k2[0]), wv2=f(w_cmp_v2[0]),
        pek=f(np.asarray(pe_cmp_k[0]).T), pev=f(np.asarray(pe_cmp_v[0]).T),
        lbl=f(hgrn_lb_logits), hg=f(hgrn_norm_g),
        wba=f(w_branch_a[0]), wbb=f(w_branch_b[0]), wo=f(w_out[0]), lng=f(ln_g), lnb=f(ln_b),
    )
    d.update(_consts())
    return d


def kernel(x, **params):
    x = np.asarray(x, dtype=np.float32)
    shared = prep_shared(**params)
    nc = build()
    in_maps = []
    for b in range(8):
        m = dict(shared)
        m["x"] = np.ascontiguousarray(x[b])
        m["xT"] = np.ascontiguousarray(x[b].T.reshape(8, 128, SEQ))
        in_maps.append(m)
    res = run_bass_kernel_spmd(nc, in_maps, core_ids=list(range(8)))
    return np.stack([np.asarray(r["out"]) for r in res.results], axis=0)
```

```python
import numpy as np
from contextlib import ExitStack
import concourse.bass as bass
import concourse.mybir as mybir
from concourse.bass_utils import run_bass_kernel_spmd

F32 = mybir.dt.float32
BF16 = mybir.dt.bfloat16
AF = mybir.ActivationFunctionType
ALU = mybir.AluOpType

NTILES = 32
SEQ = 4096
NEG = -30000.0
TINY = 1e-30
ALPHA = 2.0 ** 0.25
KEEPWARM = False
FUSE_WAIT = True


class Res:
    __slots__ = ("name", "w", "r")

    def __init__(self, name=""):
        self.name = name
        self.w = None
        self.r = {}


class Prog:
    ENG = ("pe", "act", "dve", "pool", "sp")
    CAP = 30000

    def __init__(self, nc, same_engine_sync=True):
        self.nc = nc
        self.ops = {e: [] for e in self.ENG}
        self.waited = {e: {} for e in self.ENG}
        self.dma_count = {}
        self.same_engine_sync = same_engine_sync

    def _deps(self, eng, reads, writes):
        toks = {}

        def add(ch, idx):
            if toks.get(ch, -1) < idx:
                toks[ch] = idx

        for r in reads:
            if r.w is not None:
                add(*r.w)
        for w in writes:
            if w.w is not None and w.w[0] != eng:
                add(*w.w)
            for ch, idx in w.r.items():
                if ch != eng:
                    add(ch, idx)
        waits = []
        for ch, idx in toks.items():
            if ch == eng and (eng == "pe" or not self.same_engine_sync):
                continue
            if self.waited[eng].get(ch, -1) >= idx:
                continue
            self.waited[eng][ch] = idx
            waits.append((ch, idx))
        return waits

    def _finish(self, tok, reads, writes):
        ch, idx = tok
        for r in reads:
            if r.r.get(ch, -1) < idx:
                r.r[ch] = idx
        for w in writes:
            w.w = tok
            w.r = {}

    def op(self, eng, fn, reads=(), writes=()):
        waits = self._deps(eng, reads, writes)
        idx = len(self.ops[eng])
        self.ops[eng].append(dict(fn=fn, waits=waits, ms=False, dma=None))
        self._finish((eng, idx), reads, writes)

    def dma(self, eng, fn, chan, reads=(), writes=()):
        waits = self._deps(eng, reads, writes)
        k = self.dma_count.get(chan, 0)
        self.dma_count[chan] = k + 1
        self.ops[eng].append(dict(fn=fn, waits=waits, ms=False, dma=chan))
        self._finish((("dma", chan), k), reads, writes)

    def _all_waits(self, eng, engines=True):
        waits = []
        if engines:
            for ch in self.ENG:
                if ch == eng:
                    continue
                last = -1
                for i in range(len(self.ops[ch]) - 1, -1, -1):
                    o = self.ops[ch][i]
                    if o["fn"] is not None and o["dma"] is None:
                        last = i
                        break
                if last >= 0 and self.waited[eng].get(ch, -1) < last:
                    self.waited[eng][ch] = last
                    waits.append((ch, last))
        for chan, k in self.dma_count.items():
            ch = ("dma", chan)
            if self.waited[eng].get(ch, -1) < k - 1:
                self.waited[eng][ch] = k - 1
                waits.append((ch, k - 1))
        return waits

    def barrier(self):
        allw = {e: self._all_waits(e) for e in self.ENG}
        for e in self.ENG:
            self.ops[e].append(dict(fn=None, waits=allw[e], ms=False, dma=None))

    def wait_all_dma(self, eng):
        self.ops[eng].append(dict(fn=None, waits=self._all_waits(eng, engines=False), ms=False, dma=None))

    def emit(self, stack):
        nc = self.nc
        for e in self.ENG:
            for o in self.ops[e]:
                for ch, idx in o["waits"]:
                    if isinstance(ch, str):
                        self.ops[ch][idx]["ms"] = True
        msnum = {}
        nsem = {}
        for e in self.ENG:
            c = 0
            for i, o in enumerate(self.ops[e]):
                if o["ms"]:
                    c += 1
                    msnum[(e, i)] = c
            nsem[e] = max(1, (c + self.CAP - 1) // self.CAP)
        esems = {e: [stack.enter_context(nc.semaphore(f"s_{e}_{j}")) for j in range(nsem[e])] for e in self.ENG}
        dsems = {ch: stack.enter_context(nc.semaphore(f"d_{ch}")) for ch in self.dma_count}
        CAP = self.CAP

        def semval(ch, idx):
            if isinstance(ch, str):
                m = msnum[(ch, idx)]
                return esems[ch][(m - 1) // CAP], (m - 1) % CAP + 1
            return dsems[ch[1]], 16 * (idx + 1)

        block = stack.enter_context(nc.Block())

        def run(e):
            def body(eng):
                for i, o in enumerate(self.ops[e]):
                    waits = list(o["waits"])
                    fused = None
                    if FUSE_WAIT and waits and o["fn"] is not None and o["dma"] is None:
                        fused = waits.pop()
                    for ch, idx in waits:
                        s, v = semval(ch, idx)
                        eng.wait_ge(s, v)
                    if o["fn"] is None:
                        continue
                    ins = o["fn"](eng)
                    if fused is not None:
                        s, v = semval(*fused)
                        ins._wait_ge(s, v)
                    if o["dma"] is not None:
                        ins.then_inc(dsems[o["dma"]], 16)
                    elif o["ms"]:
                        m = msnum[(e, i)]
                        ins.then_inc(esems[e][(m - 1) // CAP], 1)
            return body

        block.tensor(run("pe"))
        block.scalar(run("act"))
        block.vector(run("dve"))
        block.gpsimd(run("pool"))
        block.sync(run("sp"))


class Arena:
    def __init__(self, handle, nwords):
        self.h = handle
        self.n = nwords
        self.off = 0

    def alloc(self, shape, dtype=F32, parts=128):
        nel = int(np.prod(shape))
        nw = (nel * (2 if dtype == BF16 else 4) + 3) // 4
        nw = (nw + 1) // 2 * 2
        assert self.off + nw <= self.n, f"SBUF arena overflow: {self.off}+{nw} > {self.n}"
        a = self.h[0:parts, self.off:self.off + nw]
        self.off += nw
        if dtype == BF16:
            a = a.bitcast(BF16)
        a = a[:, 0:nel]
        if len(shape) == 2:
            a = a.rearrange("p (a b) -> p a b", a=shape[0], b=shape[1])
        elif len(shape) == 3:
            a = a.rearrange("p (a b c) -> p a b c", a=shape[0], b=shape[1], c=shape[2])
        elif len(shape) != 1:
            raise ValueError(shape)
        return a


def bc(ap, shape):
    return ap.to_broadcast(list(shape))


class K:
    pass


def build(nt_tiles=NTILES, dbg=False, passes=(1, 2, 3), stop=99):
    nc = bass.Bass("TRN2", target_bir_lowering=False)
    k = K()
    k.stop = stop
    k.nc = nc
    k.NT = nt_tiles
    k.dbg = dbg
    S = SEQ

    def din(name, shape):
        return nc.dram_tensor(name, shape, F32, kind="ExternalInput").ap()

    k.xT_d = din("xT", [8, 128, S])
    k.x_d = din("x", [S, 1024])
    k.w1_d = din("w1", [1024, 1816])
    k.b1_d = din("b1", [1, 1816])
    k.w2_d = din("w2", [1024, 2048])
    k.b2_d = din("b2", [1, 2048])
    k.w3_d = din("w3", [1024, 2048])
    k.b3_d = din("b3", [1, 2048])
    k.wk1_d = din("wk1", [64, 32 * 128])
    k.wv1_d = din("wv1", [64, 32 * 128])
    k.wk2_d = din("wk2", [128, 64])
    k.wv2_d = din("wv2", [128, 64])
    k.pek_d = din("pek", [64, 32])
    k.pev_d = din("pev", [64, 32])
    k.lbl_d = din("lbl", [2, 512])
    k.hg_d = din("hg", [1, 512])
    k.wba_d = din("wba", [512, 1024])
    k.wbb_d = din("wbb", [512, 1024])
    k.wo_d = din("wo", [1024, 1024])
    k.lng_d = din("lng", [1, 1024])
    k.lnb_d = din("lnb", [1, 1024])
    k.cst_d = din("cst", [128, 642])
    k.maskc_d = din("maskc", [128, 33 * 128])
    k.ovl_d = din("ovl", [128, 2 * 65])
    k.cos_d = din("cos", [128, 32 * 32])
    k.sin_d = din("sin", [128, 32 * 32])
    k.et_d = din("et", [64, S])
    k.out_d = nc.dram_tensor("out", [S, 1024], F32, kind="ExternalOutput").ap()
    if dbg:
        k.dbg_oa = nc.dram_tensor("dbg_oa", [128, 4 * S], BF16, kind="ExternalOutput").ap()
        k.dbg_ob = nc.dram_tensor("dbg_ob", [128, 4 * S], BF16, kind="ExternalOutput").ap()

    P = Prog(nc)
    k.P = P
    with ExitStack() as st:
        ARENA_WORDS = 50 * 1024
        arena_h = st.enter_context(nc.sbuf_tensor("arena", [128, ARENA_WORDS], F32))
        k.A = Arena(arena_h, ARENA_WORDS)
        k.ps = [st.enter_context(nc.psum_tensor(f"ps{i}", [128, 512], F32)) for i in range(8)]
        k.psb = [p.bitcast(BF16) for p in k.ps]
        k.Rps = [Res(f"ps{i}") for i in range(8)]
        setup_persistent(k)
        if 1 in passes:
            mark = k.A.off
            pass1(k)
            P.barrier()
            k.A.off = mark
        if dbg:
            P.dma("sp", lambda e: e.dma_start(out=k.dbg_oa, in_=k.oaT[:].rearrange("p a b -> p (a b)")), "dbg", reads=[k.R_oaT])
        if 2 in passes:
            mark = k.A.off
            pass2(k)
            P.barrier()
            if dbg:
                P.dma("sp", lambda e: e.dma_start(out=k.dbg_ob, in_=k.obT[:].rearrange("p a b -> p (a b)")), "dbg", reads=[k.R_obT])
                P.barrier()
            k.A.off = mark
        if 3 in passes:
            pass3(k)
        P.wait_all_dma("sp")
        P.emit(st)
    return nc


def stage_cast(k, dst, src_d, parts, ncols, eng_cycle=("act", "dve", "pool"), p0=0):
    P = k.P
    c0 = 0
    while c0 < ncols:
        n = min(1024, ncols - c0)
        s = k.stage_i % 2
        k.stage_i += 1
        stg = k.stage[s]
        Rs = k.R_stage[s]
        P.dma("sp", lambda e, stg=stg, c0=c0, n=n: e.dma_start(out=stg[p0:p0 + parts, 0:n], in_=src_d[:, c0:c0 + n]), f"stg{s}", writes=[Rs])
        eng = eng_cycle[k.stage_i % len(eng_cycle)]
        d = dst[:, c0:c0 + n]
        if eng == "act":
            P.op("act", lambda e, d=d, stg=stg, n=n: e.copy(out=d, in_=stg[p0:p0 + parts, 0:n]), reads=[Rs], writes=[k.R_init])
        else:
            P.op(eng, lambda e, d=d, stg=stg, n=n: e.tensor_copy(out=d, in_=stg[p0:p0 + parts, 0:n]), reads=[Rs], writes=[k.R_init])
        c0 += n


def load_weight_groups(k, name, W, w_d, nchunks, colgroups):
    P = k.P
    res = []
    for gi, (c0, c1) in enumerate(colgroups):
        r = Res(f"{name}{gi}")
        src = w_d[:, c0:c1].rearrange("(c p) n -> p c n", p=128)
        P.dma("pool", lambda e, c0=c0, c1=c1, src=src: e.dma_start(out=W[:, :, c0:c1], in_=src), f"{name}{gi}", writes=[r])
        res.append(r)
    return res


def load_cast(k, dst, src_d):
    k.P.dma("pool", lambda e: e.dma_start(out=dst, in_=src_d), "initc", writes=[k.R_initc])


def join_init(k):
    k.P.op("pool", lambda e: e.memset(k.joinbuf, 0.0), reads=[k.R_init, k.R_initc], writes=[k.R_init])


def setup_persistent(k):
    P, A = k.P, k.A
    k.R_init = Res("init")
    k.R_initc = Res("initc")
    k.joinbuf = A.alloc([2])
    k.stage = [A.alloc([1024]), A.alloc([1024])]
    k.R_stage = [Res("stg0"), Res("stg1")]
    k.stage_i = 0
    k.cstf = A.alloc([642])
    P.dma("sp", lambda e: e.dma_start(out=k.cstf, in_=k.cst_d), "init", writes=[k.R_init])
    k.cstb = A.alloc([512], BF16)
    P.op("dve", lambda e: e.tensor_copy(out=k.cstb, in_=k.cstf[:, 0:512]), reads=[k.R_init], writes=[k.R_init])
    k.ident = k.cstb[:, 0:128]
    k.tri = k.cstb[:, 128:256]
    k.win2 = k.cstb[:, 256:384]
    k.mintra_b = k.cstb[:, 384:512]
    k.mintra_f = k.cstf[:, 384:512]
    k.mrev_f = k.cstf[:, 512:640]
    k.cind_f = k.cstf[:, 640:642]
    k.oaT = A.alloc([4, SEQ], BF16)
    k.R_oaT = Res("oaT")


def load_xT(k, i, slot, dma=True, cast=True):
    P = k.P
    xs = k.xTs[slot]
    xb = k.xTb[slot]
    src = k.xT_d[:, :, i * 128:(i + 1) * 128].rearrange("c p t -> p c t")
    if dma:
        P.dma("sp", lambda e: e.dma_start(out=xs, in_=src), f"xT{slot}", writes=[k.R_xTs[slot]])
    if not cast:
        return
    if getattr(k, "xcast", "pool") == "act":
        P.op("act", lambda e: e.copy(out=xb, in_=xs), reads=[k.R_xTs[slot]], writes=[k.R_xTb[slot]])
    else:
        P.op("pool", lambda e: e.tensor_copy(out=xb, in_=xs), reads=[k.R_xTs[slot]], writes=[k.R_xTb[slot]])


def project(k, slot, W, bbc, groups, h, R_h, banks=(0, 1), R_W=None):
    P = k.P
    xb = k.xTb[slot]
    for gi, (c0, c1) in enumerate(groups):
        b = banks[gi % len(banks)]
        bank = k.ps[b]
        for c in range(8):
            P.op("pe", lambda e, bank=bank, c=c, c0=c0, c1=c1: e.matmul(bank[:, 0:c1 - c0], lhsT=xb[:, c, :], rhs=W[:, c, c0:c1], start=(c == 0), stop=(c == 7)),
                 reads=[k.R_xTb[slot], k.R_init if R_W is None else R_W[c0 // 512]], writes=[k.Rps[b]])
        P.op("dve", lambda e, bank=bank, c0=c0, c1=c1: e.tensor_tensor(out=h[:, c0:c1], in0=bank[:, 0:c1 - c0], in1=bbc[:, c0:c1], op=ALU.add),
             reads=[k.Rps[b], k.R_init], writes=[R_h[gi]])


def pass1(k):
    P, A, NT = k.P, k.A, k.NT
    ps, psb, Rps = k.ps, k.psb, k.Rps
    RI = k.R_init
    W1 = A.alloc([8, 1816], BF16)
    R_W1 = load_weight_groups(k, "W1g", W1, k.w1_d, 8, [(0, 512), (512, 1024), (1024, 1304), (1304, 1816)])
    b1bc = A.alloc([1816])
    P.dma("sp", lambda e: e.dma_start(out=b1bc, in_=k.b1_d.broadcast_to([128, 1816])), "init", writes=[RI])
    cos = A.alloc([32, 32])
    sin = A.alloc([32, 32])
    P.dma("sp", lambda e: e.dma_start(out=cos[:].rearrange("p a b -> p (a b)"), in_=k.cos_d), "init", writes=[RI])
    P.dma("sp", lambda e: e.dma_start(out=sin[:].rearrange("p a b -> p (a b)"), in_=k.sin_d), "init", writes=[RI])
    maskc = A.alloc([33, 128], BF16)
    load_cast(k, maskc[:].rearrange("p a b -> p (a b)"), k.maskc_d)
    ovl = A.alloc([2, 65], BF16)
    load_cast(k, ovl[:].rearrange("p a b -> p (a b)"), k.ovl_d)
    wk1 = A.alloc([32, 128], BF16)
    wv1 = A.alloc([32, 128], BF16)
    load_cast(k, wk1[0:64].rearrange("p a b -> p (a b)"), k.wk1_d)
    load_cast(k, wv1[0:64].rearrange("p a b -> p (a b)"), k.wv1_d)
    wk2 = A.alloc([64], BF16)
    wv2 = A.alloc([64], BF16)
    load_cast(k, wk2, k.wk2_d)
    load_cast(k, wv2, k.wv2_d)
    pek = A.alloc([32], BF16)
    pev = A.alloc([32], BF16)
    load_cast(k, pek[0:64], k.pek_d)
    load_cast(k, pev[0:64], k.pev_d)
    KaT = A.alloc([2, SEQ], BF16)
    for g in range(2):
        load_cast(k, KaT[64:128, g, :], k.et_d)
    R_KaT = [Res() for _ in range(NT)]
    KwT = A.alloc([6, 2, 128], BF16)
    R_KwT = [Res() for _ in range(6)]
    Vsel = A.alloc([32, 2, 65], BF16)
    R_Vsel = [Res() for _ in range(NT)]
    Vwin = A.alloc([6, 2, 65], BF16)
    R_Vwin = [Res() for _ in range(6)]
    kcT = A.alloc([2, 256], BF16)
    hsTv = A.alloc([2, 256], BF16)
    vca = A.alloc([2, 2, 65], BF16)
    R_kc, R_hsv, R_vca = Res("kc"), Res("hsv"), Res("vca")
    P.op("pool", lambda e: e.memset(kcT, 0.0), writes=[R_kc])
    P.op("pool", lambda e: e.memset(hsTv, 0.0), writes=[R_hsv])
    P.op("pool", lambda e: e.memset(vca, 0.0), writes=[R_vca])
    P.op("pool", lambda e: e.memset(vca[:, :, :, 64:65], 1.0), reads=[R_vca], writes=[R_vca])
    P.op("pool", lambda e: e.memset(Vsel[:, :, :, 64:65], 1.0), writes=R_Vsel)
    P.op("pool", lambda e: e.memset(Vwin[:, :, :, 64:65], 1.0), writes=R_Vwin)
    kvcT = A.alloc([4, 144], BF16)
    R_kvcT = Res("kvcT")
    P.op("pool", lambda e: e.memset(kvcT, 0.0), writes=[R_kvcT])
    ck = A.alloc([2])
    R_ck = Res("ck")

    def emit_ck():
        for (w1_, pe_, col) in ((wk1, pek, 0), (wv1, pev, 1)):
            for l in range(32):
                P.op("pe", lambda e, w1_=w1_, pe_=pe_, l=l, col=col: e.matmul(ps[3][:, col:col + 1], lhsT=w1_[0:64, l, :], rhs=pe_[0:64, l:l + 1], start=(l == 0), stop=(l == 31)),
                     reads=[RI], writes=[Rps[3]])
        P.op("dve", lambda e: e.tensor_copy(out=ck, in_=ps[3][:, 0:2]), reads=[Rps[3]], writes=[R_ck])

    join_init(k)
    k.xTs = [A.alloc([8, 128]), A.alloc([8, 128])]
    k.xTb = [A.alloc([8, 128], BF16), A.alloc([8, 128], BF16)]
    k.R_xTs = [Res(), Res()]
    k.R_xTb = [Res(), Res()]
    h = A.alloc([1816])
    R_h = [Res() for _ in range(4)]
    groups = [(0, 512), (512, 1024), (1024, 1304), (1304, 1816)]
    tq = [A.alloc([8, 32]) for _ in range(4)]
    R_tq = [Res() for _ in range(4)]
    Qaug = A.alloc([8, 128], BF16)
    R_Qaug = Res()
    qn = A.alloc([512], BF16)
    R_qn = Res()
    kr = A.alloc([4, 64], BF16)
    R_kr = Res()
    kvc = A.alloc([256], BF16)
    R_kvc = Res()
    QnT = A.alloc([8, 128], BF16)
    R_QnT = Res()
    QaT = A.alloc([8, 128], BF16)
    R_QaT = Res()
    gth = A.alloc([24])
    gs = A.alloc([8, 3])
    R_gs = Res()
    zs = A.alloc([512])
    R_zs = Res()
    u = A.alloc([32])
    th = A.alloc([32])
    hsf = A.alloc([32])
    hsk = A.alloc([16], BF16)
    R_u, R_th, R_hsf, R_hsk = Res(), Res(), Res(), Res()
    NPB = 6
    Pb = [A.alloc([512], BF16) for _ in range(NPB)]
    R_Pb = [Res() for _ in range(NPB)]
    pb_i = [0]
    rd = A.alloc([4])
    R_rd = Res()
    imp = A.alloc([2, 64])
    R_imp = Res()
    m8a = A.alloc([8])
    m8b = A.alloc([8])
    impt = A.alloc([64])
    R_m8 = Res()
    negm = A.alloc([2, 64])
    R_negm = Res()
    cfs = [A.alloc([4]), A.alloc([4])]
    R_cfs = [Res(), Res()]
    tmpc = A.alloc([4, 64])
    R_tmpc = Res()
    oab = A.alloc([512], BF16)
    R_oab = Res()
    QaTs = [QaT, A.alloc([8, 128], BF16)]
    R_QaTs = [Res(), Res()]
    accs = [A.alloc([8, 64]), A.alloc([8, 64])]
    R_accs = [[Res(), Res()], [Res(), Res()]]
    gss = [gs, A.alloc([8, 3])]
    R_gss = [Res(), Res()]
    zss = [zs, A.alloc([512])]
    R_zss = [Res(), Res()]
    sc_cnt = {}
    pv_cnt = {}
    pvs = A.alloc([260])
    R_pvs = Res()
    print("pass1 arena words", A.off)

    def add_branch(items, kts, lhs_of, rhs_q, qres, v_of, masks, sbanks, pvbanks, extra=None, done=None, ci=1):
        key = tuple(pvbanks)
        cnt = pv_cnt.get(key, 0)
        pv_cnt[key] = cnt + 1
        pvb = pvbanks[cnt % len(pvbanks)]
        pv = ps[pvb][:, 0:260].rearrange("p (h e) -> p h e", h=4)
        for idx, kt in enumerate(kts):
            lhsT, lres = lhs_of(kt)
            va, vres = v_of(kt)
            items.append(dict(kt=kt, idx=idx, n=len(kts), lhsT=lhsT, lres=lres, rhs=rhs_q, qres=qres, va=va, vres=vres, mask=masks(kt),
                              sbanks=sbanks, pvb=pvb, pv=pv, extra=extra, done=done, ci=ci))

    def flush(items, hook=None):
        def score(it):
            assert len(it["sbanks"]) >= 2
            key = tuple(it["sbanks"])
            cnt = sc_cnt.get(key, 0)
            sc_cnt[key] = cnt + 1
            sb = it["sbanks"][cnt % len(it["sbanks"])]
            it["sb"] = sb
            P.op("pe", lambda e: e.matmul(ps[sb][:, :], lhsT=it["lhsT"], rhs=it["rhs"], start=True, stop=True),
                 reads=it["lres"] + it["qres"], writes=[Rps[sb]])

        def rest(it):
            sb = it["sb"]
            pi = pb_i[0] % NPB
            pb_i[0] += 1
            pt = Pb[pi]
            P.op("act", lambda e: e.activation(out=pt, in_=ps[sb][:, :], func=AF.Exp, scale=0.125), reads=[Rps[sb]], writes=[R_Pb[pi]])
            m = it["mask"]
            if m is not None:
                pt3 = pt.rearrange("p (h q) -> p h q", h=4)
                P.op("dve", lambda e: e.tensor_tensor(out=pt3, in0=pt3, in1=bc(m[:, None, :], [128, 4, 128]), op=ALU.mult),
                     reads=[R_Pb[pi], RI], writes=[R_Pb[pi]])
            pv, pvb, idx, n, va = it["pv"], it["pvb"], it["idx"], it["n"], it["va"]
            for hh in range(4):
                P.op("pe", lambda e, hh=hh: e.matmul(pv[:, hh, :], lhsT=pt[:, hh * 128:(hh + 1) * 128], rhs=va, start=(idx == 0 and hh == 0), stop=(idx == n - 1), skip_group_check=True),
                     reads=[R_Pb[pi]] + it["vres"], writes=[Rps[pvb]])
            if KEEPWARM:
                P.op("pe", lambda e: e.matmul(ps[pvb][:, 260:512], lhsT=pt[:, 384:512], rhs=pt[:, 0:252], start=False, stop=False, skip_group_check=True),
                     reads=[R_Pb[pi]], writes=[Rps[pvb]])
            if it["extra"] is not None:
                it["extra"](it, pt, pi)
            if idx == n - 1 and it["done"] is not None:
                if it["ci"] == 1:
                    P.op("dve", lambda e: e.tensor_copy(out=pvs, in_=ps[pvb][:, 0:260]), reads=[Rps[pvb]], writes=[R_pvs])
                    it["done"](None, pvs.rearrange("p (h e) -> p h e", h=4))
                else:
                    it["done"](pvb, pv)

        if not items:
            return
        look = len(items[0]["sbanks"]) - 1
        for j in range(min(look, len(items))):
            score(items[j])
        for j, it in enumerate(items):
            if j + look < len(items):
                score(items[j + look])
            rest(it)
            if hook is not None:
                hook(j)

    def combine(par, g, br, pvb, pv, first, ci):
        cf, R_cf = cfs[ci], R_cfs[ci]
        acc, R_acc, gs_, R_gs_ = accs[par], R_accs[par], gss[par], R_gss[par]
        R_src = R_pvs if pvb is None else Rps[pvb]
        P.op("dve", lambda e: e.tensor_scalar_max(out=cf, in0=pv[:, :, 64], scalar1=TINY), reads=[R_src], writes=[R_cf])
        P.op("dve", lambda e: e.reciprocal(out=cf, in_=cf), reads=[R_cf], writes=[R_cf])
        P.op("dve", lambda e: e.tensor_tensor(out=cf, in0=cf, in1=gs_[:, 4 * g:4 * g + 4, br], op=ALU.mult), reads=[R_cf, R_gs_], writes=[R_cf])
        accg = acc[:, 4 * g:4 * g + 4, :]
        if first:
            P.op("dve", lambda e: e.tensor_tensor(out=accg, in0=pv[:, :, 0:64], in1=bc(cf[:, :, None], [128, 4, 64]), op=ALU.mult),
                 reads=[R_src, R_cf], writes=[R_acc[g]])
        else:
            P.op("dve", lambda e: e.tensor_tensor(out=tmpc, in0=pv[:, :, 0:64], in1=bc(cf[:, :, None], [128, 4, 64]), op=ALU.mult),
                 reads=[R_src, R_cf], writes=[R_tmpc])
            P.op("pool", lambda e: e.tensor_tensor(out=accg, in0=accg, in1=tmpc, op=ALU.add), reads=[R_tmpc, R_acc[g]], writes=[R_acc[g]])

    def rope(src, nh, cb, sb_, dst, R_src, R_dst, eng):
        t = [x[:, 0:nh, :] for x in tq]
        P.op(eng, lambda e: e.tensor_tensor(out=t[0], in0=src[:, :, 0, :], in1=cb, op=ALU.mult), reads=[R_src, RI], writes=[R_tq[0]])
        P.op(eng, lambda e: e.tensor_tensor(out=t[1], in0=src[:, :, 1, :], in1=sb_, op=ALU.mult), reads=[R_src, RI], writes=[R_tq[1]])
        P.op(eng, lambda e: e.tensor_tensor(out=t[2], in0=src[:, :, 1, :], in1=cb, op=ALU.mult), reads=[R_src, RI], writes=[R_tq[2]])
        P.op(eng, lambda e: e.tensor_tensor(out=t[3], in0=src[:, :, 0, :], in1=sb_, op=ALU.mult), reads=[R_src, RI], writes=[R_tq[3]])
        P.op(eng, lambda e: e.tensor_tensor(out=dst[:, :, 0:32], in0=t[0], in1=t[1], op=ALU.subtract), reads=[R_tq[0], R_tq[1]], writes=[R_dst])
        P.op(eng, lambda e: e.tensor_tensor(out=dst[:, :, 32:64], in0=t[2], in1=t[3], op=ALU.add), reads=[R_tq[2], R_tq[3]], writes=[R_dst])

    def stage_a(i):
        slot = i % 2
        par = i % 2
        QaT_, R_QaT_ = QaTs[par], R_QaTs[par]
        gs_, R_gs_, zs_, R_zs_ = gss[par], R_gss[par], zss[par], R_zss[par]
        if i == 0:
            load_xT(k, 0, 0, cast=False)
        if i + 1 < NT:
            load_xT(k, i + 1, (i + 1) % 2, cast=False)
        k.xcast = "act"
        load_xT(k, i, slot, dma=False)
        yield
        yield
        xb = k.xTb[slot]
        for gi, (c0, c1) in enumerate(groups):
            b = gi % 2
            for c in range(8):
                P.op("pe", lambda e, b=b, c=c, c0=c0, c1=c1: e.matmul(ps[b][:, 0:c1 - c0], lhsT=xb[:, c, :], rhs=W1[:, c, c0:c1], start=(c == 0), stop=(c == 7)),
                     reads=[k.R_xTb[slot], R_W1[gi]], writes=[Rps[b]])
                if c == 3:
                    yield
            P.op("dve", lambda e, b=b, c0=c0, c1=c1: e.tensor_tensor(out=h[:, c0:c1], in0=ps[b][:, 0:c1 - c0], in1=b1bc[:, c0:c1], op=ALU.add),
                 reads=[Rps[b], RI], writes=[R_h[gi]])
            yield
        cosb8 = bc(cos[:, i:i + 1, :], [128, 8, 32])
        sinb8 = bc(sin[:, i:i + 1, :], [128, 8, 32])
        cosb4 = bc(cos[:, i:i + 1, :], [128, 4, 32])
        sinb4 = bc(sin[:, i:i + 1, :], [128, 4, 32])
        hq = h[:, 0:512].rearrange("p (h t j) -> p h t j", h=8, t=2, j=32)
        hk = h[:, 512:768].rearrange("p (h t j) -> p h t j", h=4, t=2, j=32)
        P.op("act", lambda e: e.copy(out=qn, in_=h[:, 0:512]), reads=[R_h[0]], writes=[R_qn])
        P.op("act", lambda e: e.copy(out=kvc, in_=h[:, 768:1024]), reads=[R_h[1]], writes=[R_kvc])
        rope(hk, 4, cosb4, sinb4, kr, R_h[1], R_kr, "pool")
        rope(hq, 8, cosb8, sinb8, Qaug, R_h[0], R_Qaug, "pool")
        ws = i % 6
        P.op("pool", lambda e: e.tensor_copy(out=Vsel[:, i, :, 0:64], in_=h[:, 1024:1152].rearrange("p (g d) -> p g d", g=2)), reads=[R_h[2]], writes=[R_Vsel[i]])
        P.op("pool", lambda e: e.tensor_copy(out=Vwin[:, ws, :, 0:64], in_=h[:, 1152:1280].rearrange("p (g d) -> p g d", g=2)), reads=[R_h[2]], writes=[R_Vwin[ws]])
        P.op("act", lambda e: e.activation(out=gth, in_=h[:, 1280:1304], func=AF.Tanh, scale=0.5), reads=[R_h[2]], writes=[R_gs_])
        P.op("dve", lambda e: e.tensor_scalar(out=gs_[:].rearrange("p a b -> p (a b)"), in0=gth, scalar1=0.5, scalar2=0.5, op0=ALU.mult, op1=ALU.add), reads=[R_gs_], writes=[R_gs_])
        P.op("act", lambda e: e.activation(out=zs_, in_=h[:, 1304:1816], func=AF.Tanh, scale=0.5), reads=[R_h[3]], writes=[R_zs_])
        P.op("dve", lambda e: e.scalar_tensor_tensor(out=zs_, in0=zs_, scalar=1.0, in1=h[:, 1304:1816], op0=ALU.add, op1=ALU.mult), reads=[R_zs_, R_h[3]], writes=[R_zs_])
        yield
        yield
        for hh in range(8):
            P.op("pe", lambda e, hh=hh: e.transpose(out=psb[2][0:64, hh * 128:(hh + 1) * 128], in_=qn[:, hh * 64:(hh + 1) * 64], identity=k.ident), reads=[R_qn, RI], writes=[Rps[2]])
        P.op("dve", lambda e: e.tensor_copy(out=QnT[0:64].rearrange("p a b -> p (a b)"), in_=psb[2][0:64, 0:1024]), reads=[Rps[2]], writes=[R_QnT])
        yield
        for j in range(4):
            P.op("pe", lambda e, j=j: e.transpose(out=psb[3][0:64, (4 + j) * 128:(5 + j) * 128], in_=kvc[:, j * 64:(j + 1) * 64], identity=k.ident), reads=[R_kvc, RI], writes=[Rps[3]])
        P.op("dve", lambda e: e.tensor_copy(out=kvcT[0:64, :, 0:16], in_=kvcT[0:64, :, 128:144]), reads=[R_kvcT], writes=[R_kvcT])
        P.op("dve", lambda e: e.tensor_copy(out=kvcT[0:64, :, 16:144], in_=psb[3][0:64, 512:1024].rearrange("p (j t) -> p j t", j=4)), reads=[Rps[3], R_kvcT], writes=[R_kvcT])
        yield
        yield
        if i == 0:
            emit_ck()
        m0 = 1 if i == 0 else 0
        nb = 8 - m0
        n0 = 8 * i - 1 + m0
        for (w1_, j0, col0) in ((wk1, 0, 0), (wv1, 2, 16)):
            o_ap = ps[3][:, col0:col0 + 16]
            for l in range(32):
                P.op("pe", lambda e, w1_=w1_, l=l, j0=j0, o_ap=o_ap: e.matmul(o_ap, lhsT=w1_[0:64, l, :], rhs=kvcT[0:64, j0:j0 + 2, l:l + 16 * 7 + 1:16], start=(l == 0), stop=(l == 31)),
                     reads=[R_kvcT, RI], writes=[Rps[3]])
                if l == 15:
                    yield
            yield
        for col0, cc in ((0, 0), (16, 1)):
            P.op("dve", lambda e, col0=col0, cc=cc: e.tensor_scalar(out=u[:, col0:col0 + 16], in0=ps[3][:, col0:col0 + 16], scalar1=ck[:, cc:cc + 1], scalar2=None, op0=ALU.add), reads=[Rps[3], R_ck], writes=[R_u])
        P.op("act", lambda e: e.activation(out=th, in_=u, func=AF.Tanh, scale=0.5), reads=[R_u], writes=[R_th])
        P.op("dve", lambda e: e.scalar_tensor_tensor(out=hsf, in0=th, scalar=1.0, in1=u, op0=ALU.add, op1=ALU.mult), reads=[R_th, R_u], writes=[R_hsf])
        P.op("dve", lambda e: e.tensor_scalar(out=hsk, in0=hsf[:, 0:16], scalar1=0.5, scalar2=None, op0=ALU.mult), reads=[R_hsf], writes=[R_hsk])
        P.op("dve", lambda e: e.tensor_scalar(out=hsTv[:, :, n0:n0 + nb], in0=hsf[:, 16:32].rearrange("p (g m) -> p g m", g=2)[:, :, m0:8], scalar1=0.5, scalar2=None, op0=ALU.mult), reads=[R_hsf, R_hsv], writes=[R_hsv])
        for j in range(4):
            P.op("pe", lambda e, j=j: e.transpose(out=psb[2][0:64, j * 128:(j + 1) * 128], in_=kr[:, j, :], identity=k.ident), reads=[R_kr, RI], writes=[Rps[2]])
        P.op("act", lambda e: e.copy(out=KaT[0:64, :, i * 128:(i + 1) * 128], in_=psb[2][0:64, 0:256].rearrange("p (g t) -> p g t", g=2)), reads=[Rps[2]], writes=[R_KaT[i]])
        P.op("act", lambda e: e.copy(out=KwT[0:64, ws, :, :], in_=psb[2][0:64, 256:512].rearrange("p (g t) -> p g t", g=2)), reads=[Rps[2]], writes=[R_KwT[ws]])
        yield
        yield
        yield
        P.op("pe", lambda e: e.matmul(ps[3][0:64, 32:48], lhsT=wk2, rhs=hsk, start=True, stop=True), reads=[R_hsk, RI], writes=[Rps[3]])
        kc_ps = ps[3][0:64, 32:48].rearrange("p (g m) -> p g m", g=2)[:, :, m0:8]
        P.op("act", lambda e: e.copy(out=kcT[0:64, :, n0:n0 + nb], in_=kc_ps), reads=[Rps[3], R_kc], writes=[R_kc])
        for nt in sorted({n0 // 128, (8 * i + 6) // 128}):
            for g in range(2):
                P.op("pe", lambda e, nt=nt, g=g: e.matmul(ps[3][:, 64:128], lhsT=hsTv[:, g, nt * 128:(nt + 1) * 128], rhs=wv2, start=True, stop=True), reads=[R_hsv, RI], writes=[Rps[3]])
                P.op("act", lambda e, nt=nt, g=g: e.copy(out=vca[:, nt, g, 0:64], in_=ps[3][:, 64:128]), reads=[Rps[3], R_vca], writes=[R_vca])
        yield
        yield
        nts = [0] if 8 * i + 6 < 128 else [0, 1]

        def cmp_mask(nt):
            if nt == 0 and i <= 16:
                return maskc[:, i, :]
            if nt == 1 and i >= 16:
                return maskc[:, 17 + i - 16, :]
            return None

        imp_ps = ps[3][:, 128:388].rearrange("p (h e) -> p h e", h=4)

        def cmp_group(g):
            def extra(it, pt, pi):
                nt = it["kt"]
                for hh in range(4):
                    P.op("pe", lambda e, hh=hh: e.matmul(imp_ps[:, hh, :], lhsT=pt[:, hh * 128:(hh + 1) * 128], rhs=ovl[:, nt, :], start=(nt == nts[0] and hh == 0), stop=(nt == nts[-1]), skip_group_check=True),
                         reads=[R_Pb[pi], RI], writes=[Rps[3]])

            def done(pvb, pv):
                combine(par, g, 0, pvb, pv, True, 0)
                P.op("dve", lambda e: e.tensor_scalar_max(out=rd, in0=imp_ps[:, :, 64], scalar1=TINY), reads=[Rps[3]], writes=[R_rd])
                P.op("dve", lambda e: e.reciprocal(out=rd, in_=rd), reads=[R_rd], writes=[R_rd])
                P.op("dve", lambda e: e.tensor_scalar(out=imp[:, g, :], in0=imp_ps[:, 0, 0:64], scalar1=rd[:, 0:1], scalar2=None, op0=ALU.mult), reads=[Rps[3], R_rd], writes=[R_imp])
                for hh in range(1, 4):
                    P.op("dve", lambda e, hh=hh: e.scalar_tensor_tensor(out=imp[:, g, :], in0=imp_ps[:, hh, 0:64], scalar=rd[:, hh:hh + 1], in1=imp[:, g, :], op0=ALU.mult, op1=ALU.add),
                         reads=[Rps[3], R_rd, R_imp], writes=[R_imp])

            items = []
            add_branch(items, nts, lambda nt: (kcT[0:64, g, nt * 128:(nt + 1) * 128], [R_kc]), QnT[0:64, 4 * g:4 * g + 4, :], [R_QnT],
                       lambda nt: (vca[:, nt, g, :], [R_vca]), cmp_mask, [0, 1], [2], extra=extra, done=done, ci=0)
            flush(items)

        for g in range(2):
            cmp_group(g)
            yield
        if i < 8:
            P.op("pool", lambda e: e.memset(Qaug[:, :, 64:128], 0.0), reads=[R_Qaug], writes=[R_Qaug])
        else:
            c0, c1 = 2 * i, 2 * i + 1
            P.op("pool", lambda e: e.memset(imp[0:64, :, c0 - 1:64], -1.0), reads=[R_imp], writes=[R_imp])
            P.op("pool", lambda e: e.memset(imp[64:128, :, c1 - 1:64], -1.0), reads=[R_imp], writes=[R_imp])
            P.op("pool", lambda e: e.memset(imp[:, :, 0:1], -1.0), reads=[R_imp], writes=[R_imp])
            for g in range(2):
                P.op("dve", lambda e, g=g: e.max(out=m8a, in_=imp[:, g, :]), reads=[R_imp], writes=[R_m8])
                P.op("dve", lambda e, g=g: e.match_replace(out=impt, in_to_replace=m8a, in_values=imp[:, g, :], imm_value=-2.0), reads=[R_imp, R_m8], writes=[R_m8])
                P.op("dve", lambda e: e.max(out=m8b, in_=impt), reads=[R_m8], writes=[R_m8])
                P.op("dve", lambda e, g=g: e.tensor_scalar(out=negm[:, g, :], in0=imp[:, g, :], scalar1=m8b[:, 4:5], scalar2=NEG, op0=ALU.is_lt, op1=ALU.mult), reads=[R_imp, R_m8], writes=[R_negm])
            P.op("pool", lambda e: e.memset(negm[0:64, :, c0 - 1:c0 + 1], 0.0), reads=[R_negm], writes=[R_negm])
            P.op("pool", lambda e: e.memset(negm[64:128, :, c1 - 1:c1 + 1], 0.0), reads=[R_negm], writes=[R_negm])
            P.op("pool", lambda e: e.memset(negm[:, :, 0:1], 0.0), reads=[R_negm], writes=[R_negm])
            for g in range(2):
                P.op("pool", lambda e, g=g: e.tensor_copy(out=Qaug[:, 4 * g:4 * g + 4, 64:128], in_=bc(negm[:, g:g + 1, :], [128, 4, 64])), reads=[R_negm, R_Qaug], writes=[R_Qaug])
        yield
        yield
        yield
        yield
        for hh in range(8):
            P.op("pe", lambda e, hh=hh: e.transpose(out=psb[2][:, hh * 128:(hh + 1) * 128], in_=Qaug[:, hh, :], identity=k.ident), reads=[R_Qaug, RI], writes=[Rps[2]])
        P.op("dve", lambda e: e.tensor_copy(out=QaT_[:].rearrange("p a b -> p (a b)"), in_=psb[2][:, 0:1024]), reads=[Rps[2]], writes=[R_QaT_])
        yield

    N_A_STEPS = 33

    def stage_b(i, agen, prev_tail=None):
        par = i % 2
        QaT_, R_QaT_ = QaTs[par], R_QaTs[par]
        wkts = list(range(max(0, i - 4), i + 1))

        def win_mask(kt):
            if kt == i:
                return k.tri
            if kt == i - 4:
                return k.win2
            return None

        items = []
        for g in range(2):
            add_branch(items, list(range(i + 1)), (lambda g: lambda kt: (KaT[:, g, kt * 128:(kt + 1) * 128], [R_KaT[kt], RI]))(g), QaT_[:, 4 * g:4 * g + 4, :], [R_QaT_],
                       (lambda g: lambda kt: (Vsel[:, kt, g, :], [R_Vsel[kt]]))(g), lambda kt: k.tri if kt == i else None, [4, 5, 6], [7],
                       done=(lambda g: lambda pvb, pv: combine(par, g, 1, pvb, pv, False, 1))(g))
        for g in range(2):
            add_branch(items, wkts, (lambda g: lambda kt: (KwT[0:64, kt % 6, g, :], [R_KwT[kt % 6]]))(g), QaT_[0:64, 4 * g:4 * g + 4, :], [R_QaT_],
                       (lambda g: lambda kt: (Vwin[:, kt % 6, g, :], [R_Vwin[kt % 6]]))(g), win_mask, [4, 5, 6], [7],
                       done=(lambda g: lambda pvb, pv: combine(par, g, 2, pvb, pv, False, 1))(g))
        n = len(items)
        taken = [0]

        pt_ = [prev_tail, None]

        def hook(j):
            want = ((j + 1) * N_A_STEPS + n - 1) // n
            if pt_[0] is not None and (j >= 2 or want > 6 or j == n - 1):
                pt_[0][0]()
                pt_[1] = pt_[0][1]
                pt_[0] = None
            if pt_[1] is not None and items[j]["idx"] == items[j]["n"] - 1:
                pt_[1]()
                pt_[1] = None
            if agen is None:
                return
            while taken[0] < want:
                taken[0] += 1
                next(agen, None)

        flush(items, hook)
        if pt_[0] is not None:
            pt_[0][0]()
            pt_[1] = pt_[0][1]
            pt_[0] = None
        if pt_[1] is not None:
            pt_[1]()
            pt_[1] = None
        if agen is not None:
            for _ in agen:
                pass
        acc, R_acc, zs_, R_zs_ = accs[par], R_accs[par], zss[par], R_zss[par]

        def tail_dve():
            P.op("dve", lambda e: e.scalar_tensor_tensor(out=oab, in0=acc[:].rearrange("p a b -> p (a b)"), scalar=0.5, in1=zs_, op0=ALU.mult, op1=ALU.mult), reads=[R_acc[0], R_acc[1], R_zs_], writes=[R_oab])

        def tail_pe():
            for half in range(2):
                for c in range(2):
                    cc = half * 2 + c
                    P.op("pe", lambda e, c=c, cc=cc: e.transpose(out=psb[7][:, 520 + c * 128:520 + (c + 1) * 128], in_=oab[:, cc * 128:(cc + 1) * 128], identity=k.ident), reads=[R_oab, RI], writes=[Rps[7]])
                P.op("dve", lambda e, half=half: e.tensor_copy(out=k.oaT[:, 2 * half:2 * half + 2, i * 128:(i + 1) * 128], in_=psb[7][:, 520:776].rearrange("p (c t) -> p c t", c=2)), reads=[Rps[7]], writes=[k.R_oaT])
        return (tail_dve, tail_pe)

    for _ in stage_a(0):
        pass
    tail_fn = None
    for i in range(NT):
        tail_fn = stage_b(i, stage_a(i + 1) if i + 1 < NT else None, tail_fn)
    tail_fn[0]()
    tail_fn[1]()


def alloc_xT(k):
    A = k.A
    k.xTs = [A.alloc([8, 128]), A.alloc([8, 128])]
    k.xTb = [A.alloc([8, 128], BF16), A.alloc([8, 128], BF16)]
    k.R_xTs = [Res(), Res()]
    k.R_xTb = [Res(), Res()]


def alloc_obT(k):
    k.obT = k.A.alloc([4, SEQ], BF16)
    if not hasattr(k, "R_obT"):
        k.R_obT = Res("obT")


def pass2(k):
    P, A, NT = k.P, k.A, k.NT
    k.xcast = "act"
    ps, psb, Rps = k.ps, k.psb, k.Rps
    RI = k.R_init
    alloc_obT(k)
    W2 = A.alloc([8, 2048], BF16)
    R_W2 = load_weight_groups(k, "W2g", W2, k.w2_d, 8, [(0, 512), (512, 1024), (1024, 1536), (1536, 2048)])
    b2bc = A.alloc([2048])
    P.dma("sp", lambda e: e.dma_start(out=b2bc, in_=k.b2_d.broadcast_to([128, 2048])), "init", writes=[RI])
    lbA = A.alloc([512])
    lbB = A.alloc([512])
    ghalf = A.alloc([512])
    P.dma("sp", lambda e: e.dma_start(out=lbA, in_=k.lbl_d[0:1, :].broadcast_to([128, 512])), "init", writes=[RI])
    P.dma("sp", lambda e: e.dma_start(out=lbB, in_=k.lbl_d[1:2, :].broadcast_to([128, 512])), "init", writes=[RI])
    P.dma("sp", lambda e: e.dma_start(out=ghalf, in_=k.hg_d.broadcast_to([128, 512])), "init", writes=[RI])
    P.op("dve", lambda e: e.tensor_tensor(out=lbA, in0=lbA, in1=lbB, op=ALU.subtract), reads=[RI], writes=[RI])
    P.op("act", lambda e: e.activation(out=lbA, in_=lbA, func=AF.Tanh, scale=0.5), reads=[RI], writes=[RI])
    P.op("dve", lambda e: e.tensor_scalar(out=lbB, in0=lbA, scalar1=0.25, scalar2=0.75, op0=ALU.mult, op1=ALU.add), reads=[RI], writes=[RI])
    P.op("dve", lambda e: e.tensor_scalar(out=lbA, in0=lbA, scalar1=-0.25, scalar2=0.25, op0=ALU.mult, op1=ALU.add), reads=[RI], writes=[RI])
    P.op("dve", lambda e: e.tensor_scalar(out=ghalf, in0=ghalf, scalar1=0.5, scalar2=None, op0=ALU.mult), reads=[RI], writes=[RI])
    St = A.alloc([4, 128])
    R_St = Res()
    Sb0 = [A.alloc([4, 128], BF16), A.alloc([4, 128], BF16)]
    R_Sb0 = [Res(), Res()]
    Sb1 = A.alloc([4, 128], BF16)
    R_Sb1 = Res()
    P.op("pool", lambda e: e.memset(St, 0.0), writes=[R_St])
    P.op("pool", lambda e: e.memset(Sb0[0], 0.0), writes=[R_Sb0[0]])
    alloc_xT(k)
    h2s = [A.alloc([2048]), A.alloc([2048])]
    R_h2s = [[Res() for _ in range(4)] for _ in range(2)]
    groups = [(0, 512), (512, 1024), (1024, 1536), (1536, 2048)]
    tqz, tff, tzz, logf, kk = (A.alloc([512]) for _ in range(5))
    R_tqz, R_tff, R_tzz, R_logf, R_kk = (Res() for _ in range(5))
    tzzs = [tzz, A.alloc([512])]
    R_tzzs = [R_tzz, Res()]
    eb, enb, erev = (A.alloc([512]) for _ in range(3))
    R_eb, R_enb, R_erev = (Res() for _ in range(3))
    qe_b, ke_b, kd_b, v_b, ob_b = (A.alloc([512], BF16) for _ in range(5))
    R_qe, R_ke, R_kd, R_v, R_ob = (Res() for _ in range(5))
    qeT, qeT0, qeT1, keT, attn_b = (A.alloc([4, 128], BF16) for _ in range(5))
    R_qeT, R_qeT0, R_qeT1, R_keT, R_attn = (Res() for _ in range(5))
    P.op("pool", lambda e: e.memset(qeT0, 0.0), writes=[R_qeT0])
    kd1_b = A.alloc([512], BF16)
    P.op("pool", lambda e: e.memset(kd_b, 0.0), writes=[R_kd])
    P.op("pool", lambda e: e.memset(kd1_b, 0.0), writes=[R_kd])
    P.op("pool", lambda e: e.memset(qeT1, 0.0), writes=[R_qeT1])
    dl = A.alloc([8])
    R_dl = Res()
    ssq, lnv, rstd = (A.alloc([4]) for _ in range(3))
    R_ssq = Res()
    junk = A.alloc([128])
    R_junk = Res()

    def s1(i):
        slot = i % 2
        load_xT(k, i, slot)
        project(k, slot, W2, b2bc, groups, h2s[slot], R_h2s[slot], R_W=R_W2)

    def tail_a(i):
        tzz, R_tzz = tzzs[i % 2], R_tzzs[i % 2]
        for hh in range(4):
            P.op("act", lambda e, hh=hh: e.activation(out=junk, in_=ps[7][:, hh * 128:(hh + 1) * 128], func=AF.Square, accum_out=ssq[:, hh:hh + 1]), reads=[Rps[7]], writes=[R_junk, R_ssq])
        P.op("act", lambda e: e.activation(out=lnv, in_=ssq, func=AF.Ln, scale=1.0 / 128.0, bias=1e-5), reads=[R_ssq], writes=[R_ssq])
        P.op("act", lambda e: e.activation(out=rstd, in_=lnv, func=AF.Exp, scale=-0.5), reads=[R_ssq], writes=[R_ssq])
        for hh in range(4):
            P.op("dve", lambda e, hh=hh: e.scalar_tensor_tensor(out=ob_b[:, hh * 128:(hh + 1) * 128], in0=ps[7][:, hh * 128:(hh + 1) * 128], scalar=rstd[:, hh:hh + 1], in1=tzz[:, hh * 128:(hh + 1) * 128], op0=ALU.mult, op1=ALU.mult),
                 reads=[Rps[7], R_ssq, R_tzz], writes=[R_ob])

    def tail_b(i):
        for c in range(4):
            P.op("pe", lambda e, c=c: e.transpose(out=psb[4][:, c * 128:(c + 1) * 128], in_=ob_b[:, c * 128:(c + 1) * 128], identity=k.ident), reads=[R_ob, RI], writes=[Rps[4]])
        P.op("act", lambda e: e.copy(out=k.obT[:, :, i * 128:(i + 1) * 128], in_=psb[4][:, 0:512].rearrange("p (c t) -> p c t", c=4)), reads=[Rps[4]], writes=[k.R_obT])

    def tile(i):
        slot = i % 2
        tzz, R_tzz = tzzs[i % 2], R_tzzs[i % 2]
        h2, R_h2 = h2s[slot], R_h2s[slot]
        hq, hf, hi, hz = (h2[:, a:a + 512] for a in (0, 512, 1024, 1536))
        if getattr(k, 'stop', 99) <= 1:
            return
        P.op("act", lambda e: e.activation(out=tqz, in_=hq, func=AF.Tanh, scale=0.5), reads=[R_h2[0]], writes=[R_tqz])
        P.op("act", lambda e: e.activation(out=tff, in_=hf, func=AF.Tanh, scale=0.5), reads=[R_h2[1]], writes=[R_tff])
        P.op("act", lambda e: e.activation(out=tzz, in_=hz, func=AF.Tanh, scale=0.5), reads=[R_h2[3]], writes=[R_tzz])
        P.op("act", lambda e: e.copy(out=v_b, in_=hi), reads=[R_h2[2]], writes=[R_v])
        P.op("dve", lambda e: e.scalar_tensor_tensor(out=tqz, in0=tqz, scalar=1.0, in1=hq, op0=ALU.add, op1=ALU.mult), reads=[R_tqz, R_h2[0]], writes=[R_tqz])
        P.op("pool", lambda e: e.tensor_tensor(out=tff, in0=tff, in1=lbA, op=ALU.mult), reads=[R_tff, RI], writes=[R_tff])
        P.op("pool", lambda e: e.tensor_tensor(out=tff, in0=tff, in1=lbB, op=ALU.add), reads=[R_tff, RI], writes=[R_tff])
        P.op("dve", lambda e: e.scalar_tensor_tensor(out=tzz, in0=tzz, scalar=1.0, in1=hz, op0=ALU.add, op1=ALU.mult), reads=[R_tzz, R_h2[3]], writes=[R_tzz])
        P.op("pool", lambda e: e.tensor_tensor(out=tzz, in0=tzz, in1=ghalf, op=ALU.mult), reads=[R_tzz, RI], writes=[R_tzz])
        P.op("act", lambda e: e.activation(out=logf, in_=tff, func=AF.Ln), reads=[R_tff], writes=[R_logf])
        P.op("pool", lambda e: e.tensor_scalar(out=kk, in0=tff, scalar1=-1.0, scalar2=1.0, op0=ALU.mult, op1=ALU.add), reads=[R_tff], writes=[R_kk])
        if getattr(k, 'stop', 99) <= 2:
            return
        P.op("pe", lambda e: e.matmul(ps[2][:, :], lhsT=k.mintra_f, rhs=logf, start=True, stop=True), reads=[R_logf, RI], writes=[Rps[2]])
        P.op("pe", lambda e: e.matmul(ps[3][:, :], lhsT=k.mrev_f, rhs=logf, start=True, stop=True), reads=[R_logf, RI], writes=[Rps[3]])
        for hh in range(4):
            P.op("pe", lambda e, hh=hh: e.matmul(ps[4][:, 2 * hh:2 * hh + 2], lhsT=logf[:, hh * 128:(hh + 1) * 128], rhs=k.cind_f, start=True, stop=True), reads=[R_logf, RI], writes=[Rps[4]])
        if getattr(k, 'stop', 99) <= 3:
            return
        P.op("act", lambda e: e.activation(out=eb, in_=ps[2][:, :], func=AF.Exp), reads=[Rps[2]], writes=[R_eb])
        P.op("act", lambda e: e.activation(out=enb, in_=ps[2][:, :], func=AF.Exp, scale=-1.0), reads=[Rps[2]], writes=[R_enb])
        P.op("act", lambda e: e.activation(out=erev, in_=ps[3][:, :], func=AF.Exp), reads=[Rps[3]], writes=[R_erev])
        P.op("act", lambda e: e.activation(out=dl, in_=ps[4][:, 0:8], func=AF.Exp), reads=[Rps[4]], writes=[R_dl])
        if i > 0:
            tail_a(i - 1)
        P.op("dve", lambda e: e.scalar_tensor_tensor(out=qe_b, in0=tqz, scalar=0.5, in1=eb, op0=ALU.mult, op1=ALU.mult), reads=[R_tqz, R_eb], writes=[R_qe])
        P.op("pool", lambda e: e.tensor_tensor(out=ke_b, in0=kk, in1=enb, op=ALU.mult), reads=[R_kk, R_enb], writes=[R_ke])
        P.op("pool", lambda e: e.tensor_tensor(out=kd_b[0:64, :], in0=kk[0:64, :], in1=erev[0:64, :], op=ALU.mult), reads=[R_kk, R_erev], writes=[R_kd])
        P.op("pool", lambda e: e.tensor_tensor(out=kd1_b[64:128, :], in0=kk[64:128, :], in1=erev[64:128, :], op=ALU.mult), reads=[R_kk, R_erev], writes=[R_kd])
        if getattr(k, 'stop', 99) <= 4:
            return
        for hh in range(4):
            P.op("pe", lambda e, hh=hh: e.transpose(out=psb[5][:, hh * 128:(hh + 1) * 128], in_=qe_b[:, hh * 128:(hh + 1) * 128], identity=k.ident), reads=[R_qe, RI], writes=[Rps[5]])
        for hh in range(4):
            P.op("pe", lambda e, hh=hh: e.transpose(out=psb[5][:, (4 + hh) * 128:(5 + hh) * 128], in_=ke_b[:, hh * 128:(hh + 1) * 128], identity=k.ident), reads=[R_ke, RI], writes=[Rps[5]])
        if k.stop <= 4.2:
            return
        q3 = psb[5][:, 0:512].rearrange("p (h t) -> p h t", h=4)
        P.op("dve", lambda e: e.tensor_copy(out=qeT, in_=q3), reads=[Rps[5]], writes=[R_qeT])
        if k.stop <= 4.4:
            return
        P.op("dve", lambda e: e.tensor_copy(out=qeT0[:, :, 0:64], in_=q3[:, :, 0:64]), reads=[Rps[5]], writes=[R_qeT0])
        P.op("dve", lambda e: e.tensor_copy(out=qeT1[:, :, 64:128], in_=q3[:, :, 64:128]), reads=[Rps[5]], writes=[R_qeT1])
        if k.stop <= 4.6:
            return
        P.op("dve", lambda e: e.tensor_copy(out=keT, in_=psb[5][:, 512:1024].rearrange("p (h t) -> p h t", h=4)), reads=[Rps[5]], writes=[R_keT])
        if getattr(k, 'stop', 99) <= 5:
            return
        for hh in range(4):
            P.op("pe", lambda e, hh=hh: e.matmul(ps[6][:, hh * 128:(hh + 1) * 128], lhsT=keT[:, hh, :], rhs=qeT[:, hh, :], start=True, stop=True), reads=[R_keT, R_qeT], writes=[Rps[6]])
        if i > 0:
            tail_b(i - 1)
        P.op("dve", lambda e: e.tensor_tensor(out=attn_b, in0=ps[6][:, :].rearrange("p (h t) -> p h t", h=4), in1=bc(k.mintra_b[:, None, :], [128, 4, 128]), op=ALU.mult), reads=[Rps[6], RI], writes=[R_attn])
        if getattr(k, 'stop', 99) <= 6:
            return
        def ub(hh, cc):
            b = 2 if hh < 2 else 3
            o = ((hh % 2) * 2 + cc) * 128
            return b, ps[b][:, o:o + 128]
        for hh in range(4):
            for cc in range(2):
                b, o = ub(hh, cc)
                P.op("pe", lambda e, hh=hh, cc=cc, o=o: e.matmul(o, lhsT=(kd_b, kd1_b)[cc][:, hh * 128:(hh + 1) * 128], rhs=v_b[:, hh * 128:(hh + 1) * 128], start=True, stop=True),
                     reads=[R_kd, R_v], writes=[Rps[b]])
        if getattr(k, 'stop', 99) <= 7:
            return
        nxt = (i + 1) % 2
        for cc in range(2):
            for hh in range(4):
                b, o = ub(hh, cc)
                P.op("dve", lambda e, hh=hh, cc=cc, o=o: e.scalar_tensor_tensor(out=St[:, hh, :], in0=St[:, hh, :], scalar=dl[:, 2 * hh + cc:2 * hh + cc + 1], in1=o, op0=ALU.mult, op1=ALU.add),
                     reads=[R_St, R_dl, Rps[b]], writes=[R_St])
            if cc == 0:
                P.op("act", lambda e: e.copy(out=Sb1, in_=St), reads=[R_St], writes=[R_Sb1])
            else:
                P.op("act", lambda e: e.copy(out=Sb0[nxt], in_=St), reads=[R_St], writes=[R_Sb0[nxt]])
        if getattr(k, 'stop', 99) <= 8:
            return
        cur = i % 2
        for hh in range(4):
            o = ps[7][:, hh * 128:(hh + 1) * 128]
            P.op("pe", lambda e, hh=hh, o=o: e.matmul(o, lhsT=attn_b[:, hh, :], rhs=v_b[:, hh * 128:(hh + 1) * 128], start=True, stop=False), reads=[R_attn, R_v], writes=[Rps[7]])
            P.op("pe", lambda e, hh=hh, o=o: e.matmul(o, lhsT=qeT0[:, hh, :], rhs=Sb0[cur][:, hh, :], start=False, stop=False), reads=[R_qeT0, R_Sb0[cur]], writes=[Rps[7]])
            P.op("pe", lambda e, hh=hh, o=o: e.matmul(o, lhsT=qeT1[:, hh, :], rhs=Sb1[:, hh, :], start=False, stop=True), reads=[R_qeT1, R_Sb1], writes=[Rps[7]])

    s1(0)
    for i in range(NT):
        if i + 1 < NT:
            s1(i + 1)
        tile(i)
    tail_a(NT - 1)
    tail_b(NT - 1)
    print("pass2 arena words", A.off)


def pass3(k):
    P, A, NT = k.P, k.A, k.NT
    k.xcast = "act"
    ps, psb, Rps = k.ps, k.psb, k.Rps
    RI = k.R_init
    alloc_obT(k)
    W3 = A.alloc([8, 2048], BF16)
    R_W3 = load_weight_groups(k, "W3g", W3, k.w3_d, 8, [(0, 512), (512, 1024), (1024, 1536), (1536, 2048)])
    b3bc = A.alloc([2048])
    P.dma("sp", lambda e: e.dma_start(out=b3bc, in_=k.b3_d.broadcast_to([128, 2048])), "init", writes=[RI])
    wba = A.alloc([4, 1024], BF16)
    wbb = A.alloc([4, 1024], BF16)
    wo = A.alloc([8, 1024], BF16)
    R_wba = load_weight_groups(k, "wbag", wba, k.wba_d, 4, [(0, 512), (512, 1024)])
    R_wbb = load_weight_groups(k, "wbbg", wbb, k.wbb_d, 4, [(0, 512), (512, 1024)])
    R_wo = load_weight_groups(k, "wog", wo, k.wo_d, 8, [(0, 512), (512, 1024)])
    lng = A.alloc([1024])
    lnb = A.alloc([1024])
    P.dma("sp", lambda e: e.dma_start(out=lng, in_=k.lng_d.broadcast_to([128, 1024])), "init", writes=[RI])
    P.dma("sp", lambda e: e.dma_start(out=lnb, in_=k.lnb_d.broadcast_to([128, 1024])), "init", writes=[RI])
    alloc_xT(k)
    xt = [A.alloc([1024]), A.alloc([1024]), A.alloc([1024])]
    R_xt = [Res(), Res(), Res()]
    hg = A.alloc([1024])
    R_hg = [Res(), Res()]
    t1 = k.stage[0][:, 0:512]
    t2 = k.stage[0][:, 512:1024]
    R_t1 = R_t2 = k.R_stage[0]
    y2 = [A.alloc([1024], BF16), A.alloc([1024], BF16)]
    R_y2 = [Res(), Res()]
    yT = A.alloc([8, 128], BF16)
    R_yT = Res()
    r = k.stage[1]
    R_r = k.R_stage[1]
    ot = [A.alloc([1024]), A.alloc([1024])]
    R_ot = [Res(), Res()]
    st6 = A.alloc([12])
    mv = A.alloc([2])
    lnv = A.alloc([1])
    rstd = A.alloc([1])
    nb = A.alloc([1])
    R_st = Res()

    def s1_load(i):
        load_xT(k, i, i % 2, cast=False)
        P.dma("sp", lambda e: e.dma_start(out=xt[i % 3], in_=k.x_d[i * 128:(i + 1) * 128, :]), f"xt{i % 3}", writes=[R_xt[i % 3]])

    def s1(i):
        slot = i % 2
        load_xT(k, i, slot, dma=False)
        ts = slice(i * 128, (i + 1) * 128)
        for half in range(2):
            hs = slice(half * 512, (half + 1) * 512)
            project(k, slot, W3, b3bc, [(half * 512, half * 512 + 512), (1024 + half * 512, 1536 + half * 512)], _HG(hg, half), R_hg, R_W=R_W3)
            P.op("act", lambda e: e.activation(out=hg, in_=hg, func=AF.Tanh, scale=0.5), reads=R_hg, writes=R_hg)
            for c in range(4):
                P.op("pe", lambda e, c=c, hs=hs: e.matmul(ps[2][:, :], lhsT=k.oaT[:, c, ts], rhs=wba[:, c, hs], start=(c == 0), stop=(c == 3)), reads=[k.R_oaT, R_wba[half]], writes=[Rps[2]])
            for c in range(4):
                P.op("pe", lambda e, c=c, hs=hs: e.matmul(ps[3][:, :], lhsT=k.obT[:, c, ts], rhs=wbb[:, c, hs], start=(c == 0), stop=(c == 3)), reads=[k.R_obT, R_wbb[half]], writes=[Rps[3]])
            P.op("dve", lambda e: e.scalar_tensor_tensor(out=t1, in0=hg[:, 0:512], scalar=1.0, in1=ps[2][:, :], op0=ALU.add, op1=ALU.mult), reads=[R_hg[0], Rps[2]], writes=[R_t1])
            P.op("dve", lambda e: e.scalar_tensor_tensor(out=t2, in0=hg[:, 512:1024], scalar=1.0, in1=ps[3][:, :], op0=ALU.add, op1=ALU.mult), reads=[R_hg[1], Rps[3]], writes=[R_t2])
            P.op("pool", lambda e, hs=hs: e.tensor_tensor(out=y2[slot][:, hs], in0=t1, in1=t2, op=ALU.add), reads=[R_t1, R_t2], writes=[R_y2[slot]])

    def s2(i):
        slot = i % 2
        for c in range(8):
            P.op("pe", lambda e, c=c: e.transpose(out=psb[4][:, c * 128:(c + 1) * 128], in_=y2[slot][:, c * 128:(c + 1) * 128], identity=k.ident), reads=[R_y2[slot], RI], writes=[Rps[4]])
        P.op("act", lambda e: e.copy(out=yT[:].rearrange("p a b -> p (a b)"), in_=psb[4][:, 0:1024]), reads=[Rps[4]], writes=[R_yT])
        for half in range(2):
            hs = slice(half * 512, (half + 1) * 512)
            b = 5 + half
            for c in range(8):
                P.op("pe", lambda e, c=c, hs=hs, b=b: e.matmul(ps[b][:, :], lhsT=yT[:, c, :], rhs=wo[:, c, hs], start=(c == 0), stop=(c == 7)), reads=[R_yT, R_wo[half]], writes=[Rps[b]])
            P.op("dve", lambda e, hs=hs, b=b: e.scalar_tensor_tensor(out=r[:, hs], in0=ps[b][:, :], scalar=0.5 / ALPHA, in1=xt[i % 3][:, hs], op0=ALU.mult, op1=ALU.add), reads=[Rps[b], R_xt[i % 3]], writes=[R_r])
            P.op("dve", lambda e, hs=hs, half=half: e.bn_stats(out=st6[:, half * 6:half * 6 + 6], in_=r[:, hs]), reads=[R_r], writes=[R_st])
        P.op("dve", lambda e: e.bn_aggr(out=mv, in_=st6), reads=[R_st], writes=[R_st])
        P.op("act", lambda e: e.activation(out=lnv, in_=mv[:, 1:2], func=AF.Ln, bias=1e-5 / (ALPHA * ALPHA)), reads=[R_st], writes=[R_st])
        P.op("act", lambda e: e.activation(out=rstd, in_=lnv, func=AF.Exp, scale=-0.5), reads=[R_st], writes=[R_st])
        P.op("dve", lambda e: e.scalar_tensor_tensor(out=nb, in0=mv[:, 0:1], scalar=-1.0, in1=rstd, op0=ALU.mult, op1=ALU.mult), reads=[R_st], writes=[R_st])
        o = ot[slot]
        P.op("dve", lambda e: e.tensor_scalar(out=o, in0=r, scalar1=rstd[:, 0:1], scalar2=nb[:, 0:1], op0=ALU.mult, op1=ALU.add), reads=[R_r, R_st], writes=[R_ot[slot]])
        P.op("pool", lambda e: e.tensor_tensor(out=o, in0=o, in1=lng, op=ALU.mult), reads=[R_ot[slot], RI], writes=[R_ot[slot]])
        P.op("pool", lambda e: e.tensor_tensor(out=o, in0=o, in1=lnb, op=ALU.add), reads=[R_ot[slot], RI], writes=[R_ot[slot]])
        P.dma("sp", lambda e: e.dma_start(out=k.out_d[i * 128:(i + 1) * 128, :], in_=o), f"out{slot}", reads=[R_ot[slot]])

    s1_load(0)
    if NT > 1:
        s1_load(1)
    s1(0)
    for i in range(NT):
        if i + 2 < NT:
            s1_load(i + 2)
        if i + 1 < NT:
            s1(i + 1)
        s2(i)
    print("pass3 arena words", A.off)


class _HG:
    def __init__(self, hg, half):
        self.hg = hg
        self.half = half

    def __getitem__(self, key):
        _, cs = key
        c0 = cs.start
        o = 0 if c0 < 1024 else 512
        return self.hg[:, o:o + (cs.stop - cs.start)]


def _consts():
    p = np.arange(128)
    cst = np.zeros((128, 642), np.float32)
    cst[:, 0:128] = np.eye(128)
    cst[:, 128:256] = (p[:, None] <= p[None, :])
    cst[:, 256:384] = (p[:, None] > p[None, :])
    same = (p[:, None] // 64) == (p[None, :] // 64)
    cst[:, 384:512] = same & (p[:, None] <= p[None, :])
    cst[:, 512:640] = same & (p[:, None] > p[None, :])
    cst[:, 640] = p < 64
    cst[:, 641] = p >= 64
    maskc = np.zeros((128, 33, 128), np.float32)
    q = np.arange(128)
    for idx in range(33):
        if idx <= 16:
            i, nt = idx, 0
        else:
            i, nt = idx - 17 + 16, 1
        n = nt * 128 + p
        maskc[:, idx, :] = (16 * n[:, None] + 31) <= (128 * i + q[None, :])
    ovl = np.zeros((128, 2, 65), np.float32)
    n = np.arange(256)
    cs = n * 16
    js = np.arange(64) * 64
    ov = ((cs[:, None] < js[None, :] + 64) & (cs[:, None] + 32 > js[None, :])).astype(np.float32)
    ov[255] = 0.0
    ovl[:, :, 0:64] = ov.reshape(2, 128, 64).transpose(1, 0, 2)
    ovl[:, :, 64] = 1.0
    ovl[127, 1, 64] = 0.0
    inv = np.float32(10000.0) ** (-(np.arange(0, 64, 2, dtype=np.float32)) / np.float32(64))
    pos = np.arange(SEQ, dtype=np.float32)
    ang = (pos[:, None] * inv[None, :]).astype(np.float32)
    cos = np.cos(ang).astype(np.float32).reshape(32, 128, 32).transpose(1, 0, 2)
    sin = np.sin(ang).astype(np.float32).reshape(32, 128, 32).transpose(1, 0, 2)
    et = (np.arange(SEQ)[None, :] // 64 == np.arange(64)[:, None]).astype(np.float32)
    return dict(cst=cst, maskc=np.ascontiguousarray(maskc.reshape(128, -1)), ovl=np.ascontiguousarray(ovl.reshape(128, -1)),
                cos=np.ascontiguousarray(cos.reshape(128, -1)), sin=np.ascontiguousarray(sin.reshape(128, -1)), et=et)


def prep_shared(w_in, b_in, pe_cmp_k, w_cmp_k1, w_cmp_k2, pe_cmp_v, w_cmp_v1, w_cmp_v2,
                hgrn_lb_logits, hgrn_norm_g, w_branch_a, w_branch_b, w_out, ln_g, ln_b):
    f = lambda a: np.ascontiguousarray(np.asarray(a, dtype=np.float32))
    w, b = np.asarray(w_in[0]), np.asarray(b_in[0])
    perm1 = np.concatenate([np.arange(0, 512), np.arange(768, 896), np.arange(1024, 1152), np.arange(512, 640), np.arange(640, 768),
                            np.arange(896, 1024), np.arange(1152, 1280), np.arange(1280, 1304), np.arange(1304, 1816)])
    d = dict(
        w1=f(w[:, perm1]), b1=f(b[perm1][None, :]),
        w2=f(w[:, 1816:3864]), b2=f(b[None, 1816:3864]),
        w3=f(w[:, 3864:5912]), b3=f(b[None, 3864:5912]),
        wk1=f(np.asarray(w_cmp_k1[0]).reshape(32, 64, 128).transpose(1, 0, 2).reshape(64, 4096)),
        wv1=f(np.asarray(w_cmp_v1[0]).reshape(32, 64, 128).transpose(1, 0, 2).reshape(64, 4096)),
        wk2=f(w_cmp_k2[0]), wv2=f(w_cmp_v2[0]),
        pek=f(np.asarray(pe_cmp_k[0]).T), pev=f(np.asarray(pe_cmp_v[0]).T),
        lbl=f(hgrn_lb_logits), hg=f(hgrn_norm_g),
        wba=f(w_branch_a[0]), wbb=f(w_branch_b[0]), wo=f(w_out[0]), lng=f(ln_g), lnb=f(ln_b),
    )
    d.update(_consts())
    return d


def kernel(x, **params):
    x = np.asarray(x, dtype=np.float32)
    shared = prep_shared(**params)
    nc = build()
    in_maps = []
    for b in range(8):
        m = dict(shared)
        m["x"] = np.ascontiguousarray(x[b])
        m["xT"] = np.ascontiguousarray(x[b].T.reshape(8, 128, SEQ))
        in_maps.append(m)
    res = run_bass_kernel_spmd(nc, in_maps, core_ids=list(range(8)))
    return np.stack([np.asarray(r["out"]) for r in res.results], axis=0)
```
